# Optimizing a Trainium2 kernel written in Bass

```python
import math
import jax, jax.numpy as jnp
from jax import lax
import numpy as np

D_MODEL = 1024
BATCH = 8
SEQ = 8192
DEPTH = 2

CHUNK = 128
GMLP_WIDTH = D_MODEL // 2
SGU_GROUPS = 4
SGU_GROUP_WIDTH = GMLP_WIDTH // SGU_GROUPS
DIFF_HEADS = 4
DIFF_HEAD_DIM = D_MODEL // 16
DIFF_QK_WIDTH = 2 * DIFF_HEADS * DIFF_HEAD_DIM
DIFF_V_WIDTH = DIFF_HEADS * 2 * DIFF_HEAD_DIM
ROPE_DIM = DIFF_HEAD_DIM // 4
ROPE_THETA = 500000.0
Q_BLOCK = 128
CONV_WIDTH = D_MODEL // 2
CONV_K = 3
N_BRANCHES = 3
BRANCH_WIDTH = 512
IN_WIDTH = 2 * GMLP_WIDTH + 2 * DIFF_QK_WIDTH + DIFF_V_WIDTH + 3 * CONV_WIDTH + N_BRANCHES * D_MODEL
D_FF = ((8 * D_MODEL // 3 + 255) // 256) * 256
ALPHA = (2 * DEPTH) ** 0.25
BETA = (8 * DEPTH) ** -0.25
EPS = 1e-5
MAX_POS_OFFSET = 4096

kernel_name = "hybrid_gmlp_diffattn_shortconv_deepnorm"


def _layer_norm(x, g, b):
    xf = x.astype(jnp.float32)
    mu = jnp.mean(xf, axis=-1, keepdims=True)
    var = jnp.mean(jnp.square(xf - mu), axis=-1, keepdims=True)
    return ((xf - mu) * lax.rsqrt(var + EPS) * g.astype(jnp.float32) + b.astype(jnp.float32)).astype(x.dtype)


def _rms_norm(x, g):
    xf = x.astype(jnp.float32)
    ms = jnp.mean(jnp.square(xf), axis=-1, keepdims=True)
    return (xf * lax.rsqrt(ms + EPS) * g.astype(jnp.float32)).astype(x.dtype)


def _rope_tables(positions):
    inv_freq = ROPE_THETA ** (-jnp.arange(0, ROPE_DIM, 2, dtype=jnp.float32) / ROPE_DIM)
    ang = positions.astype(jnp.float32)[..., None] * inv_freq
    return jnp.cos(ang)[:, :, None, :], jnp.sin(ang)[:, :, None, :]


def _partial_rope(t, cos, sin):
    tf = t.astype(jnp.float32)
    half = ROPE_DIM // 2
    r1 = tf[..., :half]
    r2 = tf[..., half:ROPE_DIM]
    out = jnp.concatenate([r1 * cos - r2 * sin, r2 * cos + r1 * sin, tf[..., ROPE_DIM:]], axis=-1)
    return out.astype(t.dtype)


def _chunked_sgu(z, w_s, b_s, ln_g, ln_b):
    z = jax.nn.gelu(z, approximate=False)
    u, v = jnp.split(z, 2, axis=-1)
    v = _layer_norm(v, ln_g, ln_b)
    bsz, s_len, _ = v.shape
    v = v.reshape(bsz, s_len // CHUNK, CHUNK, SGU_GROUPS, SGU_GROUP_WIDTH)
    causal = jnp.tril(jnp.ones((CHUNK, CHUNK), dtype=bool))
    w = jnp.where(causal[None], w_s, jnp.zeros((), w_s.dtype))
    s = jnp.einsum('gij,bcjgd->bcigd', w, v) + b_s.T[:, :, None]
    return u * s.reshape(bsz, s_len, GMLP_WIDTH)


def _diff_attention(q, k, v, cos, sin, lq1, lk1, lq2, lk2, subln_g, lambda_init):
    bsz, s_len, _ = q.shape
    q = _partial_rope(q.reshape(bsz, s_len, 2 * DIFF_HEADS, DIFF_HEAD_DIM), cos, sin)
    k = _partial_rope(k.reshape(bsz, s_len, 2 * DIFF_HEADS, DIFF_HEAD_DIM), cos, sin)
    v = v.reshape(bsz, s_len, DIFF_HEADS, 2 * DIFF_HEAD_DIM)
    lam = (jnp.exp(jnp.sum(lq1.astype(jnp.float32) * lk1.astype(jnp.float32)))
           - jnp.exp(jnp.sum(lq2.astype(jnp.float32) * lk2.astype(jnp.float32)))
           + lambda_init)
    n_blocks = s_len // Q_BLOCK
    q_blocks = q.reshape(bsz, n_blocks, Q_BLOCK, 2 * DIFF_HEADS, DIFF_HEAD_DIM).transpose(1, 0, 2, 3, 4)
    starts = jnp.arange(n_blocks, dtype=jnp.int32) * Q_BLOCK
    k_pos = jnp.arange(s_len, dtype=jnp.int32)
    scale = DIFF_HEAD_DIM ** -0.5

    def one_block(args):
        qb, start = args
        sc = jnp.einsum('bqhd,bkhd->bhqk', qb, k).astype(jnp.float32) * scale
        mask = (start + jnp.arange(Q_BLOCK, dtype=jnp.int32))[:, None] >= k_pos[None, :]
        sc = jnp.where(mask[None, None], sc, -jnp.inf)
        p = jax.nn.softmax(sc, axis=-1).reshape(bsz, DIFF_HEADS, 2, Q_BLOCK, s_len)
        a = p[:, :, 0] - lam * p[:, :, 1]
        return jnp.einsum('bhqk,bkhe->bqhe', a.astype(v.dtype), v)

    o = lax.map(one_block, (q_blocks, starts))
    o = o.transpose(1, 0, 2, 3, 4).reshape(bsz, s_len, DIFF_HEADS, 2 * DIFF_HEAD_DIM)
    o = _rms_norm(o, subln_g) * (1.0 - lambda_init)
    return o.reshape(bsz, s_len, DIFF_V_WIDTH)


def _short_conv(z, conv_w):
    bg, cg, xc = jnp.split(z, 3, axis=-1)
    h = cg * xc
    s_len = h.shape[1]
    hp = jnp.pad(h, ((0, 0), (CONV_K - 1, 0), (0, 0)))
    conv = sum(hp[:, j:j + s_len, :] * conv_w[:, j] for j in range(CONV_K))
    return bg * conv


def _mixer(x, cos, sin, w_in, w_sgu, b_sgu, sgu_ln_g, sgu_ln_b, lq1, lk1, lq2, lk2,
           subln_g, conv_w, w_branch, w_o, lambda_init):
    bsz, s_len, _ = x.shape
    z = x @ w_in
    o1 = 2 * GMLP_WIDTH
    o2 = o1 + DIFF_QK_WIDTH
    o3 = o2 + DIFF_QK_WIDTH
    o4 = o3 + DIFF_V_WIDTH
    o5 = o4 + 3 * CONV_WIDTH
    za, zq, zk, zv, zc, zg = jnp.split(z, [o1, o2, o3, o4, o5], axis=-1)
    y_a = _chunked_sgu(za, w_sgu, b_sgu, sgu_ln_g, sgu_ln_b)
    y_b = _diff_attention(zq, zk, zv, cos, sin, lq1, lk1, lq2, lk2, subln_g, lambda_init)
    y_c = _short_conv(zc, conv_w)
    branches = jnp.stack([y_a, y_b, y_c], axis=2)
    branches = jnp.einsum('bsnc,ncd->bsnd', branches, w_branch)
    gates = jax.nn.sigmoid(zg.reshape(bsz, s_len, N_BRANCHES, D_MODEL))
    merged = jnp.sum(gates * branches, axis=2)
    return merged @ w_o


def _swiglu(x, w_gate_up, w_down):
    gate, up = jnp.split(x @ w_gate_up, 2, axis=-1)
    return (jax.nn.silu(gate) * up) @ w_down


def setup_inputs(seed: int = 0) -> dict:
    key = jax.random.key(seed)
    ks = jax.random.split(key, 24)
    f32 = jnp.float32

    def nrm(k, shape, fan_in, scale=1.0):
        return jax.random.normal(k, shape, f32) * (scale * fan_in ** -0.5)

    def gain(k, shape):
        return 1.0 + 0.01 * jax.random.normal(k, shape, f32)

    def small(k, shape, s=0.01):
        return s * jax.random.normal(k, shape, f32)

    x = jax.random.normal(ks[0], (BATCH, SEQ, D_MODEL), f32)
    positions = (jax.random.randint(ks[1], (BATCH, 1), 0, MAX_POS_OFFSET, dtype=jnp.int32)
                 + jnp.arange(SEQ, dtype=jnp.int32)[None, :])
    return {
        "x": x,
        "positions": positions,
        "w_in": nrm(ks[2], (DEPTH, D_MODEL, IN_WIDTH), D_MODEL),
        "w_sgu": nrm(ks[3], (DEPTH, SGU_GROUPS, CHUNK, CHUNK), CHUNK),
        "b_sgu": gain(ks[4], (DEPTH, SGU_GROUPS, CHUNK)),
        "sgu_ln_g": gain(ks[5], (DEPTH, GMLP_WIDTH)),
        "sgu_ln_b": small(ks[6], (DEPTH, GMLP_WIDTH)),
        "lambda_q1": small(ks[7], (DEPTH, DIFF_HEAD_DIM), 0.1),
        "lambda_k1": small(ks[8], (DEPTH, DIFF_HEAD_DIM), 0.1),
        "lambda_q2": small(ks[9], (DEPTH, DIFF_HEAD_DIM), 0.1),
        "lambda_k2": small(ks[10], (DEPTH, DIFF_HEAD_DIM), 0.1),
        "subln_g": gain(ks[11], (DEPTH, 2 * DIFF_HEAD_DIM)),
        "conv_w": nrm(ks[12], (DEPTH, CONV_WIDTH, CONV_K), CONV_K),
        "w_branch": nrm(ks[13], (DEPTH, N_BRANCHES, BRANCH_WIDTH, D_MODEL), BRANCH_WIDTH),
        "w_o": nrm(ks[14], (DEPTH, D_MODEL, D_MODEL), D_MODEL, BETA),
        "ln1_g": gain(ks[15], (DEPTH, D_MODEL)),
        "ln1_b": small(ks[16], (DEPTH, D_MODEL)),
        "w_gate_up": nrm(ks[17], (DEPTH, D_MODEL, 2 * D_FF), D_MODEL),
        "w_down": nrm(ks[18], (DEPTH, D_FF, D_MODEL), D_FF, BETA),
        "ln2_g": gain(ks[19], (DEPTH, D_MODEL)),
        "ln2_b": small(ks[20], (DEPTH, D_MODEL)),
    }


def reference(x, positions, w_in, w_sgu, b_sgu, sgu_ln_g, sgu_ln_b, lambda_q1, lambda_k1,
              lambda_q2, lambda_k2, subln_g, conv_w, w_branch, w_o, ln1_g, ln1_b,
              w_gate_up, w_down, ln2_g, ln2_b):
    cos, sin = _rope_tables(positions)
    for l in range(DEPTH):
        lambda_init = 0.8 - 0.6 * math.exp(-0.3 * l)
        mix = _mixer(x, cos, sin, w_in[l], w_sgu[l], b_sgu[l], sgu_ln_g[l], sgu_ln_b[l],
                     lambda_q1[l], lambda_k1[l], lambda_q2[l], lambda_k2[l], subln_g[l],
                     conv_w[l], w_branch[l], w_o[l], lambda_init)
        x = _layer_norm(ALPHA * x + mix, ln1_g[l], ln1_b[l])
        x = _layer_norm(ALPHA * x + _swiglu(x, w_gate_up[l], w_down[l]), ln2_g[l], ln2_b[l])
    return x
```

```python
import contextlib
import math
import numpy as np
import concourse.bass as bass
import concourse.mybir as mybir
from concourse.bass_utils import run_bass_kernel_spmd

F32 = mybir.dt.float32
BF16 = mybir.dt.bfloat16
I32 = mybir.dt.int32
AF = mybir.ActivationFunctionType
ALU = mybir.AluOpType
AX = mybir.AxisListType

ENGS = ("pe", "act", "dve", "pool", "sp")

D = 1024
KC = 8
DEPTH = 2
TT = 512
DFF = 2816
NFF = 22
ALPHA = (2 * DEPTH) ** 0.25
EPS = 1e-5
ROPE_THETA = 500000.0
N_CORES = 8

BLK_LEN = [4096] * 8 + [4608] * 8 + [4096] * 2 + [4096] * 11 + [2816] * 8
NBLK = len(BLK_LEN)
BLK_OFF = [0]
for _x in BLK_LEN:
    BLK_OFF.append(BLK_OFF[-1] + _x)
WCOLS = BLK_OFF[-1]
SLOT = 4608
NSLOT = 3
NKV = 4

SM_L = 48
SM_LN1G, SM_LN1B, SM_LN2G, SM_LN2B, SM_SUBG, SM_CONV = 0, 8, 16, 24, 32, 33
SM_FS = 96
SM_TRI = 98
SM_RM = SM_TRI + 128
SM_WSG = SM_RM + 128
NSM = SM_WSG + DEPTH * 512
RW_L = 1792
RW_G, RW_B, RW_BS, RW_LAM = 0, 512, 1024, 1536
NRW = DEPTH * RW_L

MAGIC = 12582912.0
TWO_PI = 2.0 * math.pi
CW1 = 6.28125
CW2 = TWO_PI - CW1
PI_LO = 3.1415925


class Buf:
    __slots__ = ("name", "w", "r")

    def __init__(self, name=""):
        self.name = name
        self.w = {}
        self.r = {}


class Op:
    __slots__ = ("eng", "fn", "deps", "signals", "dma_key", "tok", "idx")


class Prog:
    def __init__(self):
        self.ops = []
        self.dma_count = {}

    def op(self, eng, fn, reads=(), writes=(), dma_key=None):
        o = Op()
        o.eng = eng
        o.fn = fn
        o.dma_key = dma_key
        o.signals = False
        o.idx = len(self.ops)
        deps = set()
        for b in reads:
            deps.update(b.w.values())
        for b in writes:
            deps.update(b.w.values())
            deps.update(b.r.values())
        o.deps = deps
        key = ("dma", dma_key) if dma_key is not None else eng
        for b in reads:
            b.r[key] = o.idx
        for b in writes:
            b.w[key] = o.idx
        if dma_key is not None:
            n = self.dma_count.get(dma_key, 0) + 1
            self.dma_count[dma_key] = n
            o.tok = (("dma", dma_key), 16 * n)
        else:
            o.tok = None
        self.ops.append(o)
        return o

    @staticmethod
    def _skip(p, o):
        return p.dma_key is None and o.dma_key is None and p.eng == "pe" and o.eng == "pe"

    def resolve(self):
        ops = self.ops
        for o in ops:
            for d in o.deps:
                p = ops[d]
                if p.dma_key is None and not self._skip(p, o):
                    p.signals = True
        cnt = {e: 0 for e in ENGS}
        for o in ops:
            if o.dma_key is None and o.signals:
                cnt[o.eng] += 1
                o.tok = (o.eng, cnt[o.eng])
        waited = {e: {} for e in ENGS}
        self.waits = []
        for o in ops:
            need = {}
            for d in o.deps:
                p = ops[d]
                if self._skip(p, o) or p.tok is None:
                    continue
                s, v = p.tok
                if v > need.get(s, 0):
                    need[s] = v
            ws = []
            wd = waited[o.eng]
            for s, v in need.items():
                if v > wd.get(s, 0):
                    wd[s] = v
                    ws.append((s, v))
            self.waits.append(ws)
        return cnt

    def emit(self, nc, final_wait_tokens=()):
        self.resolve()
        sem_keys = list(ENGS) + [("dma", k) for k in self.dma_count]
        with contextlib.ExitStack() as st:
            sems = {}
            for k in sem_keys:
                nm = "s_" + (k if isinstance(k, str) else "d_" + str(k[1]))
                sems[k] = st.enter_context(nc.semaphore(nm))
            block = st.enter_context(nc.Block())
            per = {e: [] for e in ENGS}
            for o in self.ops:
                per[o.eng].append(o)
            waits = self.waits

            def run(engine_name, eng):
                for o in per[engine_name]:
                    for s, v in waits[o.idx]:
                        eng.wait_ge(sems[s], v)
                    ins = o.fn(eng)
                    if o.dma_key is not None:
                        ins.then_inc(sems[("dma", o.dma_key)], 16)
                    elif o.signals:
                        ins.then_inc(sems[o.eng], 1)
                if engine_name == "sp":
                    for (s, v) in final_wait_tokens:
                        eng.wait_ge(sems[s], v)

            @block.tensor
            def _(e):
                run("pe", e)

            @block.scalar
            def _(e):
                run("act", e)

            @block.vector
            def _(e):
                run("dve", e)

            @block.gpsimd
            def _(e):
                run("pool", e)

            @block.sync
            def _(e):
                run("sp", e)


class _Stop(Exception):
    pass


def build_program(S, lambda_inits, stop=None, dump=False):
    NT = S // TT
    nc = bass.Bass("TRN2", target_bir_lowering=False)
    P = Prog()

    xT_d = nc.dram_tensor("xT", [D, S], F32, kind="ExternalInput").ap()
    pos_d = nc.dram_tensor("pos", [1, S], I32, kind="ExternalInput").ap()
    wst_d = nc.dram_tensor("wst", [DEPTH, 128, WCOLS], F32, kind="ExternalInput").ap()
    sm_d = nc.dram_tensor("sm", [128, NSM], F32, kind="ExternalInput").ap()
    rw_d = nc.dram_tensor("rw", [1, NRW], F32, kind="ExternalInput").ap()
    outT_d = nc.dram_tensor("outT", [D, S], F32, kind="ExternalOutput").ap()
    wbf_d = nc.dram_tensor("wbf", [DEPTH, 128, WCOLS], BF16, kind="Internal").ap()
    cs_d = nc.dram_tensor("cs", [128, 2, S], F32, kind="Internal").ap()
    kt_d = nc.dram_tensor("ktc", [DEPTH, 4, 128, S], BF16, kind="Internal").ap()
    vv_d = nc.dram_tensor("vvc", [DEPTH, 4, S, 128], BF16, kind="Internal").ap()

    xT_v = xT_d.rearrange("(kc p) s -> p kc s", p=128)
    outT_v = outT_d.rearrange("(kc p) s -> p kc s", p=128)

    st = contextlib.ExitStack()
    with st:
        def sb(name, shape, dt):
            return st.enter_context(nc.sbuf_tensor(name, shape, dt))

        wring = sb("wring", [128, NSLOT, SLOT], BF16)
        kring = sb("kring", [128, NKV, 512], BF16)
        vring = sb("vring", [128, NKV, 512], BF16)
        A = sb("A", [128, KC, TT], F32)
        Bx = sb("Bx", [128, KC, TT], F32)
        xb = sb("xb", [128, KC, TT], BF16)
        smt = sb("smt", [128, NSM], F32)
        gbc = sb("gbc", [128, DEPTH, 2, 512], F32)
        wsg = sb("wsg", [128, DEPTH, 4, 128], BF16)
        trib = sb("trib", [128, 128], BF16)
        trif = sb("trif", [128, 128], F32)
        rmb = sb("rmb", [128, 128], BF16)
        onesb = sb("onesb", [128, 128], BF16)
        brow = sb("brow", [64, DEPTH, 4, 4, 128], BF16)
        cst = sb("cst", [128, 16], F32)
        carry = sb("carry", [128, DEPTH, 4, 2], F32)
        NPG = 44
        arena = sb("arena", [128, NPG, 512], BF16)
        pg = [Buf("pg%d" % i) for i in range(NPG)]

        def carve(lo, n, dt=BF16):
            ap = arena[:, lo:lo + n, :]
            if dt == F32:
                ap = arena[:, lo:lo + n, :].bitcast(F32)
            return ap, pg[lo:lo + n]

        u_ap = arena[:, 0:8, :].bitcast(F32).rearrange("p (j h) c -> p j (h c)", h=2)
        u_b = [pg[2 * j:2 * j + 2] for j in range(4)]
        vb_ap = arena[:, 8:12, :]
        vb_b = [pg[8 + i:9 + i] for i in range(4)]
        q_ap = arena[:, 12:16, :]
        q_b = [pg[12 + i:13 + i] for i in range(4)]
        kst_ap = arena[:, 16:20, :]
        kst_b = [pg[16 + i:17 + i] for i in range(4)]
        vst_ap = arena[:, 20:24, :]
        vst_b = [pg[20 + i:21 + i] for i in range(4)]
        ya_ap = arena[:, 24:28, :]
        ya_b = [pg[24 + i:25 + i] for i in range(4)]
        yb_ap = arena[:, 28:32, :]
        yb_b = [pg[28 + i:29 + i] for i in range(4)]
        yc_ap = arena[:, 32:36, :]
        yc_b = [pg[32 + i:33 + i] for i in range(4)]
        mg_ap = arena[:, 36:44, :]
        mg_b = [pg[36 + i:37 + i] for i in range(8)]
        hm_ap = arena[:, 0:22, :]
        hm_b = [pg[i:i + 1] for i in range(22)]
        sl_ap = arena[:, 22:26, :].bitcast(F32).rearrange("p (j h) c -> p j (h c)", h=2)
        sl_b = [pg[22:24], pg[24:26]]
        hb_ap = arena[:, 26:28, :]
        hb_b = [pg[26:27], pg[27:28]]
        sq_ap = arena[:, 28:30, :]
        sq_b = [pg[28:29], pg[29:30]]
        lnf_ap = arena[:, 30:36, :].bitcast(F32).rearrange("p (j h) c -> p j (h c)", h=2)
        lnf_b = [pg[30:32], pg[32:34], pg[34:36]]

        vt = sb("vt", [128, 2, 512], F32)
        vt_b = [Buf("vt0"), Buf("vt1")]
        vst6 = sb("vst6", [128, 2, 6], F32)
        vmv = sb("vmv", [128, 2, 4], F32)
        vs_b = [Buf("vs0"), Buf("vs1")]
        cst_t = sb("cs_t", [128, 2, 512], F32)
        cs_b = Buf("cs")
        raw = sb("raw", [128, 2, 512], BF16)
        raw_b = [Buf("raw0"), Buf("raw1")]
        rt1 = sb("rt1", [128, 2, 512], F32)
        rt1_b = [Buf("rt1_0"), Buf("rt1_1")]
        rt2 = sb("rt2", [128, 2, 512], F32)
        rt2_b = [Buf("rt2_0"), Buf("rt2_1")]
        cgc = sb("cgc", [128, 512], F32)
        cgc_b = Buf("cgc")
        hcv = sb("hcv", [128, 516], F32)
        hcv_b = Buf("hcv")
        cacc = sb("cacc", [128, 512], F32)
        cacc_b = Buf("cacc")
        sg = sb("sg", [128, 3, 512], F32)
        sg_b = [Buf("sg0"), Buf("sg1"), Buf("sg2")]
        mt = sb("mt", [128, 3, 512], F32)
        mt_b = [Buf("mt0"), Buf("mt1"), Buf("mt2")]
        NPT = 4
        pt = sb("pt", [128, NPT, 512], BF16)
        pt_b = [Buf("pt%d" % i) for i in range(NPT)]
        ap_t = sb("ap_t", [128, 5, 512], F32)
        ap_b = [Buf("apt%d" % i) for i in range(5)]
        apq = sb("apq", [128, 512], BF16)
        apq_b = Buf("apq")
        ostg = sb("ostg", [128, 2, 512], F32)
        ostg_b = [Buf("ostg0"), Buf("ostg1")]

        banks = [st.enter_context(nc.psum_tensor("bank%d" % i, [128, 512], F32)) for i in range(8)]
        bank_b = [Buf("bank%d" % i) for i in range(8)]

        A_b = [Buf("A%d" % i) for i in range(KC)]
        B_b = [Buf("B%d" % i) for i in range(KC)]
        xb_b = Buf("xb")
        const_b = Buf("const")
        carry_b = Buf("carry")
        wslot_b = [Buf("ws%d" % i) for i in range(NSLOT)]
        kvslot_b = [Buf("kv%d" % i) for i in range(NKV)]
        wbf_b = [[Buf("wbf%d_%d" % (l, b)) for b in range(NBLK)] for l in range(DEPTH)]
        csd_b = Buf("csd")
        ktd_b = [[Buf("ktd%d_%d" % (l, t)) for t in range(NT)] for l in range(DEPTH)]
        vvd_b = [[Buf("vvd%d_%d" % (l, t)) for t in range(NT)] for l in range(DEPTH)]
        out_toks = []

        gp_state = {"i": 0}

        def gp_bank(use_all=True):
            n = 4 if use_all else 2
            i = gp_state["i"] % n
            gp_state["i"] += 1
            return i

        if dump:
            P.op("pool", lambda e: e.memset(arena[:], 0.0), writes=pg)
            P.op("pool", lambda e: e.memset(pt[:], 0.0), writes=pt_b)
            P.op("pool", lambda e: e.memset(ap_t[:], 0.0), writes=ap_b)
        P.op("sp", lambda e: e.dma_start(out=smt[:], in_=sm_d), writes=[const_b], dma_key="c_sm")
        for l in range(DEPTH):
            P.op("sp", (lambda l: lambda e: e.dma_start(
                out=gbc[:, l, :, :], in_=rw_d[:, l * RW_L:l * RW_L + 1024].rearrange("o (a c) -> o a c", a=2).partition_broadcast(128)))(l),
                writes=[const_b], dma_key="c_gb")
        sf_ap = [arena[:, 0:9, :].bitcast(F32).rearrange("p a c -> p (a c)"), arena[:, 9:18, :].bitcast(F32).rearrange("p a c -> p (a c)")]
        sf_b = [pg[0:9], pg[9:18]]
        ob_ap = [arena[:, 18:23, :].rearrange("p a c -> p (a c)"), arena[:, 23:28, :].rearrange("p a c -> p (a c)")]
        ob_b = [pg[18:23], pg[23:28]]
        wbf_all_b = Buf("wbf_all")
        hi_ = 0
        for l in range(DEPTH):
            for b in range(NBLK):
                half = BLK_LEN[b] // 2
                for h in range(2):
                    c0 = BLK_OFF[b] + h * half
                    r = hi_ % 2
                    P.op("sp", (lambda l, c0, half, r: lambda e: e.dma_start(out=sf_ap[r][:, 0:half], in_=wst_d[l, :, c0:c0 + half]))(l, c0, half, r),
                         writes=sf_b[r], dma_key="wpi%d" % r)
                    if hi_ % 4 < 2:
                        P.op("dve", (lambda half, r: lambda e: e.tensor_copy(out=ob_ap[r][:, 0:half], in_=sf_ap[r][:, 0:half]))(half, r),
                             reads=sf_b[r], writes=ob_b[r])
                    else:
                        P.op("act", (lambda half, r: lambda e: e.activation(out=ob_ap[r][:, 0:half], in_=sf_ap[r][:, 0:half], func=AF.Copy))(half, r),
                             reads=sf_b[r], writes=ob_b[r])
                    P.op("sp", (lambda l, c0, half, r: lambda e: e.dma_start(out=wbf_d[l, :, c0:c0 + half], in_=ob_ap[r][:, 0:half]))(l, c0, half, r),
                         reads=ob_b[r], writes=[wbf_all_b], dma_key="wcst")
                    hi_ += 1
        P.op("dve", lambda e: e.memset(cst[:], 0.0), writes=[const_b])
        P.op("dve", lambda e: e.memset(cst[:, 0:1], EPS), writes=[const_b])
        P.op("dve", lambda e: e.memset(onesb[:], 1.0), writes=[const_b])
        P.op("dve", lambda e: e.memset(carry[:], 0.0), writes=[carry_b])
        P.op("dve", lambda e: e.memset(brow[:], 0.0), writes=[const_b])
        P.op("dve", lambda e: e.tensor_copy(out=trib[:], in_=smt[:, SM_TRI:SM_TRI + 128]), reads=[const_b], writes=[const_b])
        P.op("dve", lambda e: e.tensor_copy(out=trif[:], in_=smt[:, SM_TRI:SM_TRI + 128]), reads=[const_b], writes=[const_b])
        P.op("dve", lambda e: e.tensor_copy(out=rmb[:], in_=smt[:, SM_RM:SM_RM + 128]), reads=[const_b], writes=[const_b])
        for l in range(DEPTH):
            for g in range(4):
                o0 = SM_WSG + l * 512 + g * 128
                P.op("dve", (lambda l, g, o0: lambda e: e.tensor_tensor(out=wsg[:, l, g, :], in0=smt[:, o0:o0 + 128], in1=trif[:], op=ALU.mult))(l, g, o0),
                     reads=[const_b], writes=[const_b])
        tmpf = arena[:, 0:32, :].bitcast(F32)
        tmpf = tmpf.rearrange("p a c -> p (a c)")
        setup_b = pg[0:44]
        for l in range(DEPTH):
            bsrc = rw_d[:, l * RW_L + RW_BS:l * RW_L + RW_BS + 512]
            P.op("sp", (lambda bsrc: lambda e: e.dma_start(out=tmpf[0:1, 0:512], in_=bsrc))(bsrc), writes=setup_b, dma_key="c_b")
            bh = arena[0:1, 40, :]
            bl = arena[0:1, 41, :]
            P.op("dve", lambda e: e.tensor_copy(out=bh, in_=tmpf[0:1, 0:512]), reads=setup_b, writes=setup_b)
            P.op("dve", lambda e: e.tensor_copy(out=tmpf[0:1, 512:1024], in_=bh), reads=setup_b, writes=setup_b)
            P.op("dve", lambda e: e.tensor_tensor(out=tmpf[0:1, 1024:1536], in0=tmpf[0:1, 0:512], in1=tmpf[0:1, 512:1024], op=ALU.subtract), reads=setup_b, writes=setup_b)
            P.op("dve", lambda e: e.tensor_copy(out=bl, in_=tmpf[0:1, 1024:1536]), reads=setup_b, writes=setup_b)
            for sub in range(4):
                P.op("sp", (lambda l, sub, bh: lambda e: e.dma_start(out=brow[0:1, l, :, sub, :], in_=bh.rearrange("o (g i) -> o g i", g=4)))(l, sub, bh),
                     reads=setup_b, writes=[const_b], dma_key="c_b")
                P.op("sp", (lambda l, sub, bl: lambda e: e.dma_start(out=brow[32:33, l, :, sub, :], in_=bl.rearrange("o (g i) -> o g i", g=4)))(l, sub, bl),
                     reads=setup_b, writes=[const_b], dma_key="c_b")
            lsrc = rw_d[:, l * RW_L + RW_LAM:l * RW_L + RW_LAM + 256].partition_broadcast(128)
            P.op("sp", (lambda lsrc: lambda e: e.dma_start(out=tmpf[:, 2048:2304], in_=lsrc))(lsrc), writes=setup_b, dma_key="c_b")
            P.op("dve", lambda e: e.tensor_tensor(out=tmpf[:, 2304:2368], in0=tmpf[:, 2048:2112], in1=tmpf[:, 2112:2176], op=ALU.mult), reads=setup_b, writes=setup_b)
            P.op("dve", lambda e: e.tensor_tensor(out=tmpf[:, 2368:2432], in0=tmpf[:, 2176:2240], in1=tmpf[:, 2240:2304], op=ALU.mult), reads=setup_b, writes=setup_b)
            P.op("dve", lambda e: e.reduce_sum(out=tmpf[:, 2432:2434], in_=tmpf[:, 2304:2432].rearrange("p (a c) -> p a c", a=2), axis=AX.X), reads=setup_b, writes=setup_b)
            P.op("act", lambda e: e.activation(out=tmpf[:, 2434:2436], in_=tmpf[:, 2432:2434], func=AF.Exp), reads=setup_b, writes=setup_b)
            li = float(lambda_inits[l])
            P.op("dve", (lambda l, li: lambda e: e.scalar_tensor_tensor(out=cst[:, 1 + l:2 + l], in0=tmpf[:, 2435:2436], scalar=-li, in1=tmpf[:, 2434:2435], op0=ALU.add, op1=ALU.subtract))(l, li),
                 reads=setup_b, writes=[const_b])
            P.op("dve", (lambda l, li: lambda e: e.tensor_scalar(out=cst[:, 4 + l:5 + l], in0=smt[:, l * SM_L + SM_SUBG:l * SM_L + SM_SUBG + 1], scalar1=1.0 - li, scalar2=None, op0=ALU.mult))(l, li),
                 reads=[const_b], writes=[const_b])
        RW = min(S, 2048)
        posi = arena[:, 32:32 + RW // 256, :].bitcast(I32).rearrange("p a c -> p (a c)")
        for r0 in range(0, S, RW):
            ang = tmpf[:, 0:RW]
            kk = tmpf[:, RW:2 * RW]
            yy = tmpf[:, 2 * RW:3 * RW]
            zz = tmpf[:, 3 * RW:4 * RW]
            P.op("sp", (lambda r0: lambda e: e.dma_start(out=posi, in_=pos_d[:, r0:r0 + RW].partition_broadcast(128)))(r0), writes=setup_b, dma_key="c_b")
            P.op("dve", lambda e: e.tensor_copy(out=ang, in_=posi), reads=setup_b, writes=setup_b)
            P.op("dve", lambda e: e.tensor_scalar(out=ang, in0=ang, scalar1=smt[:, SM_FS:SM_FS + 1], scalar2=None, op0=ALU.mult), reads=setup_b + [const_b], writes=setup_b)
            P.op("dve", lambda e: e.tensor_scalar(out=kk, in0=ang, scalar1=1.0 / TWO_PI, scalar2=MAGIC, op0=ALU.mult, op1=ALU.add), reads=setup_b, writes=setup_b)
            P.op("dve", lambda e: e.tensor_scalar(out=kk, in0=kk, scalar1=-MAGIC, scalar2=None, op0=ALU.add), reads=setup_b, writes=setup_b)
            P.op("dve", lambda e: e.scalar_tensor_tensor(out=yy, in0=kk, scalar=-CW1, in1=ang, op0=ALU.mult, op1=ALU.add), reads=setup_b, writes=setup_b)
            P.op("dve", lambda e: e.scalar_tensor_tensor(out=yy, in0=kk, scalar=-CW2, in1=yy, op0=ALU.mult, op1=ALU.add), reads=setup_b, writes=setup_b)
            P.op("dve", lambda e: e.tensor_scalar(out=zz, in0=yy, scalar1=PI_LO, scalar2=-PI_LO, op0=ALU.min, op1=ALU.max), reads=setup_b, writes=setup_b)
            P.op("act", lambda e: e.activation(out=zz, in_=zz, func=AF.Sin), reads=setup_b, writes=setup_b)
            P.op("sp", (lambda r0: lambda e: e.dma_start(out=cs_d[:, 1, r0:r0 + RW], in_=zz))(r0), reads=setup_b, writes=[csd_b], dma_key="c_cs")
            P.op("dve", lambda e: e.tensor_scalar(out=yy, in0=yy, scalar1=math.pi / 2, scalar2=None, op0=ALU.add), reads=setup_b, writes=setup_b)
            P.op("dve", lambda e: e.tensor_scalar(out=kk, in0=yy, scalar1=math.pi, scalar2=None, op0=ALU.is_gt), reads=setup_b, writes=setup_b)
            P.op("dve", lambda e: e.scalar_tensor_tensor(out=yy, in0=kk, scalar=-TWO_PI, in1=yy, op0=ALU.mult, op1=ALU.add), reads=setup_b, writes=setup_b)
            P.op("dve", lambda e: e.tensor_scalar(out=ang, in0=yy, scalar1=PI_LO, scalar2=-PI_LO, op0=ALU.min, op1=ALU.max), reads=setup_b + [csd_b], writes=setup_b)
            P.op("act", lambda e: e.activation(out=ang, in_=ang, func=AF.Sin), reads=setup_b, writes=setup_b)
            P.op("sp", (lambda r0: lambda e: e.dma_start(out=cs_d[:, 0, r0:r0 + RW], in_=ang))(r0), reads=setup_b, writes=[csd_b], dma_key="c_cs")

        wstate = {"emitted": 0}
        wseq = [(t, l, b) for t in range(NT) for l in range(DEPTH) for b in range(NBLK)]

        def wload_upto(n):
            while wstate["emitted"] <= min(n, len(wseq) - 1):
                k = wstate["emitted"]
                t, l, b = wseq[k]
                s = k % NSLOT
                P.op("sp", (lambda l, b, s: lambda e: e.dma_start(out=wring[:, s, 0:BLK_LEN[b]], in_=wbf_d[l, :, BLK_OFF[b]:BLK_OFF[b + 1]]))(l, b, s),
                     reads=[wbf_all_b], writes=[wslot_b[s]], dma_key="w%d" % s)
                wstate["emitted"] += 1

        def wblock(t, l, b):
            n = (t * DEPTH + l) * NBLK + b
            wload_upto(n + NSLOT - 1)
            s = n % NSLOT
            return wring[:, s, :], wslot_b[s]

        kvstate = {"n": 0}

        def kvload(l, c, kb):
            n = kvstate["n"]
            kvstate["n"] += 1
            s = n % NKV
            P.op("sp", (lambda l, c, kb, s: lambda e: e.dma_start(out=kring[:, s, :], in_=kt_d[l, c, :, kb * TT:(kb + 1) * TT]))(l, c, kb, s),
                 reads=[ktd_b[l][kb], vvd_b[l][kb]], writes=[kvslot_b[s]], dma_key="kk%d" % s)
            P.op("sp", (lambda l, c, kb, s: lambda e: e.dma_start(
                out=vring[:, s, :].rearrange("p (a c) -> p a c", a=4),
                in_=vv_d[l, c, kb * TT:(kb + 1) * TT, :].rearrange("(a p) c -> p a c", p=128)))(l, c, kb, s),
                reads=[ktd_b[l][kb], vvd_b[l][kb]], writes=[kvslot_b[s]], dma_key="kv%d" % s)
            return s

        def mm(out, lhsT, rhs, start, stop, reads, bank):
            P.op("pe", lambda e: e.matmul(out, lhsT=lhsT, rhs=rhs, start=start, stop=stop), reads=reads, writes=[bank_b[bank]])

        def layer_norm(src_ap, src_b, l, g_col, b_col, last):
            bs, bq = 6, 7
            for kc in range(KC):
                r = kc % 2
                P.op("act", (lambda kc, r: lambda e: e.activation(out=hb_ap[:, r, :], in_=src_ap[:, kc, :], func=AF.Copy))(kc, r),
                     reads=[src_b[kc]], writes=hb_b[r])
                P.op("act", (lambda kc, r: lambda e: e.activation(out=sq_ap[:, r, :], in_=src_ap[:, kc, :], func=AF.Square))(kc, r),
                     reads=[src_b[kc]], writes=sq_b[r])
                mm(banks[bs][:], onesb[:], hb_ap[:, r, :], kc == 0, kc == KC - 1, hb_b[r] + [const_b], bs)
                mm(banks[bq][:], onesb[:], sq_ap[:, r, :], kc == 0, kc == KC - 1, sq_b[r] + [const_b], bq)
            mean = lnf_ap[:, 0, :]
            rstd = lnf_ap[:, 1, :]
            msq = lnf_ap[:, 2, :]
            P.op("dve", lambda e: e.tensor_scalar(out=mean, in0=banks[bs][:], scalar1=1.0 / D, scalar2=None, op0=ALU.mult), reads=[bank_b[bs]], writes=lnf_b[0])
            P.op("dve", lambda e: e.tensor_tensor(out=msq, in0=mean, in1=mean, op=ALU.mult), reads=lnf_b[0], writes=lnf_b[2])
            P.op("dve", lambda e: e.scalar_tensor_tensor(out=msq, in0=banks[bq][:], scalar=1.0 / D, in1=msq, op0=ALU.mult, op1=ALU.subtract), reads=[bank_b[bq]] + lnf_b[2], writes=lnf_b[2])
            P.op("act", lambda e: e.activation(out=rstd, in_=msq, func=AF.Sqrt, bias=cst[:, 0:1], scale=1.0), reads=lnf_b[2] + [const_b], writes=lnf_b[1])
            P.op("dve", lambda e: e.reciprocal(out=rstd, in_=rstd), reads=lnf_b[1], writes=lnf_b[1])
            for kc in range(KC):
                r = kc % 2
                gcol = smt[:, l * SM_L + g_col + kc:l * SM_L + g_col + kc + 1]
                bcol = smt[:, l * SM_L + b_col + kc:l * SM_L + b_col + kc + 1]
                tmp = sl_ap[:, r, :]
                P.op("dve", (lambda kc, tmp: lambda e: e.tensor_tensor(out=tmp, in0=src_ap[:, kc, :], in1=mean, op=ALU.subtract))(kc, tmp),
                     reads=[src_b[kc]] + lnf_b[0], writes=sl_b[r])
                P.op("pool", (lambda tmp: lambda e: e.tensor_tensor(out=tmp, in0=tmp, in1=rstd, op=ALU.mult))(tmp),
                     reads=sl_b[r] + lnf_b[1], writes=sl_b[r])
                if not last:
                    P.op("act", (lambda kc, tmp, gcol, bcol: lambda e: e.activation(out=src_ap[:, kc, :], in_=tmp, func=AF.Identity, scale=gcol, bias=bcol))(kc, tmp, gcol, bcol),
                         reads=sl_b[r] + [const_b], writes=[src_b[kc]])
                    P.op("act", (lambda kc, tmp, gcol, bcol: lambda e: e.activation(out=xb[:, kc, :], in_=tmp, func=AF.Identity, scale=gcol, bias=bcol))(kc, tmp, gcol, bcol),
                         reads=sl_b[r] + [const_b], writes=[xb_b])
                else:
                    P.op("act", (lambda kc, tmp, gcol, bcol, r: lambda e: e.activation(out=ostg[:, r, :], in_=tmp, func=AF.Identity, scale=gcol, bias=bcol))(kc, tmp, gcol, bcol, r),
                         reads=sl_b[r] + [const_b], writes=[ostg_b[r]])
                    yield kc, r

        dbg_d = nc.dram_tensor("dbg", [128, 36, 512], BF16, kind="ExternalOutput").ap() if dump else None
        dbgu_d = nc.dram_tensor("dbgu", [128, 4, 512], F32, kind="ExternalOutput").ap() if dump else None
        dbgp_d = nc.dram_tensor("dbgp", [128, 4, 512], BF16, kind="ExternalOutput").ap() if dump else None
        dbga_d = nc.dram_tensor("dbga", [128, 5, 512], F32, kind="ExternalOutput").ap() if dump else None

        def stage_done(k):
            if stop is not None and k >= stop:
                raise _Stop()

        def tile_layer(t, l):
            stage_done(0)
            t0 = t * TT
            sml = l * SM_L
            if l == 0:
                P.op("sp", lambda e: e.dma_start(out=A[:], in_=xT_v[:, :, t0:t0 + TT]), writes=A_b, dma_key="xin")
                P.op("pool", lambda e: e.tensor_copy(out=xb[:], in_=A[:]), reads=A_b, writes=[xb_b])
                P.op("sp", lambda e: e.dma_start(out=cst_t[:], in_=cs_d[:, :, t0:t0 + TT]), reads=[csd_b], writes=[cs_b], dma_key="csin")
            Ct = cst_t[:, 0, :]
            St = cst_t[:, 1, :]

            wv, wb_ = wblock(t, l, 0)
            wv3 = wv[:, 0:4096].rearrange("p (kc c) -> p kc c", kc=KC)
            for sub in range(4):
                bk = gp_bank()
                for kc in range(KC):
                    mm(banks[bk][:], xb[:, kc, sub * 128:(sub + 1) * 128], wv3[:, kc, :], kc == 0, kc == KC - 1, [xb_b, wb_], bk)
                r = sub % 2
                P.op("act", (lambda bk, r: lambda e: e.activation(out=vt[:, r, :], in_=banks[bk][:], func=AF.Gelu))(bk, r),
                     reads=[bank_b[bk]], writes=[vt_b[r]])
                P.op("dve", (lambda r: lambda e: e.bn_stats(out=vst6[:, r, :], in_=vt[:, r, :]))(r), reads=[vt_b[r]], writes=[vs_b[r]])
                P.op("dve", (lambda r: lambda e: e.bn_aggr(out=vmv[:, r, 0:2], in_=vst6[:, r, :]))(r), reads=[vs_b[r]], writes=[vs_b[r]])
                P.op("act", (lambda r: lambda e: e.activation(out=vmv[:, r, 2:3], in_=vmv[:, r, 1:2], func=AF.Sqrt, bias=cst[:, 0:1], scale=1.0))(r),
                     reads=[vs_b[r], const_b], writes=[vs_b[r]])
                P.op("dve", (lambda r: lambda e: e.reciprocal(out=vmv[:, r, 3:4], in_=vmv[:, r, 2:3]))(r), reads=[vs_b[r]], writes=[vs_b[r]])
                P.op("dve", (lambda r: lambda e: e.tensor_scalar(out=vt[:, r, :], in0=vt[:, r, :], scalar1=vmv[:, r, 0:1], scalar2=vmv[:, r, 3:4], op0=ALU.subtract, op1=ALU.mult))(r),
                     reads=[vt_b[r], vs_b[r]], writes=[vt_b[r]])
                P.op("pool", (lambda r: lambda e: e.tensor_tensor(out=vt[:, r, :], in0=vt[:, r, :], in1=gbc[:, l, 0, :], op=ALU.mult))(r),
                     reads=[vt_b[r], const_b], writes=[vt_b[r]])
                P.op("pool", (lambda r, sub: lambda e: e.tensor_tensor(out=vb_ap[:, sub, :], in0=vt[:, r, :], in1=gbc[:, l, 1, :], op=ALU.add))(r, sub),
                     reads=[vt_b[r], const_b], writes=vb_b[sub])
            wv, wb_ = wblock(t, l, 1)
            wv3 = wv[:, 0:4096].rearrange("p (kc c) -> p kc c", kc=KC)
            for j in range(4):
                bk = gp_bank()
                for kc in range(KC):
                    mm(banks[bk][:], wv3[:, kc, j * 128:(j + 1) * 128], xb[:, kc, :], kc == 0, kc == KC - 1, [xb_b, wb_], bk)
                P.op("act", (lambda bk, j: lambda e: e.activation(out=u_ap[:, j, :], in_=banks[bk][:], func=AF.Gelu))(bk, j),
                     reads=[bank_b[bk]], writes=u_b[j])
            for bi, (dst_ap, dst_b) in ((2, (q_ap, q_b)), (3, (kst_ap, kst_b))):
                wv, wb_ = wblock(t, l, bi)
                wv3 = wv[:, 0:4096].rearrange("p (kc c) -> p kc c", kc=KC)
                for j in range(4):
                    bk = gp_bank()
                    for kc in range(KC):
                        mm(banks[bk][:], wv3[:, kc, j * 128:(j + 1) * 128], xb[:, kc, :], kc == 0, kc == KC - 1, [xb_b, wb_], bk)
                    r = j % 2
                    P.op("act", (lambda bk, r: lambda e: e.activation(out=raw[:, r, :], in_=banks[bk][:], func=AF.Copy))(bk, r),
                         reads=[bank_b[bk]], writes=[raw_b[r], bank_b[bk]])
                    P.op("dve", (lambda bk, r: lambda e: e.tensor_tensor(out=rt1[:, r, :], in0=banks[bk][:], in1=Ct, op=ALU.mult))(bk, r),
                         reads=[bank_b[bk], cs_b], writes=[rt1_b[r]])
                    bk2 = gp_bank()
                    mm(banks[bk2][:], rmb[:], raw[:, r, :], True, True, [raw_b[r], const_b], bk2)
                    P.op("dve", (lambda bk2, r: lambda e: e.tensor_tensor(out=rt2[:, r, :], in0=banks[bk2][:], in1=St, op=ALU.mult))(bk2, r),
                         reads=[bank_b[bk2], cs_b], writes=[rt2_b[r]])
                    P.op("pool", (lambda r, j, dst_ap: lambda e: e.tensor_tensor(out=dst_ap[:, j, :], in0=rt1[:, r, :], in1=rt2[:, r, :], op=ALU.add))(r, j, dst_ap),
                         reads=[rt1_b[r], rt2_b[r]], writes=dst_b[j])
            for c in range(4):
                P.op("sp", (lambda c: lambda e: e.dma_start(out=kt_d[l, c, :, t0:t0 + TT], in_=kst_ap[:, c, :]))(c),
                     reads=kst_b[c], writes=[ktd_b[l][t]], dma_key="kvw")
            wv, wb_ = wblock(t, l, 4)
            wv3 = wv[:, 0:4096].rearrange("p (kc c) -> p kc c", kc=KC)
            for sub in range(4):
                bk = gp_bank()
                for kc in range(KC):
                    mm(banks[bk][:], xb[:, kc, sub * 128:(sub + 1) * 128], wv3[:, kc, :], kc == 0, kc == KC - 1, [xb_b, wb_], bk)
                P.op("act", (lambda bk, sub: lambda e: e.activation(out=vst_ap[:, sub, :], in_=banks[bk][:], func=AF.Copy))(bk, sub),
                     reads=[bank_b[bk]], writes=vst_b[sub])
            for c in range(4):
                P.op("sp", (lambda c: lambda e: e.dma_start(
                    out=vv_d[l, c, t0:t0 + TT, :].rearrange("(a p) c -> p a c", p=128),
                    in_=vst_ap[:, :, c * 128:(c + 1) * 128]))(c),
                    reads=[b for bb in vst_b for b in bb], writes=[vvd_b[l][t]], dma_key="kvw")
            for ci in range(12):
                j, kind = ci // 3, ci % 3
                bi, cc = 5 + ci // 4, ci % 4
                if cc == 0:
                    wv, wb_ = wblock(t, l, bi)
                    wv3 = wv[:, 0:4096].rearrange("p (kc c) -> p kc c", kc=KC)
                bk = gp_bank()
                for kc in range(KC):
                    mm(banks[bk][:], wv3[:, kc, cc * 128:(cc + 1) * 128], xb[:, kc, :], kc == 0, kc == KC - 1, [xb_b, wb_], bk)
                if kind == 0:
                    P.op("act", (lambda bk: lambda e: e.activation(out=cgc[:], in_=banks[bk][:], func=AF.Copy))(bk), reads=[bank_b[bk]], writes=[cgc_b])
                elif kind == 1:
                    P.op("pool", (lambda j: lambda e: e.tensor_copy(out=hcv[:, 0:2], in_=carry[:, l, j, :]))(j), reads=[carry_b], writes=[hcv_b])
                    P.op("dve", (lambda bk: lambda e: e.tensor_tensor(out=hcv[:, 2:514], in0=cgc[:], in1=banks[bk][:], op=ALU.mult))(bk),
                         reads=[cgc_b, bank_b[bk]], writes=[hcv_b])
                    P.op("pool", (lambda j: lambda e: e.tensor_copy(out=carry[:, l, j, :], in_=hcv[:, 512:514]))(j), reads=[hcv_b], writes=[carry_b])
                    w0 = smt[:, sml + SM_CONV + j * 3 + 0:sml + SM_CONV + j * 3 + 1]
                    w1 = smt[:, sml + SM_CONV + j * 3 + 1:sml + SM_CONV + j * 3 + 2]
                    w2 = smt[:, sml + SM_CONV + j * 3 + 2:sml + SM_CONV + j * 3 + 3]
                    P.op("dve", (lambda w0: lambda e: e.tensor_scalar(out=cacc[:], in0=hcv[:, 0:512], scalar1=w0, scalar2=None, op0=ALU.mult))(w0),
                         reads=[hcv_b, const_b], writes=[cacc_b])
                    P.op("dve", (lambda w1: lambda e: e.scalar_tensor_tensor(out=cacc[:], in0=hcv[:, 1:513], scalar=w1, in1=cacc[:], op0=ALU.mult, op1=ALU.add))(w1),
                         reads=[hcv_b, const_b, cacc_b], writes=[cacc_b])
                    P.op("dve", (lambda w2: lambda e: e.scalar_tensor_tensor(out=cacc[:], in0=hcv[:, 2:514], scalar=w2, in1=cacc[:], op0=ALU.mult, op1=ALU.add))(w2),
                         reads=[hcv_b, const_b, cacc_b], writes=[cacc_b])
                else:
                    P.op("dve", (lambda bk, j: lambda e: e.tensor_tensor(out=yc_ap[:, j, :], in0=cacc[:], in1=banks[bk][:], op=ALU.mult))(bk, j),
                         reads=[cacc_b, bank_b[bk]], writes=yc_b[j])
            for g in range(4):
                bk = gp_bank()
                mm(banks[bk][:], onesb[0:64, :], brow[0:64, l, g, :, :].rearrange("p a c -> p (a c)"), True, False, [const_b], bk)
                for sub in range(4):
                    mm(banks[bk][:, sub * 128:(sub + 1) * 128], vb_ap[:, sub, g * 128:(g + 1) * 128], wsg[:, l, g, :], False, sub == 3,
                       vb_b[sub] + [const_b], bk)
                P.op("dve", (lambda bk, g: lambda e: e.tensor_tensor(out=ya_ap[:, g, :], in0=u_ap[:, g, :], in1=banks[bk][:], op=ALU.mult))(bk, g),
                     reads=u_b[g] + [bank_b[bk]], writes=ya_b[g])
            stage_done(1)
            yield
            neglam = cst[:, 1 + l:2 + l]
            gsc = cst[:, 4 + l:5 + l]
            kvseq = [(c, kb) for c in range(4) for kb in range(t + 1)]
            kvslots = {}
            kvn = {"i": 0}

            def kv_prefetch(upto):
                while kvn["i"] <= min(upto, len(kvseq) - 1):
                    c_, kb_ = kvseq[kvn["i"]]
                    kvslots[(c_, kb_)] = kvload(l, c_, kb_)
                    kvn["i"] += 1

            for c in range(4):
                items = []
                for kb in range(t + 1):
                    for ks in range(4):
                        for hh in range(2):
                            items.append((kb, ks, hh))
                pend = None
                nit = len(items)
                for ii, (kb, ks, hh) in enumerate(items):
                    if (ks, hh) == (0, 0):
                        kv_prefetch(c * (t + 1) + kb + 2)
                    s = kvslots[(c, kb)]
                    diag = kb == t
                    q0 = ks * 128 if diag else 0
                    sbk = 2 + (ii % 2)
                    pi = ii % NPT
                    mm(banks[sbk][:, q0:TT], kring[hh * 64:(hh + 1) * 64, s, ks * 128:(ks + 1) * 128], q_ap[hh * 64:(hh + 1) * 64, c, q0:TT],
                       True, True, [kvslot_b[s]] + q_b[c], sbk)
                    P.op("act", (lambda sbk, pi, q0: lambda e: e.activation(out=pt[:, pi, q0:TT], in_=banks[sbk][:, q0:TT], func=AF.Exp, scale=0.125))(sbk, pi, q0),
                         reads=[bank_b[sbk]], writes=[pt_b[pi]])
                    if diag:
                        P.op("pool", (lambda pi, q0: lambda e: e.tensor_tensor(out=pt[:, pi, q0:q0 + 128], in0=pt[:, pi, q0:q0 + 128], in1=trib[:], op=ALU.mult))(pi, q0),
                             reads=[pt_b[pi], const_b], writes=[pt_b[pi]])
                    cur = (s, ks, hh, pi, q0, ii)
                    if pend is not None:
                        ps_, pks, phh, ppi, pq0, pii = pend
                        first = pii < 2
                        mm(banks[4 + phh][:, pq0:TT], vring[:, ps_, pks * 128:(pks + 1) * 128], pt[:, ppi, pq0:TT], first, pii >= nit - 2, [kvslot_b[ps_], pt_b[ppi]], 4 + phh)
                        mm(banks[6 + phh][:, pq0:TT], onesb[:], pt[:, ppi, pq0:TT], first, pii >= nit - 2, [const_b, pt_b[ppi]], 6 + phh)
                    pend = cur
                ps_, pks, phh, ppi, pq0, pii = pend
                mm(banks[4 + phh][:, pq0:TT], vring[:, ps_, pks * 128:(pks + 1) * 128], pt[:, ppi, pq0:TT], pii < 2, True, [kvslot_b[ps_], pt_b[ppi]], 4 + phh)
                mm(banks[6 + phh][:, pq0:TT], onesb[:], pt[:, ppi, pq0:TT], pii < 2, True, [const_b, pt_b[ppi]], 6 + phh)
                r1, o1, r2, o2, rr = (ap_t[:, i, :] for i in range(5))
                P.op("dve", lambda e: e.reciprocal(out=r1, in_=banks[6][:]), reads=[bank_b[6]], writes=[ap_b[0]])
                P.op("dve", lambda e: e.tensor_tensor(out=o1, in0=banks[4][:], in1=r1, op=ALU.mult), reads=[bank_b[4], ap_b[0]], writes=[ap_b[1]])
                P.op("dve", lambda e: e.reciprocal(out=r2, in_=banks[7][:]), reads=[bank_b[7]], writes=[ap_b[2]])
                P.op("dve", lambda e: e.tensor_tensor(out=o2, in0=banks[5][:], in1=r2, op=ALU.mult), reads=[bank_b[5], ap_b[2]], writes=[ap_b[3]])
                P.op("dve", lambda e: e.scalar_tensor_tensor(out=o1, in0=o2, scalar=neglam, in1=o1, op0=ALU.mult, op1=ALU.add), reads=[ap_b[3], ap_b[1], const_b], writes=[ap_b[1]])
                P.op("act", lambda e: e.activation(out=apq[:], in_=o1, func=AF.Square), reads=[ap_b[1]], writes=[apq_b])
                bk = gp_bank(False)
                mm(banks[bk][:], onesb[:], apq[:], True, True, [apq_b, const_b], bk)
                P.op("act", (lambda bk: lambda e: e.activation(out=rr, in_=banks[bk][:], func=AF.Sqrt, bias=cst[:, 0:1], scale=1.0 / 128))(bk),
                     reads=[bank_b[bk], const_b], writes=[ap_b[4]])
                P.op("dve", lambda e: e.reciprocal(out=rr, in_=rr), reads=[ap_b[4]], writes=[ap_b[4]])
                P.op("dve", (lambda c: lambda e: e.scalar_tensor_tensor(out=yb_ap[:, c, :], in0=o1, scalar=gsc, in1=rr, op0=ALU.mult, op1=ALU.mult))(c),
                     reads=[ap_b[1], ap_b[4], const_b], writes=yb_b[c])
                stage_done(2)
            yield
            ys = ((ya_ap, ya_b), (yb_ap, yb_b), (yc_ap, yc_b))
            for oc in range(8):
                wv, wb_ = wblock(t, l, 8 + oc)
                gw = wv[:, 0:3072].rearrange("p (n kc c) -> p n kc c", n=3, kc=KC)
                bw = wv[:, 3072:4608].rearrange("p (n kc c) -> p n kc c", n=3, kc=4)
                for n in range(3):
                    bk = gp_bank()
                    for kc in range(KC):
                        mm(banks[bk][:], gw[:, n, kc, :], xb[:, kc, :], kc == 0, kc == KC - 1, [xb_b, wb_], bk)
                    P.op("act", (lambda bk, n: lambda e: e.activation(out=sg[:, n, :], in_=banks[bk][:], func=AF.Sigmoid))(bk, n),
                         reads=[bank_b[bk]], writes=[sg_b[n]])
                for n in range(3):
                    bk = gp_bank()
                    y_ap, y_b = ys[n]
                    for kc in range(4):
                        mm(banks[bk][:], bw[:, n, kc, :], y_ap[:, kc, :], kc == 0, kc == 3, y_b[kc] + [wb_], bk)
                    P.op("dve", (lambda bk, n: lambda e: e.tensor_tensor(out=mt[:, n, :], in0=sg[:, n, :], in1=banks[bk][:], op=ALU.mult))(bk, n),
                         reads=[sg_b[n], bank_b[bk]], writes=[mt_b[n]])
                P.op("pool", lambda e: e.tensor_tensor(out=mt[:, 0, :], in0=mt[:, 0, :], in1=mt[:, 1, :], op=ALU.add), reads=[mt_b[0], mt_b[1]], writes=[mt_b[0]])
                P.op("pool", (lambda oc: lambda e: e.tensor_tensor(out=mg_ap[:, oc, :], in0=mt[:, 0, :], in1=mt[:, 2, :], op=ALU.add))(oc),
                     reads=[mt_b[0], mt_b[2]], writes=mg_b[oc])
            stage_done(3)
            yield
            for oc in range(8):
                if oc % 4 == 0:
                    wv, wb_ = wblock(t, l, 16 + oc // 4)
                    wv3 = wv[:, 0:4096].rearrange("p (kc c) -> p kc c", kc=KC)
                bk = gp_bank()
                for kc in range(KC):
                    mm(banks[bk][:], wv3[:, kc, (oc % 4) * 128:(oc % 4 + 1) * 128], mg_ap[:, kc, :], kc == 0, kc == KC - 1, mg_b[kc] + [wb_], bk)
                P.op("dve", (lambda bk, oc: lambda e: e.scalar_tensor_tensor(out=Bx[:, oc, :], in0=A[:, oc, :], scalar=ALPHA, in1=banks[bk][:], op0=ALU.mult, op1=ALU.add))(bk, oc),
                     reads=[A_b[oc], bank_b[bk]], writes=[B_b[oc]])
            for _ in layer_norm(Bx, B_b, l, SM_LN1G, SM_LN1B, False):
                pass
            stage_done(4)
            yield
            for jb in range(11):
                wv, wb_ = wblock(t, l, 18 + jb)
                wv3 = wv[:, 0:4096].rearrange("p (kc c) -> p kc c", kc=KC)
                for jj in range(2):
                    j = 2 * jb + jj
                    bg_ = gp_bank()
                    for kc in range(KC):
                        mm(banks[bg_][:], wv3[:, kc, (2 * jj) * 128:(2 * jj + 1) * 128], xb[:, kc, :], kc == 0, kc == KC - 1, [xb_b, wb_], bg_)
                    bu_ = gp_bank()
                    for kc in range(KC):
                        mm(banks[bu_][:], wv3[:, kc, (2 * jj + 1) * 128:(2 * jj + 2) * 128], xb[:, kc, :], kc == 0, kc == KC - 1, [xb_b, wb_], bu_)
                    r = j % 2
                    P.op("act", (lambda bg_, r: lambda e: e.activation(out=sl_ap[:, r, :], in_=banks[bg_][:], func=AF.Silu))(bg_, r),
                         reads=[bank_b[bg_]], writes=sl_b[r])
                    P.op("dve", (lambda bu_, r, j: lambda e: e.tensor_tensor(out=hm_ap[:, j, :], in0=sl_ap[:, r, :], in1=banks[bu_][:], op=ALU.mult))(bu_, r, j),
                         reads=sl_b[r] + [bank_b[bu_]], writes=hm_b[j])
                if jb % 4 == 3:
                    stage_done(5)
            yield
            for oc in range(8):
                wv, wb_ = wblock(t, l, 29 + oc)
                wv3 = wv[:, 0:2816].rearrange("p (kc c) -> p kc c", kc=NFF)
                bk = gp_bank()
                for kc in range(NFF):
                    mm(banks[bk][:], wv3[:, kc, :], hm_ap[:, kc, :], kc == 0, kc == NFF - 1, hm_b[kc] + [wb_], bk)
                P.op("dve", (lambda bk, oc: lambda e: e.scalar_tensor_tensor(out=A[:, oc, :], in0=Bx[:, oc, :], scalar=ALPHA, in1=banks[bk][:], op0=ALU.mult, op1=ALU.add))(bk, oc),
                     reads=[B_b[oc], bank_b[bk]], writes=[A_b[oc]])
            last = l == DEPTH - 1
            for res in layer_norm(A, A_b, l, SM_LN2G, SM_LN2B, last):
                kc, r = res
                o = P.op("sp", (lambda kc, r: lambda e: e.dma_start(out=outT_v[:, kc, t0:t0 + TT], in_=ostg[:, r, :]))(kc, r),
                         reads=[ostg_b[r]], writes=[], dma_key="out%d" % r)
                out_toks.append(o.tok)
            stage_done(6)
            yield

        try:
            for t in range(NT):
                for l in range(DEPTH):
                    for _ in tile_layer(t, l):
                        pass
        except _Stop:
            pass
        if stop is not None:
            o = P.op("sp", lambda e: e.dma_start(out=outT_v[:, :, 0:TT], in_=A[:]), reads=A_b + B_b, writes=[], dma_key="out0")
            out_toks.append(o.tok)
        if dump:
            o = P.op("sp", lambda e: e.dma_start(out=dbg_d, in_=arena[:, 8:44, :]), reads=pg, writes=[], dma_key="out1")
            out_toks.append(o.tok)
            o = P.op("sp", lambda e: e.dma_start(out=dbgu_d, in_=u_ap), reads=pg, writes=[], dma_key="out1")
            out_toks.append(o.tok)
            if stop is not None and stop >= 2:
                o = P.op("sp", lambda e: e.dma_start(out=dbgp_d, in_=pt[:]), reads=pt_b, writes=[], dma_key="out1")
                out_toks.append(o.tok)
                o = P.op("sp", lambda e: e.dma_start(out=dbga_d, in_=ap_t[:]), reads=ap_b, writes=[], dma_key="out1")
                out_toks.append(o.tok)

        fin = {}
        for s_, v_ in out_toks:
            fin[s_] = max(fin.get(s_, 0), v_)
        P.emit(nc, final_wait_tokens=list(fin.items()))
    return nc, len(P.ops)


def _wstream(w_in, w_branch, w_o, w_gate_up, w_down):
    def fm(cols):
        K = cols.shape[0]
        return cols.reshape(K // 128, 128, cols.shape[1]).transpose(1, 0, 2).reshape(128, -1)
    parts = []
    u = w_in[:, 0:512]
    v = w_in[:, 512:1024]
    q = w_in[:, 1024:1536]
    k = w_in[:, 1536:2048]
    vv = w_in[:, 2048:2560]
    bg = w_in[:, 2560:3072]
    cg = w_in[:, 3072:3584]
    xc = w_in[:, 3584:4096]
    zg = w_in[:, 4096:7168]
    conv_cols = []
    for j in range(4):
        for src in (cg, xc, bg):
            conv_cols.append(src[:, j * 128:(j + 1) * 128])
    conv = np.concatenate(conv_cols, axis=1)
    for blk in (v, u, q, k, vv, conv[:, 0:512], conv[:, 512:1024], conv[:, 1024:1536]):
        parts.append(fm(blk))
    for oc in range(8):
        for n in range(3):
            parts.append(fm(zg[:, n * 1024 + oc * 128:n * 1024 + (oc + 1) * 128]))
        for n in range(3):
            parts.append(fm(w_branch[n][:, oc * 128:(oc + 1) * 128]))
    for h in range(2):
        parts.append(fm(w_o[:, h * 512:(h + 1) * 512]))
    for jb in range(11):
        cols = []
        for jj in range(2):
            j = 2 * jb + jj
            cols.append(w_gate_up[:, j * 128:(j + 1) * 128])
            cols.append(w_gate_up[:, DFF + j * 128:DFF + (j + 1) * 128])
        parts.append(fm(np.concatenate(cols, axis=1)))
    for oc in range(8):
        parts.append(fm(w_down[:, oc * 128:(oc + 1) * 128]))
    out = np.concatenate(parts, axis=1)
    assert out.shape == (128, WCOLS), out.shape
    return out


def _const_tables():
    tri = (np.arange(128)[:, None] <= np.arange(128)[None, :]).astype(np.float32)
    rm = np.zeros((128, 128), np.float32)
    fs = np.zeros((128,), np.float32)
    inv_freq = (np.float32(ROPE_THETA) ** (-np.arange(0, 16, 2, dtype=np.float32) / np.float32(16))).astype(np.float32)
    for h in range(2):
        for d in range(16):
            src = d + 8 if d < 8 else d - 8
            rm[h * 64 + src, h * 64 + d] = 1.0
            fs[h * 64 + d] = -inv_freq[d] if d < 8 else inv_freq[d - 8]
    return tri, rm, fs


def _host_layout(inp, depth=DEPTH):
    f32 = np.float32
    tri, rm, fs = _const_tables()
    sm = np.zeros((128, NSM), f32)
    rw = np.zeros((1, NRW), f32)
    wst = np.zeros((depth, 128, WCOLS), f32)
    for l in range(depth):
        o = l * SM_L
        sm[:, o + SM_LN1G:o + SM_LN1G + 8] = inp["ln1_g"][l].reshape(8, 128).T
        sm[:, o + SM_LN1B:o + SM_LN1B + 8] = inp["ln1_b"][l].reshape(8, 128).T
        sm[:, o + SM_LN2G:o + SM_LN2G + 8] = inp["ln2_g"][l].reshape(8, 128).T
        sm[:, o + SM_LN2B:o + SM_LN2B + 8] = inp["ln2_b"][l].reshape(8, 128).T
        sm[:, o + SM_SUBG] = inp["subln_g"][l]
        sm[:, o + SM_CONV:o + SM_CONV + 12] = inp["conv_w"][l].reshape(4, 128, 3).transpose(1, 0, 2).reshape(128, 12)
        sm[:, SM_WSG + l * 512:SM_WSG + (l + 1) * 512] = inp["w_sgu"][l].transpose(2, 0, 1).reshape(128, 512)
        r = l * RW_L
        rw[0, r + RW_G:r + RW_G + 512] = inp["sgu_ln_g"][l]
        rw[0, r + RW_B:r + RW_B + 512] = inp["sgu_ln_b"][l]
        rw[0, r + RW_BS:r + RW_BS + 512] = inp["b_sgu"][l].reshape(512)
        rw[0, r + RW_LAM:r + RW_LAM + 256] = np.concatenate(
            [inp["lambda_q1"][l], inp["lambda_k1"][l], inp["lambda_q2"][l], inp["lambda_k2"][l]])
        wst[l] = _wstream(inp["w_in"][l], inp["w_branch"][l], inp["w_o"][l], inp["w_gate_up"][l], inp["w_down"][l])
    sm[:, SM_FS] = fs
    sm[:, SM_TRI:SM_TRI + 128] = tri
    sm[:, SM_RM:SM_RM + 128] = rm
    return sm, rw, wst


_CACHE = {}


def run_cores(inp, n_cores, S):
    inp = {k: np.asarray(v) for k, v in inp.items()}
    sm, rw, wst = _host_layout(inp)
    lambda_inits = [0.8 - 0.6 * math.exp(-0.3 * l) for l in range(DEPTH)]
    key = S
    if key not in _CACHE:
        _CACHE[key] = build_program(S, lambda_inits)[0]
    nc = _CACHE[key]
    in_maps = []
    for c in range(n_cores):
        in_maps.append({
            "xT": np.ascontiguousarray(inp["x"][c].T.astype(np.float32)),
            "pos": np.ascontiguousarray(inp["positions"][c].reshape(1, S).astype(np.int32)),
            "wst": wst, "sm": sm, "rw": rw,
        })
    res = run_bass_kernel_spmd(nc, in_maps, core_ids=list(range(n_cores)))
    out = np.stack([np.ascontiguousarray(r["outT"].T) for r in res.results], axis=0)
    return out.astype(np.float32)


def kernel(**inputs):
    x = np.asarray(inputs["x"])
    B, S, _ = x.shape
    assert B == N_CORES
    return run_cores(inputs, N_CORES, S)
```

```python
import contextlib
import math
import numpy as np
import concourse.bass as bass
import concourse.mybir as mybir
from concourse.bass_utils import run_bass_kernel_spmd

F32 = mybir.dt.float32
BF16 = mybir.dt.bfloat16
I32 = mybir.dt.int32
AF = mybir.ActivationFunctionType
ALU = mybir.AluOpType
AX = mybir.AxisListType

ENGS = ("pe", "act", "dve", "pool", "sp")

D = 1024
KC = 8
DEPTH = 2
TT = 512
DFF = 2816
NFF = 22
ALPHA = (2 * DEPTH) ** 0.25
EPS = 1e-5
ROPE_THETA = 500000.0
N_CORES = 8

BLK_LEN = [4096] * 8 + [4608] * 8 + [4096] * 2 + [4096] * 11 + [2816] * 8
NBLK = len(BLK_LEN)
BLK_OFF = [0]
for _x in BLK_LEN:
    BLK_OFF.append(BLK_OFF[-1] + _x)
WCOLS = BLK_OFF[-1]
SLOT = 4608
NSLOT = 3
NKV = 4

SM_L = 48
SM_LN1G, SM_LN1B, SM_LN2G, SM_LN2B, SM_SUBG, SM_CONV = 0, 8, 16, 24, 32, 33
SM_FS = 96
SM_TRI = 98
SM_RM = SM_TRI + 128
SM_WSG = SM_RM + 128
NSM = SM_WSG + DEPTH * 512
RW_L = 1792
RW_G, RW_B, RW_BS, RW_LAM = 0, 512, 1024, 1536
NRW = DEPTH * RW_L

MAGIC = 12582912.0
TWO_PI = 2.0 * math.pi
CW1 = 6.28125
CW2 = TWO_PI - CW1
PI_LO = 3.1415925


class Buf:
    __slots__ = ("name", "w", "r")

    def __init__(self, name=""):
        self.name = name
        self.w = {}
        self.r = {}


class Op:
    __slots__ = ("eng", "fn", "deps", "signals", "dma_key", "tok", "idx", "tag")


class Prog:
    def __init__(self):
        self.ops = []
        self.dma_count = {}
        self.tag = ""

    def op(self, eng, fn, reads=(), writes=(), dma_key=None):
        o = Op()
        o.eng = eng
        o.fn = fn
        o.dma_key = dma_key
        o.signals = False
        o.tag = self.tag
        o.idx = len(self.ops)
        deps = set()
        for b in reads:
            deps.update(b.w.values())
        for b in writes:
            deps.update(b.w.values())
            deps.update(b.r.values())
        o.deps = deps
        key = ("dma", dma_key) if dma_key is not None else eng
        for b in reads:
            b.r[key] = o.idx
        for b in writes:
            b.w[key] = o.idx
        if dma_key is not None:
            n = self.dma_count.get(dma_key, 0) + 1
            self.dma_count[dma_key] = n
            o.tok = (("dma", dma_key), 16 * n)
        else:
            o.tok = None
        self.ops.append(o)
        return o

    @staticmethod
    def _skip(p, o):
        return p.dma_key is None and o.dma_key is None and p.eng == "pe" and o.eng == "pe"

    def resolve(self):
        ops = self.ops
        for o in ops:
            for d in o.deps:
                p = ops[d]
                if p.dma_key is None and not self._skip(p, o):
                    p.signals = True
        cnt = {e: 0 for e in ENGS}
        for o in ops:
            if o.dma_key is None and o.signals:
                cnt[o.eng] += 1
                o.tok = (o.eng, cnt[o.eng])
        waited = {e: {} for e in ENGS}
        self.waits = []
        for o in ops:
            need = {}
            for d in o.deps:
                p = ops[d]
                if self._skip(p, o) or p.tok is None:
                    continue
                s, v = p.tok
                if v > need.get(s, 0):
                    need[s] = v
            ws = []
            wd = waited[o.eng]
            for s, v in need.items():
                if v > wd.get(s, 0):
                    wd[s] = v
                    ws.append((s, v))
            self.waits.append(ws)
        return cnt

    def emit(self, nc, final_wait_tokens=()):
        self.resolve()
        sem_keys = list(ENGS) + [("dma", k) for k in self.dma_count]
        with contextlib.ExitStack() as st:
            sems = {}
            for k in sem_keys:
                nm = "s_" + (k if isinstance(k, str) else "d_" + str(k[1]))
                sems[k] = st.enter_context(nc.semaphore(nm))
            block = st.enter_context(nc.Block())
            per = {e: [] for e in ENGS}
            for o in self.ops:
                per[o.eng].append(o)
            waits = self.waits

            def run(engine_name, eng):
                for o in per[engine_name]:
                    for s, v in waits[o.idx]:
                        eng.wait_ge(sems[s], v)
                    ins = o.fn(eng)
                    if o.dma_key is not None:
                        ins.then_inc(sems[("dma", o.dma_key)], 16)
                    elif o.signals:
                        ins.then_inc(sems[o.eng], 1)
                if engine_name == "sp":
                    for (s, v) in final_wait_tokens:
                        eng.wait_ge(sems[s], v)

            @block.tensor
            def _(e):
                run("pe", e)

            @block.scalar
            def _(e):
                run("act", e)

            @block.vector
            def _(e):
                run("dve", e)

            @block.gpsimd
            def _(e):
                run("pool", e)

            @block.sync
            def _(e):
                run("sp", e)


class _Stop(Exception):
    pass


def build_program(S, lambda_inits, stop=None, dump=False):
    NT = S // TT
    nc = bass.Bass("TRN2", target_bir_lowering=False)
    P = Prog()

    xT_d = nc.dram_tensor("xT", [D, S], F32, kind="ExternalInput").ap()
    pos_d = nc.dram_tensor("pos", [1, S], I32, kind="ExternalInput").ap()
    wst_d = nc.dram_tensor("wst", [DEPTH, 128, WCOLS], F32, kind="ExternalInput").ap()
    sm_d = nc.dram_tensor("sm", [128, NSM], F32, kind="ExternalInput").ap()
    rw_d = nc.dram_tensor("rw", [1, NRW], F32, kind="ExternalInput").ap()
    outT_d = nc.dram_tensor("outT", [D, S], F32, kind="ExternalOutput").ap()
    wbf_d = nc.dram_tensor("wbf", [DEPTH, 128, WCOLS], BF16, kind="Internal").ap()
    cs_d = nc.dram_tensor("cs", [128, 2, S], F32, kind="Internal").ap()
    kt_d = nc.dram_tensor("ktc", [DEPTH, 4, 128, S], BF16, kind="Internal").ap()
    vv_d = nc.dram_tensor("vvc", [DEPTH, 4, S, 128], BF16, kind="Internal").ap()

    xT_v = xT_d.rearrange("(kc p) s -> p kc s", p=128)
    outT_v = outT_d.rearrange("(kc p) s -> p kc s", p=128)

    st = contextlib.ExitStack()
    with st:
        def sb(name, shape, dt):
            return st.enter_context(nc.sbuf_tensor(name, shape, dt))

        wring = sb("wring", [128, NSLOT, SLOT], BF16)
        kring = sb("kring", [128, NKV, 512], BF16)
        vring = sb("vring", [128, NKV, 512], BF16)
        A = sb("A", [128, KC, TT], F32)
        Bx = sb("Bx", [128, KC, TT], F32)
        xb = sb("xb", [128, KC, TT], BF16)
        smt = sb("smt", [128, NSM], F32)
        gbc = sb("gbc", [128, DEPTH, 2, 512], F32)
        wsg = sb("wsg", [128, DEPTH, 4, 128], BF16)
        trib = sb("trib", [128, 128], BF16)
        trif = sb("trif", [128, 128], F32)
        rmb = sb("rmb", [128, 128], BF16)
        onesb = sb("onesb", [128, 128], BF16)
        brow = sb("brow", [64, DEPTH, 4, 4, 128], BF16)
        cst = sb("cst", [128, 16], F32)
        carry = sb("carry", [128, DEPTH, 4, 2], F32)
        NPG = 44
        arena = sb("arena", [128, NPG, 512], BF16)
        pg = [Buf("pg%d" % i) for i in range(NPG)]

        def carve(lo, n, dt=BF16):
            ap = arena[:, lo:lo + n, :]
            if dt == F32:
                ap = arena[:, lo:lo + n, :].bitcast(F32)
            return ap, pg[lo:lo + n]

        u_ap = arena[:, 0:8, :].bitcast(F32).rearrange("p (j h) c -> p j (h c)", h=2)
        u_b = [pg[2 * j:2 * j + 2] for j in range(4)]
        vb_ap = arena[:, 8:12, :]
        vb_b = [pg[8 + i:9 + i] for i in range(4)]
        q_ap = arena[:, 12:16, :]
        q_b = [pg[12 + i:13 + i] for i in range(4)]
        kst_ap = arena[:, 16:20, :]
        kst_b = [pg[16 + i:17 + i] for i in range(4)]
        vst_ap = arena[:, 20:24, :]
        vst_b = [pg[20 + i:21 + i] for i in range(4)]
        ya_ap = arena[:, 24:28, :]
        ya_b = [pg[24 + i:25 + i] for i in range(4)]
        yb_ap = arena[:, 28:32, :]
        yb_b = [pg[28 + i:29 + i] for i in range(4)]
        yc_ap = arena[:, 32:36, :]
        yc_b = [pg[32 + i:33 + i] for i in range(4)]
        mg_ap = arena[:, 36:44, :]
        mg_b = [pg[36 + i:37 + i] for i in range(8)]
        hm_ap = arena[:, 0:22, :]
        hm_b = [pg[i:i + 1] for i in range(22)]
        sl_ap = arena[:, 22:26, :].bitcast(F32).rearrange("p (j h) c -> p j (h c)", h=2)
        sl_b = [pg[22:24], pg[24:26]]
        hb_ap = arena[:, 26:28, :]
        hb_b = [pg[26:27], pg[27:28]]
        sq_ap = arena[:, 28:30, :]
        sq_b = [pg[28:29], pg[29:30]]
        lnf_ap = arena[:, 30:36, :].bitcast(F32).rearrange("p (j h) c -> p j (h c)", h=2)
        lnf_b = [pg[30:32], pg[32:34], pg[34:36]]

        vt = sb("vt", [128, 2, 512], F32)
        vt_b = [Buf("vt0"), Buf("vt1")]
        vst6 = sb("vst6", [128, 2, 6], F32)
        vmv = sb("vmv", [128, 2, 4], F32)
        vs_b = [Buf("vs0"), Buf("vs1")]
        cst_t = sb("cs_t", [128, 2, 512], F32)
        cs_b = Buf("cs")
        raw = sb("raw", [128, 2, 512], BF16)
        raw_b = [Buf("raw0"), Buf("raw1")]
        rt1 = sb("rt1", [128, 2, 512], F32)
        rt1_b = [Buf("rt1_0"), Buf("rt1_1")]
        rt2 = sb("rt2", [128, 2, 512], F32)
        rt2_b = [Buf("rt2_0"), Buf("rt2_1")]
        cgc = sb("cgc", [128, 512], F32)
        cgc_b = Buf("cgc")
        hcv = sb("hcv", [128, 516], F32)
        hcv_b = Buf("hcv")
        cacc = sb("cacc", [128, 512], F32)
        cacc_b = Buf("cacc")
        sg = sb("sg", [128, 3, 512], F32)
        sg_b = [Buf("sg0"), Buf("sg1"), Buf("sg2")]
        mt = sb("mt", [128, 3, 512], F32)
        mt_b = [Buf("mt0"), Buf("mt1"), Buf("mt2")]
        NPT = 3
        pt = sb("pt", [128, NPT, 2, 512], BF16)
        pt_b = [Buf("pt%d" % i) for i in range(NPT)]
        trib2 = sb("trib2", [128, 2, 128], BF16)
        ap_t = sb("ap_t", [128, 5, 512], F32)
        ap_b = [Buf("apt%d" % i) for i in range(5)]
        apq = sb("apq", [128, 512], BF16)
        apq_b = Buf("apq")
        ostg = sb("ostg", [128, 2, 512], F32)
        ostg_b = [Buf("ostg0"), Buf("ostg1")]

        pp = [st.enter_context(nc.psum_tensor("pp%d" % i, [128, 1024], F32)) for i in range(4)]
        banks = [pp[i // 2][:, (i % 2) * 512:(i % 2 + 1) * 512] for i in range(8)]
        bank_b = [Buf("bank%d" % i) for i in range(8)]

        A_b = [Buf("A%d" % i) for i in range(KC)]
        B_b = [Buf("B%d" % i) for i in range(KC)]
        xb_b = Buf("xb")
        const_b = Buf("const")
        carry_b = Buf("carry")
        wslot_b = [Buf("ws%d" % i) for i in range(NSLOT)]
        kvslot_b = [Buf("kv%d" % i) for i in range(NKV)]
        wbf_b = [[Buf("wbf%d_%d" % (l, b)) for b in range(NBLK)] for l in range(DEPTH)]
        csd_b = Buf("csd")
        ktd_b = [[Buf("ktd%d_%d" % (l, t)) for t in range(NT)] for l in range(DEPTH)]
        vvd_b = [[Buf("vvd%d_%d" % (l, t)) for t in range(NT)] for l in range(DEPTH)]
        out_toks = []

        gp_state = {"i": 0}

        def gp_bank(use_all=True):
            n = 4 if use_all else 2
            i = gp_state["i"] % n
            gp_state["i"] += 1
            return i

        if dump:
            P.op("pool", lambda e: e.memset(arena[:], 0.0), writes=pg)
            P.op("pool", lambda e: e.memset(pt[:], 0.0), writes=pt_b)
            P.op("pool", lambda e: e.memset(ap_t[:], 0.0), writes=ap_b)
        P.op("sp", lambda e: e.dma_start(out=smt[:], in_=sm_d), writes=[const_b], dma_key="c_sm")
        for l in range(DEPTH):
            P.op("sp", (lambda l: lambda e: e.dma_start(
                out=gbc[:, l, :, :], in_=rw_d[:, l * RW_L:l * RW_L + 1024].rearrange("o (a c) -> o a c", a=2).partition_broadcast(128)))(l),
                writes=[const_b], dma_key="c_gb")
        sf_ap = [arena[:, 0:9, :].bitcast(F32).rearrange("p a c -> p (a c)"), arena[:, 9:18, :].bitcast(F32).rearrange("p a c -> p (a c)")]
        sf_b = [pg[0:9], pg[9:18]]
        ob_ap = [arena[:, 18:23, :].rearrange("p a c -> p (a c)"), arena[:, 23:28, :].rearrange("p a c -> p (a c)")]
        ob_b = [pg[18:23], pg[23:28]]
        wbf_all_b = Buf("wbf_all")
        hi_ = 0
        for l in range(DEPTH):
            for b in range(NBLK):
                half = BLK_LEN[b] // 2
                for h in range(2):
                    c0 = BLK_OFF[b] + h * half
                    r = hi_ % 2
                    P.op("sp", (lambda l, c0, half, r: lambda e: e.dma_start(out=sf_ap[r][:, 0:half], in_=wst_d[l, :, c0:c0 + half]))(l, c0, half, r),
                         writes=sf_b[r], dma_key="wpi%d" % r)
                    if hi_ % 4 < 2:
                        P.op("dve", (lambda half, r: lambda e: e.tensor_copy(out=ob_ap[r][:, 0:half], in_=sf_ap[r][:, 0:half]))(half, r),
                             reads=sf_b[r], writes=ob_b[r])
                    else:
                        P.op("act", (lambda half, r: lambda e: e.activation(out=ob_ap[r][:, 0:half], in_=sf_ap[r][:, 0:half], func=AF.Copy))(half, r),
                             reads=sf_b[r], writes=ob_b[r])
                    P.op("sp", (lambda l, c0, half, r: lambda e: e.dma_start(out=wbf_d[l, :, c0:c0 + half], in_=ob_ap[r][:, 0:half]))(l, c0, half, r),
                         reads=ob_b[r], writes=[wbf_all_b], dma_key="wcst")
                    hi_ += 1
        P.op("dve", lambda e: e.memset(cst[:], 0.0), writes=[const_b])
        P.op("dve", lambda e: e.memset(cst[:, 0:1], EPS), writes=[const_b])
        P.op("dve", lambda e: e.memset(onesb[:], 1.0), writes=[const_b])
        P.op("dve", lambda e: e.memset(carry[:], 0.0), writes=[carry_b])
        P.op("dve", lambda e: e.memset(brow[:], 0.0), writes=[const_b])
        P.op("dve", lambda e: e.tensor_copy(out=trib[:], in_=smt[:, SM_TRI:SM_TRI + 128]), reads=[const_b], writes=[const_b])
        P.op("dve", lambda e: e.tensor_copy(out=trif[:], in_=smt[:, SM_TRI:SM_TRI + 128]), reads=[const_b], writes=[const_b])
        for hh_ in range(2):
            P.op("dve", (lambda hh_: lambda e: e.tensor_copy(out=trib2[:, hh_, :], in_=smt[:, SM_TRI:SM_TRI + 128]))(hh_), reads=[const_b], writes=[const_b])
        P.op("dve", lambda e: e.tensor_copy(out=rmb[:], in_=smt[:, SM_RM:SM_RM + 128]), reads=[const_b], writes=[const_b])
        for l in range(DEPTH):
            for g in range(4):
                o0 = SM_WSG + l * 512 + g * 128
                P.op("dve", (lambda l, g, o0: lambda e: e.tensor_tensor(out=wsg[:, l, g, :], in0=smt[:, o0:o0 + 128], in1=trif[:], op=ALU.mult))(l, g, o0),
                     reads=[const_b], writes=[const_b])
        tmpf = arena[:, 0:32, :].bitcast(F32)
        tmpf = tmpf.rearrange("p a c -> p (a c)")
        setup_b = pg[0:44]
        for l in range(DEPTH):
            bsrc = rw_d[:, l * RW_L + RW_BS:l * RW_L + RW_BS + 512]
            P.op("sp", (lambda bsrc: lambda e: e.dma_start(out=tmpf[0:1, 0:512], in_=bsrc))(bsrc), writes=setup_b, dma_key="c_b")
            bh = arena[0:1, 40, :]
            bl = arena[0:1, 41, :]
            P.op("dve", lambda e: e.tensor_copy(out=bh, in_=tmpf[0:1, 0:512]), reads=setup_b, writes=setup_b)
            P.op("dve", lambda e: e.tensor_copy(out=tmpf[0:1, 512:1024], in_=bh), reads=setup_b, writes=setup_b)
            P.op("dve", lambda e: e.tensor_tensor(out=tmpf[0:1, 1024:1536], in0=tmpf[0:1, 0:512], in1=tmpf[0:1, 512:1024], op=ALU.subtract), reads=setup_b, writes=setup_b)
            P.op("dve", lambda e: e.tensor_copy(out=bl, in_=tmpf[0:1, 1024:1536]), reads=setup_b, writes=setup_b)
            for sub in range(4):
                P.op("sp", (lambda l, sub, bh: lambda e: e.dma_start(out=brow[0:1, l, :, sub, :], in_=bh.rearrange("o (g i) -> o g i", g=4)))(l, sub, bh),
                     reads=setup_b, writes=[const_b], dma_key="c_b")
                P.op("sp", (lambda l, sub, bl: lambda e: e.dma_start(out=brow[32:33, l, :, sub, :], in_=bl.rearrange("o (g i) -> o g i", g=4)))(l, sub, bl),
                     reads=setup_b, writes=[const_b], dma_key="c_b")
            lsrc = rw_d[:, l * RW_L + RW_LAM:l * RW_L + RW_LAM + 256].partition_broadcast(128)
            P.op("sp", (lambda lsrc: lambda e: e.dma_start(out=tmpf[:, 2048:2304], in_=lsrc))(lsrc), writes=setup_b, dma_key="c_b")
            P.op("dve", lambda e: e.tensor_tensor(out=tmpf[:, 2304:2368], in0=tmpf[:, 2048:2112], in1=tmpf[:, 2112:2176], op=ALU.mult), reads=setup_b, writes=setup_b)
            P.op("dve", lambda e: e.tensor_tensor(out=tmpf[:, 2368:2432], in0=tmpf[:, 2176:2240], in1=tmpf[:, 2240:2304], op=ALU.mult), reads=setup_b, writes=setup_b)
            P.op("dve", lambda e: e.reduce_sum(out=tmpf[:, 2432:2434], in_=tmpf[:, 2304:2432].rearrange("p (a c) -> p a c", a=2), axis=AX.X), reads=setup_b, writes=setup_b)
            P.op("act", lambda e: e.activation(out=tmpf[:, 2434:2436], in_=tmpf[:, 2432:2434], func=AF.Exp), reads=setup_b, writes=setup_b)
            li = float(lambda_inits[l])
            P.op("dve", (lambda l, li: lambda e: e.scalar_tensor_tensor(out=cst[:, 1 + l:2 + l], in0=tmpf[:, 2435:2436], scalar=-li, in1=tmpf[:, 2434:2435], op0=ALU.add, op1=ALU.subtract))(l, li),
                 reads=setup_b, writes=[const_b])
            P.op("dve", (lambda l, li: lambda e: e.tensor_scalar(out=cst[:, 4 + l:5 + l], in0=smt[:, l * SM_L + SM_SUBG:l * SM_L + SM_SUBG + 1], scalar1=1.0 - li, scalar2=None, op0=ALU.mult))(l, li),
                 reads=[const_b], writes=[const_b])
        RW = min(S, 2048)
        posi = arena[:, 32:32 + RW // 256, :].bitcast(I32).rearrange("p a c -> p (a c)")
        for r0 in range(0, S, RW):
            ang = tmpf[:, 0:RW]
            kk = tmpf[:, RW:2 * RW]
            yy = tmpf[:, 2 * RW:3 * RW]
            zz = tmpf[:, 3 * RW:4 * RW]
            P.op("sp", (lambda r0: lambda e: e.dma_start(out=posi, in_=pos_d[:, r0:r0 + RW].partition_broadcast(128)))(r0), writes=setup_b, dma_key="c_b")
            P.op("dve", lambda e: e.tensor_copy(out=ang, in_=posi), reads=setup_b, writes=setup_b)
            P.op("dve", lambda e: e.tensor_scalar(out=ang, in0=ang, scalar1=smt[:, SM_FS:SM_FS + 1], scalar2=None, op0=ALU.mult), reads=setup_b + [const_b], writes=setup_b)
            P.op("dve", lambda e: e.tensor_scalar(out=kk, in0=ang, scalar1=1.0 / TWO_PI, scalar2=MAGIC, op0=ALU.mult, op1=ALU.add), reads=setup_b, writes=setup_b)
            P.op("dve", lambda e: e.tensor_scalar(out=kk, in0=kk, scalar1=-MAGIC, scalar2=None, op0=ALU.add), reads=setup_b, writes=setup_b)
            P.op("dve", lambda e: e.scalar_tensor_tensor(out=yy, in0=kk, scalar=-CW1, in1=ang, op0=ALU.mult, op1=ALU.add), reads=setup_b, writes=setup_b)
            P.op("dve", lambda e: e.scalar_tensor_tensor(out=yy, in0=kk, scalar=-CW2, in1=yy, op0=ALU.mult, op1=ALU.add), reads=setup_b, writes=setup_b)
            P.op("dve", lambda e: e.tensor_scalar(out=zz, in0=yy, scalar1=PI_LO, scalar2=-PI_LO, op0=ALU.min, op1=ALU.max), reads=setup_b, writes=setup_b)
            P.op("act", lambda e: e.activation(out=zz, in_=zz, func=AF.Sin), reads=setup_b, writes=setup_b)
            P.op("sp", (lambda r0: lambda e: e.dma_start(out=cs_d[:, 1, r0:r0 + RW], in_=zz))(r0), reads=setup_b, writes=[csd_b], dma_key="c_cs")
            P.op("dve", lambda e: e.tensor_scalar(out=yy, in0=yy, scalar1=math.pi / 2, scalar2=None, op0=ALU.add), reads=setup_b, writes=setup_b)
            P.op("dve", lambda e: e.tensor_scalar(out=kk, in0=yy, scalar1=math.pi, scalar2=None, op0=ALU.is_gt), reads=setup_b, writes=setup_b)
            P.op("dve", lambda e: e.scalar_tensor_tensor(out=yy, in0=kk, scalar=-TWO_PI, in1=yy, op0=ALU.mult, op1=ALU.add), reads=setup_b, writes=setup_b)
            P.op("dve", lambda e: e.tensor_scalar(out=ang, in0=yy, scalar1=PI_LO, scalar2=-PI_LO, op0=ALU.min, op1=ALU.max), reads=setup_b + [csd_b], writes=setup_b)
            P.op("act", lambda e: e.activation(out=ang, in_=ang, func=AF.Sin), reads=setup_b, writes=setup_b)
            P.op("sp", (lambda r0: lambda e: e.dma_start(out=cs_d[:, 0, r0:r0 + RW], in_=ang))(r0), reads=setup_b, writes=[csd_b], dma_key="c_cs")

        wstate = {"emitted": 0}
        wseq = [(t, l, b) for t in range(NT) for l in range(DEPTH) for b in range(NBLK)]

        def wload_upto(n):
            while wstate["emitted"] <= min(n, len(wseq) - 1):
                k = wstate["emitted"]
                t, l, b = wseq[k]
                s = k % NSLOT
                P.op("sp", (lambda l, b, s: lambda e: e.dma_start(out=wring[:, s, 0:BLK_LEN[b]], in_=wbf_d[l, :, BLK_OFF[b]:BLK_OFF[b + 1]]))(l, b, s),
                     reads=[wbf_all_b], writes=[wslot_b[s]], dma_key="w%d" % s)
                wstate["emitted"] += 1

        def wblock(t, l, b):
            n = (t * DEPTH + l) * NBLK + b
            wload_upto(n + NSLOT - 1)
            s = n % NSLOT
            return wring[:, s, :], wslot_b[s]

        kvstate = {"n": 0}

        def kvload(l, c, kb):
            n = kvstate["n"]
            kvstate["n"] += 1
            s = n % NKV
            P.op("sp", (lambda l, c, kb, s: lambda e: e.dma_start(out=kring[:, s, :], in_=kt_d[l, c, :, kb * TT:(kb + 1) * TT]))(l, c, kb, s),
                 reads=[ktd_b[l][kb], vvd_b[l][kb]], writes=[kvslot_b[s]], dma_key="kk%d" % s)
            P.op("sp", (lambda l, c, kb, s: lambda e: e.dma_start(
                out=vring[:, s, :].rearrange("p (a c) -> p a c", a=4),
                in_=vv_d[l, c, kb * TT:(kb + 1) * TT, :].rearrange("(a p) c -> p a c", p=128)))(l, c, kb, s),
                reads=[ktd_b[l][kb], vvd_b[l][kb]], writes=[kvslot_b[s]], dma_key="kv%d" % s)
            return s

        def mm(out, lhsT, rhs, start, stop, reads, bank):
            P.op("pe", lambda e: e.matmul(out, lhsT=lhsT, rhs=rhs, start=start, stop=stop), reads=reads, writes=[bank_b[bank]])

        def layer_norm(src_ap, src_b, l, g_col, b_col, last):
            bs, bq = 6, 7
            for kc in range(KC):
                r = kc % 2
                P.op("act", (lambda kc, r: lambda e: e.activation(out=hb_ap[:, r, :], in_=src_ap[:, kc, :], func=AF.Copy))(kc, r),
                     reads=[src_b[kc]], writes=hb_b[r])
                P.op("act", (lambda kc, r: lambda e: e.activation(out=sq_ap[:, r, :], in_=src_ap[:, kc, :], func=AF.Square))(kc, r),
                     reads=[src_b[kc]], writes=sq_b[r])
                mm(banks[bs][:], onesb[:], hb_ap[:, r, :], kc == 0, kc == KC - 1, hb_b[r] + [const_b], bs)
                mm(banks[bq][:], onesb[:], sq_ap[:, r, :], kc == 0, kc == KC - 1, sq_b[r] + [const_b], bq)
            mean = lnf_ap[:, 0, :]
            rstd = lnf_ap[:, 1, :]
            msq = lnf_ap[:, 2, :]
            P.op("dve", lambda e: e.tensor_scalar(out=mean, in0=banks[bs][:], scalar1=1.0 / D, scalar2=None, op0=ALU.mult), reads=[bank_b[bs]], writes=lnf_b[0])
            P.op("dve", lambda e: e.tensor_tensor(out=msq, in0=mean, in1=mean, op=ALU.mult), reads=lnf_b[0], writes=lnf_b[2])
            P.op("dve", lambda e: e.scalar_tensor_tensor(out=msq, in0=banks[bq][:], scalar=1.0 / D, in1=msq, op0=ALU.mult, op1=ALU.subtract), reads=[bank_b[bq]] + lnf_b[2], writes=lnf_b[2])
            P.op("act", lambda e: e.activation(out=rstd, in_=msq, func=AF.Sqrt, bias=cst[:, 0:1], scale=1.0), reads=lnf_b[2] + [const_b], writes=lnf_b[1])
            P.op("dve", lambda e: e.reciprocal(out=rstd, in_=rstd), reads=lnf_b[1], writes=lnf_b[1])
            for kc in range(KC):
                r = kc % 2
                gcol = smt[:, l * SM_L + g_col + kc:l * SM_L + g_col + kc + 1]
                bcol = smt[:, l * SM_L + b_col + kc:l * SM_L + b_col + kc + 1]
                tmp = sl_ap[:, r, :]
                P.op("dve", (lambda kc, tmp: lambda e: e.tensor_tensor(out=tmp, in0=src_ap[:, kc, :], in1=mean, op=ALU.subtract))(kc, tmp),
                     reads=[src_b[kc]] + lnf_b[0], writes=sl_b[r])
                P.op("dve", (lambda tmp: lambda e: e.tensor_tensor(out=tmp, in0=tmp, in1=rstd, op=ALU.mult))(tmp),
                     reads=sl_b[r] + lnf_b[1], writes=sl_b[r])
                if not last:
                    P.op("act", (lambda kc, tmp, gcol, bcol: lambda e: e.activation(out=src_ap[:, kc, :], in_=tmp, func=AF.Identity, scale=gcol, bias=bcol))(kc, tmp, gcol, bcol),
                         reads=sl_b[r] + [const_b], writes=[src_b[kc]])
                    P.op("act", (lambda kc, tmp, gcol, bcol: lambda e: e.activation(out=xb[:, kc, :], in_=tmp, func=AF.Identity, scale=gcol, bias=bcol))(kc, tmp, gcol, bcol),
                         reads=sl_b[r] + [const_b], writes=[xb_b])
                else:
                    P.op("act", (lambda kc, tmp, gcol, bcol, r: lambda e: e.activation(out=ostg[:, r, :], in_=tmp, func=AF.Identity, scale=gcol, bias=bcol))(kc, tmp, gcol, bcol, r),
                         reads=sl_b[r] + [const_b], writes=[ostg_b[r]])
                    yield kc, r

        dbg_d = nc.dram_tensor("dbg", [128, 36, 512], BF16, kind="ExternalOutput").ap() if dump else None
        dbgu_d = nc.dram_tensor("dbgu", [128, 4, 512], F32, kind="ExternalOutput").ap() if dump else None
        dbgp_d = nc.dram_tensor("dbgp", [128, 4, 512], BF16, kind="ExternalOutput").ap() if dump else None
        dbga_d = nc.dram_tensor("dbga", [128, 5, 512], F32, kind="ExternalOutput").ap() if dump else None

        def stage_done(k):
            if stop is not None and k >= stop:
                raise _Stop()

        def tile_layer(t, l):
            stage_done(0)
            t0 = t * TT
            sml = l * SM_L
            P.tag = "load"
            if l == 0:
                P.op("sp", lambda e: e.dma_start(out=A[:], in_=xT_v[:, :, t0:t0 + TT]), writes=A_b, dma_key="xin")
                P.op("pool", lambda e: e.tensor_copy(out=xb[:], in_=A[:]), reads=A_b, writes=[xb_b])
                P.op("sp", lambda e: e.dma_start(out=cst_t[:], in_=cs_d[:, :, t0:t0 + TT]), reads=[csd_b], writes=[cs_b], dma_key="csin")
            Ct = cst_t[:, 0, :]
            St = cst_t[:, 1, :]

            P.tag = "w_v"
            wv, wb_ = wblock(t, l, 0)
            wv3 = wv[:, 0:4096].rearrange("p (kc c) -> p kc c", kc=KC)
            for sub in range(4):
                bk = gp_bank()
                for kc in range(KC):
                    mm(banks[bk][:], xb[:, kc, sub * 128:(sub + 1) * 128], wv3[:, kc, :], kc == 0, kc == KC - 1, [xb_b, wb_], bk)
                r = sub % 2
                P.op("act", (lambda bk, r: lambda e: e.activation(out=vt[:, r, :], in_=banks[bk][:], func=AF.Gelu))(bk, r),
                     reads=[bank_b[bk]], writes=[vt_b[r]])
                P.op("dve", (lambda r: lambda e: e.bn_stats(out=vst6[:, r, :], in_=vt[:, r, :]))(r), reads=[vt_b[r]], writes=[vs_b[r]])
                P.op("dve", (lambda r: lambda e: e.bn_aggr(out=vmv[:, r, 0:2], in_=vst6[:, r, :]))(r), reads=[vs_b[r]], writes=[vs_b[r]])
                P.op("act", (lambda r: lambda e: e.activation(out=vmv[:, r, 2:3], in_=vmv[:, r, 1:2], func=AF.Sqrt, bias=cst[:, 0:1], scale=1.0))(r),
                     reads=[vs_b[r], const_b], writes=[vs_b[r]])
                P.op("dve", (lambda r: lambda e: e.reciprocal(out=vmv[:, r, 3:4], in_=vmv[:, r, 2:3]))(r), reads=[vs_b[r]], writes=[vs_b[r]])
                P.op("dve", (lambda r: lambda e: e.tensor_scalar(out=vt[:, r, :], in0=vt[:, r, :], scalar1=vmv[:, r, 0:1], scalar2=vmv[:, r, 3:4], op0=ALU.subtract, op1=ALU.mult))(r),
                     reads=[vt_b[r], vs_b[r]], writes=[vt_b[r]])
                P.op("pool", (lambda r: lambda e: e.tensor_tensor(out=vt[:, r, :], in0=vt[:, r, :], in1=gbc[:, l, 0, :], op=ALU.mult))(r),
                     reads=[vt_b[r], const_b], writes=[vt_b[r]])
                P.op("pool", (lambda r, sub: lambda e: e.tensor_tensor(out=vb_ap[:, sub, :], in0=vt[:, r, :], in1=gbc[:, l, 1, :], op=ALU.add))(r, sub),
                     reads=[vt_b[r], const_b], writes=vb_b[sub])
            P.tag = "w_u"
            wv, wb_ = wblock(t, l, 1)
            wv3 = wv[:, 0:4096].rearrange("p (kc c) -> p kc c", kc=KC)
            for j in range(4):
                bk = gp_bank()
                for kc in range(KC):
                    mm(banks[bk][:], wv3[:, kc, j * 128:(j + 1) * 128], xb[:, kc, :], kc == 0, kc == KC - 1, [xb_b, wb_], bk)
                P.op("act", (lambda bk, j: lambda e: e.activation(out=u_ap[:, j, :], in_=banks[bk][:], func=AF.Gelu))(bk, j),
                     reads=[bank_b[bk]], writes=u_b[j])
            P.tag = "w_qk"
            for bi, (dst_ap, dst_b) in ((2, (q_ap, q_b)), (3, (kst_ap, kst_b))):
                wv, wb_ = wblock(t, l, bi)
                wv3 = wv[:, 0:4096].rearrange("p (kc c) -> p kc c", kc=KC)
                for j in range(4):
                    bk = gp_bank()
                    for kc in range(KC):
                        mm(banks[bk][:], wv3[:, kc, j * 128:(j + 1) * 128], xb[:, kc, :], kc == 0, kc == KC - 1, [xb_b, wb_], bk)
                    r = j % 2
                    P.op("act", (lambda bk, r: lambda e: e.activation(out=raw[:, r, :], in_=banks[bk][:], func=AF.Copy))(bk, r),
                         reads=[bank_b[bk]], writes=[raw_b[r], bank_b[bk]])
                    P.op("dve", (lambda bk, r: lambda e: e.tensor_tensor(out=rt1[:, r, :], in0=banks[bk][:], in1=Ct, op=ALU.mult))(bk, r),
                         reads=[bank_b[bk], cs_b], writes=[rt1_b[r]])
                    bk2 = gp_bank()
                    mm(banks[bk2][:], rmb[:], raw[:, r, :], True, True, [raw_b[r], const_b], bk2)
                    P.op("dve", (lambda bk2, r: lambda e: e.tensor_tensor(out=rt2[:, r, :], in0=banks[bk2][:], in1=St, op=ALU.mult))(bk2, r),
                         reads=[bank_b[bk2], cs_b], writes=[rt2_b[r]])
                    P.op("pool", (lambda r, j, dst_ap: lambda e: e.tensor_tensor(out=dst_ap[:, j, :], in0=rt1[:, r, :], in1=rt2[:, r, :], op=ALU.add))(r, j, dst_ap),
                         reads=[rt1_b[r], rt2_b[r]], writes=dst_b[j])
            for c in range(4):
                P.op("sp", (lambda c: lambda e: e.dma_start(out=kt_d[l, c, :, t0:t0 + TT], in_=kst_ap[:, c, :]))(c),
                     reads=kst_b[c], writes=[ktd_b[l][t]], dma_key="kw%d" % c)
            P.tag = "w_V"
            wv, wb_ = wblock(t, l, 4)
            wv3 = wv[:, 0:4096].rearrange("p (kc c) -> p kc c", kc=KC)
            for sub in range(4):
                bk = gp_bank()
                for kc in range(KC):
                    mm(banks[bk][:], xb[:, kc, sub * 128:(sub + 1) * 128], wv3[:, kc, :], kc == 0, kc == KC - 1, [xb_b, wb_], bk)
                P.op("act", (lambda bk, sub: lambda e: e.activation(out=vst_ap[:, sub, :], in_=banks[bk][:], func=AF.Copy))(bk, sub),
                     reads=[bank_b[bk]], writes=vst_b[sub])
            for c in range(4):
                P.op("sp", (lambda c: lambda e: e.dma_start(
                    out=vv_d[l, c, t0:t0 + TT, :].rearrange("(a p) c -> p a c", p=128),
                    in_=vst_ap[:, :, c * 128:(c + 1) * 128]))(c),
                    reads=[b for bb in vst_b for b in bb], writes=[vvd_b[l][t]], dma_key="vw%d" % c)
            P.tag = "w_conv"
            for ci in range(12):
                j, kind = ci // 3, ci % 3
                bi, cc = 5 + ci // 4, ci % 4
                if cc == 0:
                    wv, wb_ = wblock(t, l, bi)
                    wv3 = wv[:, 0:4096].rearrange("p (kc c) -> p kc c", kc=KC)
                bk = gp_bank()
                for kc in range(KC):
                    mm(banks[bk][:], wv3[:, kc, cc * 128:(cc + 1) * 128], xb[:, kc, :], kc == 0, kc == KC - 1, [xb_b, wb_], bk)
                if kind == 0:
                    P.op("act", (lambda bk: lambda e: e.activation(out=cgc[:], in_=banks[bk][:], func=AF.Copy))(bk), reads=[bank_b[bk]], writes=[cgc_b])
                elif kind == 1:
                    P.op("pool", (lambda j: lambda e: e.tensor_copy(out=hcv[:, 0:2], in_=carry[:, l, j, :]))(j), reads=[carry_b], writes=[hcv_b])
                    P.op("dve", (lambda bk: lambda e: e.tensor_tensor(out=hcv[:, 2:514], in0=cgc[:], in1=banks[bk][:], op=ALU.mult))(bk),
                         reads=[cgc_b, bank_b[bk]], writes=[hcv_b])
                    P.op("pool", (lambda j: lambda e: e.tensor_copy(out=carry[:, l, j, :], in_=hcv[:, 512:514]))(j), reads=[hcv_b], writes=[carry_b])
                    w0 = smt[:, sml + SM_CONV + j * 3 + 0:sml + SM_CONV + j * 3 + 1]
                    w1 = smt[:, sml + SM_CONV + j * 3 + 1:sml + SM_CONV + j * 3 + 2]
                    w2 = smt[:, sml + SM_CONV + j * 3 + 2:sml + SM_CONV + j * 3 + 3]
                    P.op("dve", (lambda w0: lambda e: e.tensor_scalar(out=cacc[:], in0=hcv[:, 0:512], scalar1=w0, scalar2=None, op0=ALU.mult))(w0),
                         reads=[hcv_b, const_b], writes=[cacc_b])
                    P.op("dve", (lambda w1: lambda e: e.scalar_tensor_tensor(out=cacc[:], in0=hcv[:, 1:513], scalar=w1, in1=cacc[:], op0=ALU.mult, op1=ALU.add))(w1),
                         reads=[hcv_b, const_b, cacc_b], writes=[cacc_b])
                    P.op("dve", (lambda w2: lambda e: e.scalar_tensor_tensor(out=cacc[:], in0=hcv[:, 2:514], scalar=w2, in1=cacc[:], op0=ALU.mult, op1=ALU.add))(w2),
                         reads=[hcv_b, const_b, cacc_b], writes=[cacc_b])
                else:
                    P.op("dve", (lambda bk, j: lambda e: e.tensor_tensor(out=yc_ap[:, j, :], in0=cacc[:], in1=banks[bk][:], op=ALU.mult))(bk, j),
                         reads=[cacc_b, bank_b[bk]], writes=yc_b[j])
            P.tag = "sgu"
            for g in range(4):
                bk = gp_bank()
                mm(banks[bk][:], onesb[0:64, :], brow[0:64, l, g, :, :].rearrange("p a c -> p (a c)"), True, False, [const_b], bk)
                for sub in range(4):
                    mm(banks[bk][:, sub * 128:(sub + 1) * 128], vb_ap[:, sub, g * 128:(g + 1) * 128], wsg[:, l, g, :], False, sub == 3,
                       vb_b[sub] + [const_b], bk)
                P.op("dve", (lambda bk, g: lambda e: e.tensor_tensor(out=ya_ap[:, g, :], in0=u_ap[:, g, :], in1=banks[bk][:], op=ALU.mult))(bk, g),
                     reads=u_b[g] + [bank_b[bk]], writes=ya_b[g])
            stage_done(1)
            yield
            P.tag = "attn"
            neglam = cst[:, 1 + l:2 + l]
            gsc = cst[:, 4 + l:5 + l]
            kvseq = [(c, kb) for c in range(4) for kb in range(t + 1)]
            kvslots = {}
            kvn = {"i": 0}

            def kv_prefetch(upto):
                while kvn["i"] <= min(upto, len(kvseq) - 1):
                    c_, kb_ = kvseq[kvn["i"]]
                    kvslots[(c_, kb_)] = kvload(l, c_, kb_)
                    kvn["i"] += 1

            r1, o1, r2, o2, rr = (ap_t[:, i, :] for i in range(5))

            def post_head(c):
                P.op("dve", lambda e: e.tensor_copy(out=r1, in_=banks[6][:]), reads=[bank_b[6]], writes=[ap_b[0]])
                P.op("dve", lambda e: e.tensor_copy(out=o1, in_=banks[4][:]), reads=[bank_b[4]], writes=[ap_b[1]])
                P.op("dve", lambda e: e.tensor_copy(out=r2, in_=banks[7][:]), reads=[bank_b[7]], writes=[ap_b[2]])
                P.op("dve", lambda e: e.tensor_copy(out=o2, in_=banks[5][:]), reads=[bank_b[5]], writes=[ap_b[3]])
                P.op("dve", lambda e: e.reciprocal(out=r1, in_=r1), reads=[ap_b[0]], writes=[ap_b[0]])
                P.op("dve", lambda e: e.reciprocal(out=r2, in_=r2), reads=[ap_b[2]], writes=[ap_b[2]])
                P.op("dve", lambda e: e.tensor_tensor(out=o1, in0=o1, in1=r1, op=ALU.mult), reads=[ap_b[1], ap_b[0]], writes=[ap_b[1]])
                P.op("dve", lambda e: e.tensor_tensor(out=o2, in0=o2, in1=r2, op=ALU.mult), reads=[ap_b[3], ap_b[2]], writes=[ap_b[3]])
                P.op("dve", lambda e: e.scalar_tensor_tensor(out=o1, in0=o2, scalar=neglam, in1=o1, op0=ALU.mult, op1=ALU.add), reads=[ap_b[3], ap_b[1], const_b], writes=[ap_b[1]])

            def post_tail(c):
                P.op("act", lambda e: e.activation(out=apq[:], in_=o1, func=AF.Square), reads=[ap_b[1]], writes=[apq_b])
                bk = gp_bank(False)
                mm(banks[bk][:], onesb[:], apq[:], True, True, [apq_b, const_b], bk)
                P.op("act", (lambda bk: lambda e: e.activation(out=rr, in_=banks[bk][:], func=AF.Sqrt, bias=cst[:, 0:1], scale=1.0 / 128))(bk),
                     reads=[bank_b[bk], const_b], writes=[ap_b[4]])
                P.op("dve", lambda e: e.reciprocal(out=rr, in_=rr), reads=[ap_b[4]], writes=[ap_b[4]])
                P.op("dve", (lambda c: lambda e: e.scalar_tensor_tensor(out=yb_ap[:, c, :], in0=o1, scalar=gsc, in1=rr, op0=ALU.mult, op1=ALU.mult))(c),
                     reads=[ap_b[1], ap_b[4], const_b], writes=yb_b[c])

            pending_tail = None
            for c in range(4):
                pairs = [(kb, ks) for kb in range(t + 1) for ks in range(4)]
                npairs = len(pairs)
                defer_at = min(8, npairs - 1)
                pend = None

                def pv_l(pd, last):
                    ps_, pks, ppi, pq0, pii = pd
                    for hh in range(2):
                        mm(banks[4 + hh][:, pq0:TT], vring[:, ps_, pks * 128:(pks + 1) * 128], pt[:, ppi, hh, pq0:TT], pii == 0, last, [kvslot_b[ps_], pt_b[ppi]], 4 + hh)
                    for hh in range(2):
                        mm(banks[6 + hh][:, pq0:TT], onesb[:], pt[:, ppi, hh, pq0:TT], pii == 0, last, [const_b, pt_b[ppi]], 6 + hh)

                for ii, (kb, ks) in enumerate(pairs):
                    if ks == 0:
                        kv_prefetch(c * (t + 1) + kb + 2)
                    s = kvslots[(c, kb)]
                    diag = kb == t
                    q0 = ks * 128 if diag else 0
                    sp = ii % 2
                    pi = ii % NPT
                    for hh in range(2):
                        mm(pp[sp][:, hh * 512 + q0:(hh + 1) * 512], kring[hh * 64:(hh + 1) * 64, s, ks * 128:(ks + 1) * 128], q_ap[hh * 64:(hh + 1) * 64, c, q0:TT],
                           True, True, [kvslot_b[s]] + q_b[c], 2 * sp + hh)
                    P.op("act", (lambda sp, pi, q0: lambda e: e.activation(out=pt[:, pi, :, q0:TT], in_=pp[sp][:].rearrange("p (h c) -> p h c", h=2)[:, :, q0:TT], func=AF.Exp, scale=0.125))(sp, pi, q0),
                         reads=[bank_b[2 * sp], bank_b[2 * sp + 1]], writes=[pt_b[pi]])
                    if diag:
                        P.op("pool", (lambda pi, q0: lambda e: e.tensor_tensor(out=pt[:, pi, :, q0:q0 + 128], in0=pt[:, pi, :, q0:q0 + 128], in1=trib2[:], op=ALU.mult))(pi, q0),
                             reads=[pt_b[pi], const_b], writes=[pt_b[pi]])
                    if pend is not None:
                        pv_l(pend, False)
                    pend = (s, ks, pi, q0, ii)
                    if ii == defer_at and pending_tail is not None:
                        post_tail(pending_tail)
                        pending_tail = None
                pv_l(pend, True)
                post_head(c)
                pending_tail = c
                stage_done(2)
                yield
            post_tail(3)
            P.tag = "gates"
            ys = ((ya_ap, ya_b), (yb_ap, yb_b), (yc_ap, yc_b))
            for oc in range(8):
                wv, wb_ = wblock(t, l, 8 + oc)
                gw = wv[:, 0:3072].rearrange("p (n kc c) -> p n kc c", n=3, kc=KC)
                bw = wv[:, 3072:4608].rearrange("p (n kc c) -> p n kc c", n=3, kc=4)
                for n in range(3):
                    bk = gp_bank()
                    for kc in range(KC):
                        mm(banks[bk][:], gw[:, n, kc, :], xb[:, kc, :], kc == 0, kc == KC - 1, [xb_b, wb_], bk)
                    P.op("act", (lambda bk, n: lambda e: e.activation(out=sg[:, n, :], in_=banks[bk][:], func=AF.Sigmoid))(bk, n),
                         reads=[bank_b[bk]], writes=[sg_b[n]])
                for n in range(3):
                    bk = gp_bank()
                    y_ap, y_b = ys[n]
                    for kc in range(4):
                        mm(banks[bk][:], bw[:, n, kc, :], y_ap[:, kc, :], kc == 0, kc == 3, y_b[kc] + [wb_], bk)
                    P.op("dve", (lambda bk, n: lambda e: e.tensor_tensor(out=mt[:, n, :], in0=sg[:, n, :], in1=banks[bk][:], op=ALU.mult))(bk, n),
                         reads=[sg_b[n], bank_b[bk]], writes=[mt_b[n]])
                P.op("pool", lambda e: e.tensor_tensor(out=mt[:, 0, :], in0=mt[:, 0, :], in1=mt[:, 1, :], op=ALU.add), reads=[mt_b[0], mt_b[1]], writes=[mt_b[0]])
                P.op("pool", (lambda oc: lambda e: e.tensor_tensor(out=mg_ap[:, oc, :], in0=mt[:, 0, :], in1=mt[:, 2, :], op=ALU.add))(oc),
                     reads=[mt_b[0], mt_b[2]], writes=mg_b[oc])
            stage_done(3)
            yield
            P.tag = "w_o_ln1"
            for oc in range(8):
                if oc % 4 == 0:
                    wv, wb_ = wblock(t, l, 16 + oc // 4)
                    wv3 = wv[:, 0:4096].rearrange("p (kc c) -> p kc c", kc=KC)
                bk = gp_bank()
                for kc in range(KC):
                    mm(banks[bk][:], wv3[:, kc, (oc % 4) * 128:(oc % 4 + 1) * 128], mg_ap[:, kc, :], kc == 0, kc == KC - 1, mg_b[kc] + [wb_], bk)
                P.op("dve", (lambda bk, oc: lambda e: e.scalar_tensor_tensor(out=Bx[:, oc, :], in0=A[:, oc, :], scalar=ALPHA, in1=banks[bk][:], op0=ALU.mult, op1=ALU.add))(bk, oc),
                     reads=[A_b[oc], bank_b[bk]], writes=[B_b[oc]])
            for _ in layer_norm(Bx, B_b, l, SM_LN1G, SM_LN1B, False):
                pass
            stage_done(4)
            yield
            P.tag = "ffn_gu"
            for jb in range(11):
                wv, wb_ = wblock(t, l, 18 + jb)
                wv3 = wv[:, 0:4096].rearrange("p (kc c) -> p kc c", kc=KC)
                for jj in range(2):
                    j = 2 * jb + jj
                    bg_ = gp_bank()
                    for kc in range(KC):
                        mm(banks[bg_][:], wv3[:, kc, (2 * jj) * 128:(2 * jj + 1) * 128], xb[:, kc, :], kc == 0, kc == KC - 1, [xb_b, wb_], bg_)
                    bu_ = gp_bank()
                    for kc in range(KC):
                        mm(banks[bu_][:], wv3[:, kc, (2 * jj + 1) * 128:(2 * jj + 2) * 128], xb[:, kc, :], kc == 0, kc == KC - 1, [xb_b, wb_], bu_)
                    r = j % 2
                    P.op("act", (lambda bg_, r: lambda e: e.activation(out=sl_ap[:, r, :], in_=banks[bg_][:], func=AF.Silu))(bg_, r),
                         reads=[bank_b[bg_]], writes=sl_b[r])
                    P.op("dve", (lambda bu_, r, j: lambda e: e.tensor_tensor(out=hm_ap[:, j, :], in0=sl_ap[:, r, :], in1=banks[bu_][:], op=ALU.mult))(bu_, r, j),
                         reads=sl_b[r] + [bank_b[bu_]], writes=hm_b[j])
                if jb % 4 == 3:
                    stage_done(5)
            yield
            P.tag = "down_ln2"
            for oc in range(8):
                wv, wb_ = wblock(t, l, 29 + oc)
                wv3 = wv[:, 0:2816].rearrange("p (kc c) -> p kc c", kc=NFF)
                bk = gp_bank()
                for kc in range(NFF):
                    mm(banks[bk][:], wv3[:, kc, :], hm_ap[:, kc, :], kc == 0, kc == NFF - 1, hm_b[kc] + [wb_], bk)
                P.op("dve", (lambda bk, oc: lambda e: e.scalar_tensor_tensor(out=A[:, oc, :], in0=Bx[:, oc, :], scalar=ALPHA, in1=banks[bk][:], op0=ALU.mult, op1=ALU.add))(bk, oc),
                     reads=[B_b[oc], bank_b[bk]], writes=[A_b[oc]])
            last = l == DEPTH - 1
            for res in layer_norm(A, A_b, l, SM_LN2G, SM_LN2B, last):
                kc, r = res
                o = P.op("sp", (lambda kc, r: lambda e: e.dma_start(out=outT_v[:, kc, t0:t0 + TT], in_=ostg[:, r, :]))(kc, r),
                         reads=[ostg_b[r]], writes=[], dma_key="out%d" % r)
                out_toks.append(o.tok)
            stage_done(6)
            yield

        try:
            for t in range(NT):
                for l in range(DEPTH):
                    for _ in tile_layer(t, l):
                        pass
        except _Stop:
            pass
        if stop is not None:
            o = P.op("sp", lambda e: e.dma_start(out=outT_v[:, :, 0:TT], in_=A[:]), reads=A_b + B_b, writes=[], dma_key="out0")
            out_toks.append(o.tok)
        if dump:
            o = P.op("sp", lambda e: e.dma_start(out=dbg_d, in_=arena[:, 8:44, :]), reads=pg, writes=[], dma_key="out1")
            out_toks.append(o.tok)
            o = P.op("sp", lambda e: e.dma_start(out=dbgu_d, in_=u_ap), reads=pg, writes=[], dma_key="out1")
            out_toks.append(o.tok)
            if stop is not None and stop >= 2:
                o = P.op("sp", lambda e: e.dma_start(out=dbgp_d, in_=pt[:, 0:2, :, :].rearrange("p a h c -> p (a h) c")), reads=pt_b, writes=[], dma_key="out1")
                out_toks.append(o.tok)
                o = P.op("sp", lambda e: e.dma_start(out=dbga_d, in_=ap_t[:]), reads=ap_b, writes=[], dma_key="out1")
                out_toks.append(o.tok)

        fin = {}
        for s_, v_ in out_toks:
            fin[s_] = max(fin.get(s_, 0), v_)
        P.emit(nc, final_wait_tokens=list(fin.items()))
    build_program.last_prog = P
    return nc, len(P.ops)


def _wstream(w_in, w_branch, w_o, w_gate_up, w_down):
    def fm(cols):
        K = cols.shape[0]
        return cols.reshape(K // 128, 128, cols.shape[1]).transpose(1, 0, 2).reshape(128, -1)
    parts = []
    u = w_in[:, 0:512]
    v = w_in[:, 512:1024]
    q = w_in[:, 1024:1536]
    k = w_in[:, 1536:2048]
    vv = w_in[:, 2048:2560]
    bg = w_in[:, 2560:3072]
    cg = w_in[:, 3072:3584]
    xc = w_in[:, 3584:4096]
    zg = w_in[:, 4096:7168]
    conv_cols = []
    for j in range(4):
        for src in (cg, xc, bg):
            conv_cols.append(src[:, j * 128:(j + 1) * 128])
    conv = np.concatenate(conv_cols, axis=1)
    for blk in (v, u, q, k, vv, conv[:, 0:512], conv[:, 512:1024], conv[:, 1024:1536]):
        parts.append(fm(blk))
    for oc in range(8):
        for n in range(3):
            parts.append(fm(zg[:, n * 1024 + oc * 128:n * 1024 + (oc + 1) * 128]))
        for n in range(3):
            parts.append(fm(w_branch[n][:, oc * 128:(oc + 1) * 128]))
    for h in range(2):
        parts.append(fm(w_o[:, h * 512:(h + 1) * 512]))
    for jb in range(11):
        cols = []
        for jj in range(2):
            j = 2 * jb + jj
            cols.append(w_gate_up[:, j * 128:(j + 1) * 128])
            cols.append(w_gate_up[:, DFF + j * 128:DFF + (j + 1) * 128])
        parts.append(fm(np.concatenate(cols, axis=1)))
    for oc in range(8):
        parts.append(fm(w_down[:, oc * 128:(oc + 1) * 128]))
    out = np.concatenate(parts, axis=1)
    assert out.shape == (128, WCOLS), out.shape
    return out


def _const_tables():
    tri = (np.arange(128)[:, None] <= np.arange(128)[None, :]).astype(np.float32)
    rm = np.zeros((128, 128), np.float32)
    fs = np.zeros((128,), np.float32)
    inv_freq = (np.float32(ROPE_THETA) ** (-np.arange(0, 16, 2, dtype=np.float32) / np.float32(16))).astype(np.float32)
    for h in range(2):
        for d in range(16):
            src = d + 8 if d < 8 else d - 8
            rm[h * 64 + src, h * 64 + d] = 1.0
            fs[h * 64 + d] = -inv_freq[d] if d < 8 else inv_freq[d - 8]
    return tri, rm, fs


def _host_layout(inp, depth=DEPTH):
    f32 = np.float32
    tri, rm, fs = _const_tables()
    sm = np.zeros((128, NSM), f32)
    rw = np.zeros((1, NRW), f32)
    wst = np.zeros((depth, 128, WCOLS), f32)
    for l in range(depth):
        o = l * SM_L
        sm[:, o + SM_LN1G:o + SM_LN1G + 8] = inp["ln1_g"][l].reshape(8, 128).T
        sm[:, o + SM_LN1B:o + SM_LN1B + 8] = inp["ln1_b"][l].reshape(8, 128).T
        sm[:, o + SM_LN2G:o + SM_LN2G + 8] = inp["ln2_g"][l].reshape(8, 128).T
        sm[:, o + SM_LN2B:o + SM_LN2B + 8] = inp["ln2_b"][l].reshape(8, 128).T
        sm[:, o + SM_SUBG] = inp["subln_g"][l]
        sm[:, o + SM_CONV:o + SM_CONV + 12] = inp["conv_w"][l].reshape(4, 128, 3).transpose(1, 0, 2).reshape(128, 12)
        sm[:, SM_WSG + l * 512:SM_WSG + (l + 1) * 512] = inp["w_sgu"][l].transpose(2, 0, 1).reshape(128, 512)
        r = l * RW_L
        rw[0, r + RW_G:r + RW_G + 512] = inp["sgu_ln_g"][l]
        rw[0, r + RW_B:r + RW_B + 512] = inp["sgu_ln_b"][l]
        rw[0, r + RW_BS:r + RW_BS + 512] = inp["b_sgu"][l].reshape(512)
        rw[0, r + RW_LAM:r + RW_LAM + 256] = np.concatenate(
            [inp["lambda_q1"][l], inp["lambda_k1"][l], inp["lambda_q2"][l], inp["lambda_k2"][l]])
        wst[l] = _wstream(inp["w_in"][l], inp["w_branch"][l], inp["w_o"][l], inp["w_gate_up"][l], inp["w_down"][l])
    sm[:, SM_FS] = fs
    sm[:, SM_TRI:SM_TRI + 128] = tri
    sm[:, SM_RM:SM_RM + 128] = rm
    return sm, rw, wst


_CACHE = {}


def run_cores(inp, n_cores, S):
    inp = {k: np.asarray(v) for k, v in inp.items()}
    sm, rw, wst = _host_layout(inp)
    lambda_inits = [0.8 - 0.6 * math.exp(-0.3 * l) for l in range(DEPTH)]
    key = S
    if key not in _CACHE:
        _CACHE[key] = build_program(S, lambda_inits)[0]
    nc = _CACHE[key]
    in_maps = []
    for c in range(n_cores):
        in_maps.append({
            "xT": np.ascontiguousarray(inp["x"][c].T.astype(np.float32)),
            "pos": np.ascontiguousarray(inp["positions"][c].reshape(1, S).astype(np.int32)),
            "wst": wst, "sm": sm, "rw": rw,
        })
    res = run_bass_kernel_spmd(nc, in_maps, core_ids=list(range(n_cores)))
    out = np.stack([np.ascontiguousarray(r["outT"].T) for r in res.results], axis=0)
    return out.astype(np.float32)


def kernel(**inputs):
    x = np.asarray(inputs["x"])
    B, S, _ = x.shape
    assert B == N_CORES
    return run_cores(inputs, N_CORES, S)
```

```python
import contextlib
import math
import numpy as np
import concourse.bass as bass
import concourse.mybir as mybir
from concourse.bass_utils import run_bass_kernel_spmd

F32 = mybir.dt.float32
BF16 = mybir.dt.bfloat16
I32 = mybir.dt.int32
AF = mybir.ActivationFunctionType
ALU = mybir.AluOpType
AX = mybir.AxisListType

ENGS = ("pe", "act", "dve", "pool", "sp")

D = 1024
KC = 8
DEPTH = 2
TT = 512
DFF = 2816
NFF = 22
ALPHA = (2 * DEPTH) ** 0.25
EPS = 1e-5
ROPE_THETA = 500000.0
N_CORES = 8

BLK_LEN = [4096] * 8 + [4608] * 8 + [4096] * 2 + [4096] * 11 + [2816] * 8
NBLK = len(BLK_LEN)
BLK_OFF = [0]
for _x in BLK_LEN:
    BLK_OFF.append(BLK_OFF[-1] + _x)
WCOLS = BLK_OFF[-1]
SLOT = 4608
NSLOT = 3
NKV = 4

SM_L = 48
SM_LN1G, SM_LN1B, SM_LN2G, SM_LN2B, SM_SUBG, SM_CONV = 0, 8, 16, 24, 32, 33
SM_FS = 96
SM_TRI = 98
SM_RM = SM_TRI + 128
SM_WSG = SM_RM + 128
NSM = SM_WSG + DEPTH * 512
RW_L = 1792
RW_G, RW_B, RW_BS, RW_LAM = 0, 512, 1024, 1536
NRW = DEPTH * RW_L

MAGIC = 12582912.0
TWO_PI = 2.0 * math.pi
CW1 = 6.28125
CW2 = TWO_PI - CW1
PI_LO = 3.1415925


class Buf:
    __slots__ = ("name", "w", "r")

    def __init__(self, name=""):
        self.name = name
        self.w = {}
        self.r = {}


class Op:
    __slots__ = ("eng", "fn", "deps", "signals", "dma_key", "tok", "idx", "tag")


class Prog:
    def __init__(self):
        self.ops = []
        self.dma_count = {}
        self.tag = ""

    def op(self, eng, fn, reads=(), writes=(), dma_key=None):
        o = Op()
        o.eng = eng
        o.fn = fn
        o.dma_key = dma_key
        o.signals = False
        o.tag = self.tag
        o.idx = len(self.ops)
        deps = set()
        for b in reads:
            deps.update(b.w.values())
        for b in writes:
            deps.update(b.w.values())
            deps.update(b.r.values())
        o.deps = deps
        key = ("dma", dma_key) if dma_key is not None else eng
        for b in reads:
            b.r[key] = o.idx
        for b in writes:
            b.w[key] = o.idx
        if dma_key is not None:
            n = self.dma_count.get(dma_key, 0) + 1
            self.dma_count[dma_key] = n
            o.tok = (("dma", dma_key), 16 * n)
        else:
            o.tok = None
        self.ops.append(o)
        return o

    @staticmethod
    def _skip(p, o):
        return p.dma_key is None and o.dma_key is None and p.eng == "pe" and o.eng == "pe"

    def resolve(self):
        ops = self.ops
        for o in ops:
            for d in o.deps:
                p = ops[d]
                if p.dma_key is None and not self._skip(p, o):
                    p.signals = True
        cnt = {e: 0 for e in ENGS}
        for o in ops:
            if o.dma_key is None and o.signals:
                cnt[o.eng] += 1
                o.tok = (o.eng, cnt[o.eng])
        waited = {e: {} for e in ENGS}
        self.waits = []
        for o in ops:
            need = {}
            for d in o.deps:
                p = ops[d]
                if self._skip(p, o) or p.tok is None:
                    continue
                s, v = p.tok
                if v > need.get(s, 0):
                    need[s] = v
            ws = []
            wd = waited[o.eng]
            for s, v in need.items():
                if v > wd.get(s, 0):
                    wd[s] = v
                    ws.append((s, v))
            self.waits.append(ws)
        return cnt

    def emit(self, nc, final_wait_tokens=()):
        self.resolve()
        sem_keys = list(ENGS) + [("dma", k) for k in self.dma_count]
        with contextlib.ExitStack() as st:
            sems = {}
            for k in sem_keys:
                nm = "s_" + (k if isinstance(k, str) else "d_" + str(k[1]))
                sems[k] = st.enter_context(nc.semaphore(nm))
            block = st.enter_context(nc.Block())
            per = {e: [] for e in ENGS}
            for o in self.ops:
                per[o.eng].append(o)
            waits = self.waits

            def run(engine_name, eng):
                for o in per[engine_name]:
                    for s, v in waits[o.idx]:
                        eng.wait_ge(sems[s], v)
                    ins = o.fn(eng)
                    if o.dma_key is not None:
                        ins.then_inc(sems[("dma", o.dma_key)], 16)
                    elif o.signals:
                        ins.then_inc(sems[o.eng], 1)
                if engine_name == "sp":
                    for (s, v) in final_wait_tokens:
                        eng.wait_ge(sems[s], v)

            @block.tensor
            def _(e):
                run("pe", e)

            @block.scalar
            def _(e):
                run("act", e)

            @block.vector
            def _(e):
                run("dve", e)

            @block.gpsimd
            def _(e):
                run("pool", e)

            @block.sync
            def _(e):
                run("sp", e)


class _Stop(Exception):
    pass


def build_program(S, lambda_inits, stop=None, dump=False):
    NT = S // TT
    nc = bass.Bass("TRN2", target_bir_lowering=False)
    P = Prog()

    xT_d = nc.dram_tensor("xT", [D, S], F32, kind="ExternalInput").ap()
    pos_d = nc.dram_tensor("pos", [1, S], I32, kind="ExternalInput").ap()
    wst_d = nc.dram_tensor("wst", [DEPTH, 128, WCOLS], F32, kind="ExternalInput").ap()
    sm_d = nc.dram_tensor("sm", [128, NSM], F32, kind="ExternalInput").ap()
    rw_d = nc.dram_tensor("rw", [1, NRW], F32, kind="ExternalInput").ap()
    outT_d = nc.dram_tensor("outT", [D, S], F32, kind="ExternalOutput").ap()
    wbf_d = nc.dram_tensor("wbf", [DEPTH, 128, WCOLS], BF16, kind="Internal").ap()
    cs_d = nc.dram_tensor("cs", [128, 2, S], F32, kind="Internal").ap()
    kt_d = nc.dram_tensor("ktc", [DEPTH, 4, 128, S], BF16, kind="Internal").ap()
    vv_d = nc.dram_tensor("vvc", [DEPTH, 4, S, 128], BF16, kind="Internal").ap()

    xT_v = xT_d.rearrange("(kc p) s -> p kc s", p=128)
    outT_v = outT_d.rearrange("(kc p) s -> p kc s", p=128)

    st = contextlib.ExitStack()
    with st:
        def sb(name, shape, dt):
            return st.enter_context(nc.sbuf_tensor(name, shape, dt))

        wring = sb("wring", [128, NSLOT, SLOT], BF16)
        kring = sb("kring", [128, NKV, 512], BF16)
        vring = sb("vring", [128, NKV, 512], BF16)
        A = sb("A", [128, KC, TT], F32)
        Bx = sb("Bx", [128, KC, TT], F32)
        xb = sb("xb", [128, KC, TT], BF16)
        smt = sb("smt", [128, NSM], F32)
        gbc = sb("gbc", [128, DEPTH, 2, 512], F32)
        wsg = sb("wsg", [128, DEPTH, 4, 128], BF16)
        trib = sb("trib", [128, 128], BF16)
        trif = sb("trif", [128, 128], F32)
        rmb = sb("rmb", [128, 128], BF16)
        onesb = sb("onesb", [128, 128], BF16)
        brow = sb("brow", [64, DEPTH, 4, 4, 128], BF16)
        cst = sb("cst", [128, 16], F32)
        carry = sb("carry", [128, DEPTH, 4, 2], F32)
        NPG = 44
        arena = sb("arena", [128, NPG, 512], BF16)
        pg = [Buf("pg%d" % i) for i in range(NPG)]

        def carve(lo, n, dt=BF16):
            ap = arena[:, lo:lo + n, :]
            if dt == F32:
                ap = arena[:, lo:lo + n, :].bitcast(F32)
            return ap, pg[lo:lo + n]

        u_ap = arena[:, 0:8, :].bitcast(F32).rearrange("p (j h) c -> p j (h c)", h=2)
        u_b = [pg[2 * j:2 * j + 2] for j in range(4)]
        vb_ap = arena[:, 8:12, :]
        vb_b = [pg[8 + i:9 + i] for i in range(4)]
        q_ap = arena[:, 12:16, :]
        q_b = [pg[12 + i:13 + i] for i in range(4)]
        kst_ap = arena[:, 16:20, :]
        kst_b = [pg[16 + i:17 + i] for i in range(4)]
        vst_ap = arena[:, 20:24, :]
        vst_b = [pg[20 + i:21 + i] for i in range(4)]
        ya_ap = arena[:, 24:28, :]
        ya_b = [pg[24 + i:25 + i] for i in range(4)]
        yb_ap = arena[:, 28:32, :]
        yb_b = [pg[28 + i:29 + i] for i in range(4)]
        yc_ap = arena[:, 32:36, :]
        yc_b = [pg[32 + i:33 + i] for i in range(4)]
        mg_ap = arena[:, 36:44, :]
        mg_b = [pg[36 + i:37 + i] for i in range(8)]
        hm_ap = arena[:, 0:22, :]
        hm_b = [pg[i:i + 1] for i in range(22)]
        sl_ap = arena[:, 22:26, :].bitcast(F32).rearrange("p (j h) c -> p j (h c)", h=2)
        sl_b = [pg[22:24], pg[24:26]]
        hb_ap = arena[:, 26:28, :]
        hb_b = [pg[26:27], pg[27:28]]
        sq_ap = arena[:, 28:30, :]
        sq_b = [pg[28:29], pg[29:30]]
        lnf_ap = arena[:, 30:36, :].bitcast(F32).rearrange("p (j h) c -> p j (h c)", h=2)
        lnf_b = [pg[30:32], pg[32:34], pg[34:36]]

        vt = sb("vt", [128, 2, 512], F32)
        vt_b = [Buf("vt0"), Buf("vt1")]
        vst6 = sb("vst6", [128, 2, 6], F32)
        vmv = sb("vmv", [128, 2, 4], F32)
        vs_b = [Buf("vs0"), Buf("vs1")]
        cst_t = sb("cs_t", [128, 2, 512], F32)
        cs_b = Buf("cs")
        raw = sb("raw", [128, 2, 512], BF16)
        raw_b = [Buf("raw0"), Buf("raw1")]
        rt1 = sb("rt1", [128, 2, 512], F32)
        rt1_b = [Buf("rt1_0"), Buf("rt1_1")]
        rt2 = sb("rt2", [128, 2, 512], F32)
        rt2_b = [Buf("rt2_0"), Buf("rt2_1")]
        cgc = sb("cgc", [128, 512], F32)
        cgc_b = Buf("cgc")
        hcv = sb("hcv", [128, 516], F32)
        hcv_b = Buf("hcv")
        cacc = sb("cacc", [128, 512], F32)
        cacc_b = Buf("cacc")
        sg = sb("sg", [128, 3, 512], F32)
        sg_b = [Buf("sg0"), Buf("sg1"), Buf("sg2")]
        mt = sb("mt", [128, 3, 512], F32)
        mt_b = [Buf("mt0"), Buf("mt1"), Buf("mt2")]
        NPT = 3
        pt = sb("pt", [128, NPT, 2, 512], BF16)
        pt_b = [Buf("pt%d" % i) for i in range(NPT)]
        trib2 = sb("trib2", [128, 2, 128], BF16)
        ap_t = sb("ap_t", [128, 5, 512], F32)
        ap_b = [Buf("apt%d" % i) for i in range(5)]
        apq = sb("apq", [128, 512], BF16)
        apq_b = Buf("apq")
        ostg = sb("ostg", [128, 2, 512], F32)
        ostg_b = [Buf("ostg0"), Buf("ostg1")]

        pp = [st.enter_context(nc.psum_tensor("pp%d" % i, [128, 1024], F32)) for i in range(4)]
        banks = [pp[i // 2][:, (i % 2) * 512:(i % 2 + 1) * 512] for i in range(8)]
        bank_b = [Buf("bank%d" % i) for i in range(8)]

        A_b = [Buf("A%d" % i) for i in range(KC)]
        B_b = [Buf("B%d" % i) for i in range(KC)]
        xb_b = Buf("xb")
        const_b = Buf("const")
        carry_b = Buf("carry")
        wslot_b = [Buf("ws%d" % i) for i in range(NSLOT)]
        kvslot_b = [Buf("kv%d" % i) for i in range(NKV)]
        wbf_b = [[Buf("wbf%d_%d" % (l, b)) for b in range(NBLK)] for l in range(DEPTH)]
        csd_b = Buf("csd")
        ktd_b = [[Buf("ktd%d_%d" % (l, t)) for t in range(NT)] for l in range(DEPTH)]
        vvd_b = [[Buf("vvd%d_%d" % (l, t)) for t in range(NT)] for l in range(DEPTH)]
        out_toks = []

        gp_state = {"i": 0}

        def gp_bank(use_all=True):
            n = 4 if use_all else 2
            i = gp_state["i"] % n
            gp_state["i"] += 1
            return i

        if dump:
            P.op("pool", lambda e: e.memset(arena[:], 0.0), writes=pg)
            P.op("pool", lambda e: e.memset(pt[:], 0.0), writes=pt_b)
            P.op("pool", lambda e: e.memset(ap_t[:], 0.0), writes=ap_b)
        P.op("sp", lambda e: e.dma_start(out=smt[:], in_=sm_d), writes=[const_b], dma_key="c_sm")
        for l in range(DEPTH):
            P.op("sp", (lambda l: lambda e: e.dma_start(
                out=gbc[:, l, :, :], in_=rw_d[:, l * RW_L:l * RW_L + 1024].rearrange("o (a c) -> o a c", a=2).partition_broadcast(128)))(l),
                writes=[const_b], dma_key="c_gb")
        sf_ap = [arena[:, 0:9, :].bitcast(F32).rearrange("p a c -> p (a c)"), arena[:, 9:18, :].bitcast(F32).rearrange("p a c -> p (a c)")]
        sf_b = [pg[0:9], pg[9:18]]
        ob_ap = [arena[:, 18:23, :].rearrange("p a c -> p (a c)"), arena[:, 23:28, :].rearrange("p a c -> p (a c)")]
        ob_b = [pg[18:23], pg[23:28]]
        wbf_all_b = Buf("wbf_all")
        hi_ = 0
        for l in range(DEPTH):
            for b in range(NBLK):
                half = BLK_LEN[b] // 2
                for h in range(2):
                    c0 = BLK_OFF[b] + h * half
                    r = hi_ % 2
                    P.op("sp", (lambda l, c0, half, r: lambda e: e.dma_start(out=sf_ap[r][:, 0:half], in_=wst_d[l, :, c0:c0 + half]))(l, c0, half, r),
                         writes=sf_b[r], dma_key="wpi%d" % r)
                    if hi_ % 4 < 2:
                        P.op("dve", (lambda half, r: lambda e: e.tensor_copy(out=ob_ap[r][:, 0:half], in_=sf_ap[r][:, 0:half]))(half, r),
                             reads=sf_b[r], writes=ob_b[r])
                    else:
                        P.op("act", (lambda half, r: lambda e: e.activation(out=ob_ap[r][:, 0:half], in_=sf_ap[r][:, 0:half], func=AF.Copy))(half, r),
                             reads=sf_b[r], writes=ob_b[r])
                    P.op("sp", (lambda l, c0, half, r: lambda e: e.dma_start(out=wbf_d[l, :, c0:c0 + half], in_=ob_ap[r][:, 0:half]))(l, c0, half, r),
                         reads=ob_b[r], writes=[wbf_all_b], dma_key="wcst")
                    hi_ += 1
        P.op("dve", lambda e: e.memset(cst[:], 0.0), writes=[const_b])
        P.op("dve", lambda e: e.memset(cst[:, 0:1], EPS), writes=[const_b])
        P.op("dve", lambda e: e.memset(onesb[:], 1.0), writes=[const_b])
        P.op("dve", lambda e: e.memset(carry[:], 0.0), writes=[carry_b])
        P.op("dve", lambda e: e.memset(brow[:], 0.0), writes=[const_b])
        P.op("dve", lambda e: e.tensor_copy(out=trib[:], in_=smt[:, SM_TRI:SM_TRI + 128]), reads=[const_b], writes=[const_b])
        P.op("dve", lambda e: e.tensor_copy(out=trif[:], in_=smt[:, SM_TRI:SM_TRI + 128]), reads=[const_b], writes=[const_b])
        for hh_ in range(2):
            P.op("dve", (lambda hh_: lambda e: e.tensor_copy(out=trib2[:, hh_, :], in_=smt[:, SM_TRI:SM_TRI + 128]))(hh_), reads=[const_b], writes=[const_b])
        P.op("dve", lambda e: e.tensor_copy(out=rmb[:], in_=smt[:, SM_RM:SM_RM + 128]), reads=[const_b], writes=[const_b])
        for l in range(DEPTH):
            for g in range(4):
                o0 = SM_WSG + l * 512 + g * 128
                P.op("dve", (lambda l, g, o0: lambda e: e.tensor_tensor(out=wsg[:, l, g, :], in0=smt[:, o0:o0 + 128], in1=trif[:], op=ALU.mult))(l, g, o0),
                     reads=[const_b], writes=[const_b])
        tmpf = arena[:, 0:32, :].bitcast(F32)
        tmpf = tmpf.rearrange("p a c -> p (a c)")
        setup_b = pg[0:44]
        for l in range(DEPTH):
            bsrc = rw_d[:, l * RW_L + RW_BS:l * RW_L + RW_BS + 512]
            P.op("sp", (lambda bsrc: lambda e: e.dma_start(out=tmpf[0:1, 0:512], in_=bsrc))(bsrc), writes=setup_b, dma_key="c_b")
            bh = arena[0:1, 40, :]
            bl = arena[0:1, 41, :]
            P.op("dve", lambda e: e.tensor_copy(out=bh, in_=tmpf[0:1, 0:512]), reads=setup_b, writes=setup_b)
            P.op("dve", lambda e: e.tensor_copy(out=tmpf[0:1, 512:1024], in_=bh), reads=setup_b, writes=setup_b)
            P.op("dve", lambda e: e.tensor_tensor(out=tmpf[0:1, 1024:1536], in0=tmpf[0:1, 0:512], in1=tmpf[0:1, 512:1024], op=ALU.subtract), reads=setup_b, writes=setup_b)
            P.op("dve", lambda e: e.tensor_copy(out=bl, in_=tmpf[0:1, 1024:1536]), reads=setup_b, writes=setup_b)
            for sub in range(4):
                P.op("sp", (lambda l, sub, bh: lambda e: e.dma_start(out=brow[0:1, l, :, sub, :], in_=bh.rearrange("o (g i) -> o g i", g=4)))(l, sub, bh),
                     reads=setup_b, writes=[const_b], dma_key="c_b")
                P.op("sp", (lambda l, sub, bl: lambda e: e.dma_start(out=brow[32:33, l, :, sub, :], in_=bl.rearrange("o (g i) -> o g i", g=4)))(l, sub, bl),
                     reads=setup_b, writes=[const_b], dma_key="c_b")
            lsrc = rw_d[:, l * RW_L + RW_LAM:l * RW_L + RW_LAM + 256].partition_broadcast(128)
            P.op("sp", (lambda lsrc: lambda e: e.dma_start(out=tmpf[:, 2048:2304], in_=lsrc))(lsrc), writes=setup_b, dma_key="c_b")
            P.op("dve", lambda e: e.tensor_tensor(out=tmpf[:, 2304:2368], in0=tmpf[:, 2048:2112], in1=tmpf[:, 2112:2176], op=ALU.mult), reads=setup_b, writes=setup_b)
            P.op("dve", lambda e: e.tensor_tensor(out=tmpf[:, 2368:2432], in0=tmpf[:, 2176:2240], in1=tmpf[:, 2240:2304], op=ALU.mult), reads=setup_b, writes=setup_b)
            P.op("dve", lambda e: e.reduce_sum(out=tmpf[:, 2432:2434], in_=tmpf[:, 2304:2432].rearrange("p (a c) -> p a c", a=2), axis=AX.X), reads=setup_b, writes=setup_b)
            P.op("act", lambda e: e.activation(out=tmpf[:, 2434:2436], in_=tmpf[:, 2432:2434], func=AF.Exp), reads=setup_b, writes=setup_b)
            li = float(lambda_inits[l])
            P.op("dve", (lambda l, li: lambda e: e.scalar_tensor_tensor(out=cst[:, 1 + l:2 + l], in0=tmpf[:, 2435:2436], scalar=-li, in1=tmpf[:, 2434:2435], op0=ALU.add, op1=ALU.subtract))(l, li),
                 reads=setup_b, writes=[const_b])
            P.op("dve", (lambda l, li: lambda e: e.tensor_scalar(out=cst[:, 4 + l:5 + l], in0=smt[:, l * SM_L + SM_SUBG:l * SM_L + SM_SUBG + 1], scalar1=1.0 - li, scalar2=None, op0=ALU.mult))(l, li),
                 reads=[const_b], writes=[const_b])
        RW = min(S, 2048)
        posi = arena[:, 32:32 + RW // 256, :].bitcast(I32).rearrange("p a c -> p (a c)")
        for r0 in range(0, S, RW):
            ang = tmpf[:, 0:RW]
            kk = tmpf[:, RW:2 * RW]
            yy = tmpf[:, 2 * RW:3 * RW]
            zz = tmpf[:, 3 * RW:4 * RW]
            P.op("sp", (lambda r0: lambda e: e.dma_start(out=posi, in_=pos_d[:, r0:r0 + RW].partition_broadcast(128)))(r0), writes=setup_b, dma_key="c_b")
            P.op("dve", lambda e: e.tensor_copy(out=ang, in_=posi), reads=setup_b, writes=setup_b)
            P.op("dve", lambda e: e.tensor_scalar(out=ang, in0=ang, scalar1=smt[:, SM_FS:SM_FS + 1], scalar2=None, op0=ALU.mult), reads=setup_b + [const_b], writes=setup_b)
            P.op("dve", lambda e: e.tensor_scalar(out=kk, in0=ang, scalar1=1.0 / TWO_PI, scalar2=MAGIC, op0=ALU.mult, op1=ALU.add), reads=setup_b, writes=setup_b)
            P.op("dve", lambda e: e.tensor_scalar(out=kk, in0=kk, scalar1=-MAGIC, scalar2=None, op0=ALU.add), reads=setup_b, writes=setup_b)
            P.op("dve", lambda e: e.scalar_tensor_tensor(out=yy, in0=kk, scalar=-CW1, in1=ang, op0=ALU.mult, op1=ALU.add), reads=setup_b, writes=setup_b)
            P.op("dve", lambda e: e.scalar_tensor_tensor(out=yy, in0=kk, scalar=-CW2, in1=yy, op0=ALU.mult, op1=ALU.add), reads=setup_b, writes=setup_b)
            P.op("dve", lambda e: e.tensor_scalar(out=zz, in0=yy, scalar1=PI_LO, scalar2=-PI_LO, op0=ALU.min, op1=ALU.max), reads=setup_b, writes=setup_b)
            P.op("act", lambda e: e.activation(out=zz, in_=zz, func=AF.Sin), reads=setup_b, writes=setup_b)
            P.op("sp", (lambda r0: lambda e: e.dma_start(out=cs_d[:, 1, r0:r0 + RW], in_=zz))(r0), reads=setup_b, writes=[csd_b], dma_key="c_cs")
            P.op("dve", lambda e: e.tensor_scalar(out=yy, in0=yy, scalar1=math.pi / 2, scalar2=None, op0=ALU.add), reads=setup_b, writes=setup_b)
            P.op("dve", lambda e: e.tensor_scalar(out=kk, in0=yy, scalar1=math.pi, scalar2=None, op0=ALU.is_gt), reads=setup_b, writes=setup_b)
            P.op("dve", lambda e: e.scalar_tensor_tensor(out=yy, in0=kk, scalar=-TWO_PI, in1=yy, op0=ALU.mult, op1=ALU.add), reads=setup_b, writes=setup_b)
            P.op("dve", lambda e: e.tensor_scalar(out=ang, in0=yy, scalar1=PI_LO, scalar2=-PI_LO, op0=ALU.min, op1=ALU.max), reads=setup_b + [csd_b], writes=setup_b)
            P.op("act", lambda e: e.activation(out=ang, in_=ang, func=AF.Sin), reads=setup_b, writes=setup_b)
            P.op("sp", (lambda r0: lambda e: e.dma_start(out=cs_d[:, 0, r0:r0 + RW], in_=ang))(r0), reads=setup_b, writes=[csd_b], dma_key="c_cs")

        wstate = {"emitted": 0}
        wseq = [(t, l, b) for t in range(NT) for l in range(DEPTH) for b in range(NBLK)]

        def wload_upto(n):
            while wstate["emitted"] <= min(n, len(wseq) - 1):
                k = wstate["emitted"]
                t, l, b = wseq[k]
                s = k % NSLOT
                P.op("sp", (lambda l, b, s: lambda e: e.dma_start(out=wring[:, s, 0:BLK_LEN[b]], in_=wbf_d[l, :, BLK_OFF[b]:BLK_OFF[b + 1]]))(l, b, s),
                     reads=[wbf_all_b], writes=[wslot_b[s]], dma_key="w%d" % s)
                wstate["emitted"] += 1

        def wblock(t, l, b):
            n = (t * DEPTH + l) * NBLK + b
            wload_upto(n + NSLOT - 1)
            s = n % NSLOT
            return wring[:, s, :], wslot_b[s]

        kvstate = {"n": 0}

        def kvload(l, c, kb):
            n = kvstate["n"]
            kvstate["n"] += 1
            s = n % NKV
            P.op("sp", (lambda l, c, kb, s: lambda e: e.dma_start(out=kring[:, s, :], in_=kt_d[l, c, :, kb * TT:(kb + 1) * TT]))(l, c, kb, s),
                 reads=[ktd_b[l][kb], vvd_b[l][kb]], writes=[kvslot_b[s]], dma_key="kk%d" % s)
            P.op("sp", (lambda l, c, kb, s: lambda e: e.dma_start(
                out=vring[:, s, :].rearrange("p (a c) -> p a c", a=4),
                in_=vv_d[l, c, kb * TT:(kb + 1) * TT, :].rearrange("(a p) c -> p a c", p=128)))(l, c, kb, s),
                reads=[ktd_b[l][kb], vvd_b[l][kb]], writes=[kvslot_b[s]], dma_key="kv%d" % s)
            return s

        def mm(out, lhsT, rhs, start, stop, reads, bank):
            P.op("pe", lambda e: e.matmul(out, lhsT=lhsT, rhs=rhs, start=start, stop=stop), reads=reads, writes=[bank_b[bank]])

        def layer_norm(src_ap, src_b, l, g_col, b_col, last):
            bs, bq = 6, 7
            for kc in range(KC):
                r = kc % 2
                P.op("act", (lambda kc, r: lambda e: e.activation(out=hb_ap[:, r, :], in_=src_ap[:, kc, :], func=AF.Copy))(kc, r),
                     reads=[src_b[kc]], writes=hb_b[r])
                P.op("act", (lambda kc, r: lambda e: e.activation(out=sq_ap[:, r, :], in_=src_ap[:, kc, :], func=AF.Square))(kc, r),
                     reads=[src_b[kc]], writes=sq_b[r])
                mm(banks[bs][:], onesb[:], hb_ap[:, r, :], kc == 0, kc == KC - 1, hb_b[r] + [const_b], bs)
                mm(banks[bq][:], onesb[:], sq_ap[:, r, :], kc == 0, kc == KC - 1, sq_b[r] + [const_b], bq)
            mean = lnf_ap[:, 0, :]
            rstd = lnf_ap[:, 1, :]
            msq = lnf_ap[:, 2, :]
            P.op("dve", lambda e: e.tensor_scalar(out=mean, in0=banks[bs][:], scalar1=1.0 / D, scalar2=None, op0=ALU.mult), reads=[bank_b[bs]], writes=lnf_b[0])
            P.op("dve", lambda e: e.tensor_tensor(out=msq, in0=mean, in1=mean, op=ALU.mult), reads=lnf_b[0], writes=lnf_b[2])
            P.op("dve", lambda e: e.scalar_tensor_tensor(out=msq, in0=banks[bq][:], scalar=1.0 / D, in1=msq, op0=ALU.mult, op1=ALU.subtract), reads=[bank_b[bq]] + lnf_b[2], writes=lnf_b[2])
            P.op("act", lambda e: e.activation(out=rstd, in_=msq, func=AF.Ln, bias=cst[:, 0:1], scale=1.0), reads=lnf_b[2] + [const_b], writes=lnf_b[1])
            P.op("act", lambda e: e.activation(out=rstd, in_=rstd, func=AF.Exp, scale=-0.5), reads=lnf_b[1], writes=lnf_b[1])
            for kc in range(KC):
                r = kc % 2
                gcol = smt[:, l * SM_L + g_col + kc:l * SM_L + g_col + kc + 1]
                bcol = smt[:, l * SM_L + b_col + kc:l * SM_L + b_col + kc + 1]
                tmp = sl_ap[:, r, :]
                neng = "pool" if kc in (2, 5) else "dve"
                if neng == "pool":
                    tmp = ap_t[:, 2 + (kc // 4), :]
                    tmp_b = [ap_b[2 + (kc // 4)]]
                else:
                    tmp_b = sl_b[r]
                P.op(neng, (lambda kc, tmp: lambda e: e.tensor_tensor(out=tmp, in0=src_ap[:, kc, :], in1=mean, op=ALU.subtract))(kc, tmp),
                     reads=[src_b[kc]] + lnf_b[0], writes=tmp_b)
                P.op(neng, (lambda tmp: lambda e: e.tensor_tensor(out=tmp, in0=tmp, in1=rstd, op=ALU.mult))(tmp),
                     reads=tmp_b + lnf_b[1], writes=tmp_b)
                if not last:
                    P.op("act", (lambda kc, tmp, gcol, bcol: lambda e: e.activation(out=src_ap[:, kc, :], in_=tmp, func=AF.Identity, scale=gcol, bias=bcol))(kc, tmp, gcol, bcol),
                         reads=tmp_b + [const_b], writes=[src_b[kc]])
                    P.op("act", (lambda kc, tmp, gcol, bcol: lambda e: e.activation(out=xb[:, kc, :], in_=tmp, func=AF.Identity, scale=gcol, bias=bcol))(kc, tmp, gcol, bcol),
                         reads=tmp_b + [const_b], writes=[xb_b])
                else:
                    P.op("act", (lambda kc, tmp, gcol, bcol, r: lambda e: e.activation(out=ostg[:, r, :], in_=tmp, func=AF.Identity, scale=gcol, bias=bcol))(kc, tmp, gcol, bcol, r),
                         reads=tmp_b + [const_b], writes=[ostg_b[r]])
                    yield kc, r

        dbg_d = nc.dram_tensor("dbg", [128, 36, 512], BF16, kind="ExternalOutput").ap() if dump else None
        dbgu_d = nc.dram_tensor("dbgu", [128, 4, 512], F32, kind="ExternalOutput").ap() if dump else None
        dbgp_d = nc.dram_tensor("dbgp", [128, 4, 512], BF16, kind="ExternalOutput").ap() if dump else None
        dbga_d = nc.dram_tensor("dbga", [128, 5, 512], F32, kind="ExternalOutput").ap() if dump else None

        def stage_done(k):
            if stop is not None and k >= stop:
                raise _Stop()

        def tile_layer(t, l):
            stage_done(0)
            t0 = t * TT
            sml = l * SM_L
            P.tag = "load"
            if l == 0:
                P.op("sp", lambda e: e.dma_start(out=A[:], in_=xT_v[:, :, t0:t0 + TT]), writes=A_b, dma_key="xin")
                for kc_ in range(KC):
                    eng_ = ("dve", "act", "pool", "dve", "act", "dve", "act", "pool")[kc_]
                    if eng_ == "act":
                        P.op("act", (lambda kc_: lambda e: e.activation(out=xb[:, kc_, :], in_=A[:, kc_, :], func=AF.Copy))(kc_), reads=[A_b[kc_]], writes=[xb_b])
                    else:
                        P.op(eng_, (lambda kc_: lambda e: e.tensor_copy(out=xb[:, kc_, :], in_=A[:, kc_, :]))(kc_), reads=[A_b[kc_]], writes=[xb_b])
                P.op("sp", lambda e: e.dma_start(out=cst_t[:], in_=cs_d[:, :, t0:t0 + TT]), reads=[csd_b], writes=[cs_b], dma_key="csin")
            Ct = cst_t[:, 0, :]
            St = cst_t[:, 1, :]

            P.tag = "w_v"
            wv, wb_ = wblock(t, l, 0)
            wv3 = wv[:, 0:4096].rearrange("p (kc c) -> p kc c", kc=KC)
            for sub in range(4):
                bk = gp_bank()
                for kc in range(KC):
                    mm(banks[bk][:], xb[:, kc, sub * 128:(sub + 1) * 128], wv3[:, kc, :], kc == 0, kc == KC - 1, [xb_b, wb_], bk)
                r = sub % 2
                P.op("act", (lambda bk, r: lambda e: e.activation(out=vt[:, r, :], in_=banks[bk][:], func=AF.Gelu))(bk, r),
                     reads=[bank_b[bk]], writes=[vt_b[r]])
                P.op("dve", (lambda r: lambda e: e.bn_stats(out=vst6[:, r, :], in_=vt[:, r, :]))(r), reads=[vt_b[r]], writes=[vs_b[r]])
                P.op("dve", (lambda r: lambda e: e.bn_aggr(out=vmv[:, r, 0:2], in_=vst6[:, r, :]))(r), reads=[vs_b[r]], writes=[vs_b[r]])
                P.op("act", (lambda r: lambda e: e.activation(out=vmv[:, r, 2:3], in_=vmv[:, r, 1:2], func=AF.Sqrt, bias=cst[:, 0:1], scale=1.0))(r),
                     reads=[vs_b[r], const_b], writes=[vs_b[r]])
                P.op("dve", (lambda r: lambda e: e.reciprocal(out=vmv[:, r, 3:4], in_=vmv[:, r, 2:3]))(r), reads=[vs_b[r]], writes=[vs_b[r]])
                P.op("dve", (lambda r: lambda e: e.tensor_scalar(out=vt[:, r, :], in0=vt[:, r, :], scalar1=vmv[:, r, 0:1], scalar2=vmv[:, r, 3:4], op0=ALU.subtract, op1=ALU.mult))(r),
                     reads=[vt_b[r], vs_b[r]], writes=[vt_b[r]])
                P.op("pool", (lambda r: lambda e: e.tensor_tensor(out=vt[:, r, :], in0=vt[:, r, :], in1=gbc[:, l, 0, :], op=ALU.mult))(r),
                     reads=[vt_b[r], const_b], writes=[vt_b[r]])
                P.op("pool", (lambda r, sub: lambda e: e.tensor_tensor(out=vb_ap[:, sub, :], in0=vt[:, r, :], in1=gbc[:, l, 1, :], op=ALU.add))(r, sub),
                     reads=[vt_b[r], const_b], writes=vb_b[sub])
            P.tag = "w_u"
            wv, wb_ = wblock(t, l, 1)
            wv3 = wv[:, 0:4096].rearrange("p (kc c) -> p kc c", kc=KC)
            for j in range(4):
                bk = gp_bank()
                for kc in range(KC):
                    mm(banks[bk][:], wv3[:, kc, j * 128:(j + 1) * 128], xb[:, kc, :], kc == 0, kc == KC - 1, [xb_b, wb_], bk)
                P.op("act", (lambda bk, j: lambda e: e.activation(out=u_ap[:, j, :], in_=banks[bk][:], func=AF.Gelu))(bk, j),
                     reads=[bank_b[bk]], writes=u_b[j])
            P.tag = "w_qk"
            rope_pend = []

            def rope_finish(item):
                r, j, dst_ap, dst_b = item
                bk2 = gp_bank()
                mm(banks[bk2][:], rmb[:], raw[:, r, :], True, True, [raw_b[r], const_b], bk2)
                P.op("dve", (lambda bk2, r: lambda e: e.tensor_tensor(out=rt2[:, r, :], in0=banks[bk2][:], in1=St, op=ALU.mult))(bk2, r),
                     reads=[bank_b[bk2], cs_b], writes=[rt2_b[r]])
                P.op("pool", (lambda r, j, dst_ap: lambda e: e.tensor_tensor(out=dst_ap[:, j, :], in0=rt1[:, r, :], in1=rt2[:, r, :], op=ALU.add))(r, j, dst_ap),
                     reads=[rt1_b[r], rt2_b[r]], writes=dst_b[j])

            for bi, (dst_ap, dst_b) in ((2, (q_ap, q_b)), (3, (kst_ap, kst_b))):
                wv, wb_ = wblock(t, l, bi)
                wv3 = wv[:, 0:4096].rearrange("p (kc c) -> p kc c", kc=KC)
                for j in range(4):
                    bk = gp_bank()
                    for kc in range(KC):
                        mm(banks[bk][:], wv3[:, kc, j * 128:(j + 1) * 128], xb[:, kc, :], kc == 0, kc == KC - 1, [xb_b, wb_], bk)
                    r = j % 2
                    P.op("act", (lambda bk, r: lambda e: e.activation(out=raw[:, r, :], in_=banks[bk][:], func=AF.Copy))(bk, r),
                         reads=[bank_b[bk]], writes=[raw_b[r], bank_b[bk]])
                    P.op("dve", (lambda bk, r: lambda e: e.tensor_tensor(out=rt1[:, r, :], in0=banks[bk][:], in1=Ct, op=ALU.mult))(bk, r),
                         reads=[bank_b[bk], cs_b], writes=[rt1_b[r]])
                    if rope_pend:
                        rope_finish(rope_pend.pop())
                    rope_pend.append((r, j, dst_ap, dst_b[j] if False else dst_b))
            while rope_pend:
                rope_finish(rope_pend.pop())
            for c in range(4):
                P.op("sp", (lambda c: lambda e: e.dma_start(out=kt_d[l, c, :, t0:t0 + TT], in_=kst_ap[:, c, :]))(c),
                     reads=kst_b[c], writes=[ktd_b[l][t]], dma_key="kw%d" % c)
            P.tag = "w_V"
            wv, wb_ = wblock(t, l, 4)
            wv3 = wv[:, 0:4096].rearrange("p (kc c) -> p kc c", kc=KC)
            for sub in range(4):
                bk = gp_bank()
                for kc in range(KC):
                    mm(banks[bk][:], xb[:, kc, sub * 128:(sub + 1) * 128], wv3[:, kc, :], kc == 0, kc == KC - 1, [xb_b, wb_], bk)
                P.op("act", (lambda bk, sub: lambda e: e.activation(out=vst_ap[:, sub, :], in_=banks[bk][:], func=AF.Copy))(bk, sub),
                     reads=[bank_b[bk]], writes=vst_b[sub])
            for c in range(4):
                P.op("sp", (lambda c: lambda e: e.dma_start(
                    out=vv_d[l, c, t0:t0 + TT, :].rearrange("(a p) c -> p a c", p=128),
                    in_=vst_ap[:, :, c * 128:(c + 1) * 128]))(c),
                    reads=[b for bb in vst_b for b in bb], writes=[vvd_b[l][t]], dma_key="vw%d" % c)
            P.tag = "w_conv"
            for ci in range(12):
                j, kind = ci // 3, ci % 3
                bi, cc = 5 + ci // 4, ci % 4
                if cc == 0:
                    wv, wb_ = wblock(t, l, bi)
                    wv3 = wv[:, 0:4096].rearrange("p (kc c) -> p kc c", kc=KC)
                bk = gp_bank()
                for kc in range(KC):
                    mm(banks[bk][:], wv3[:, kc, cc * 128:(cc + 1) * 128], xb[:, kc, :], kc == 0, kc == KC - 1, [xb_b, wb_], bk)
                if kind == 0:
                    P.op("act", (lambda bk: lambda e: e.activation(out=cgc[:], in_=banks[bk][:], func=AF.Copy))(bk), reads=[bank_b[bk]], writes=[cgc_b])
                elif kind == 1:
                    P.op("pool", (lambda j: lambda e: e.tensor_copy(out=hcv[:, 0:2], in_=carry[:, l, j, :]))(j), reads=[carry_b], writes=[hcv_b])
                    P.op("dve", (lambda bk: lambda e: e.tensor_tensor(out=hcv[:, 2:514], in0=cgc[:], in1=banks[bk][:], op=ALU.mult))(bk),
                         reads=[cgc_b, bank_b[bk]], writes=[hcv_b])
                    P.op("pool", (lambda j: lambda e: e.tensor_copy(out=carry[:, l, j, :], in_=hcv[:, 512:514]))(j), reads=[hcv_b], writes=[carry_b])
                    w0 = smt[:, sml + SM_CONV + j * 3 + 0:sml + SM_CONV + j * 3 + 1]
                    w1 = smt[:, sml + SM_CONV + j * 3 + 1:sml + SM_CONV + j * 3 + 2]
                    w2 = smt[:, sml + SM_CONV + j * 3 + 2:sml + SM_CONV + j * 3 + 3]
                    P.op("dve", (lambda w0: lambda e: e.tensor_scalar(out=cacc[:], in0=hcv[:, 0:512], scalar1=w0, scalar2=None, op0=ALU.mult))(w0),
                         reads=[hcv_b, const_b], writes=[cacc_b])
                    P.op("dve", (lambda w1: lambda e: e.scalar_tensor_tensor(out=cacc[:], in0=hcv[:, 1:513], scalar=w1, in1=cacc[:], op0=ALU.mult, op1=ALU.add))(w1),
                         reads=[hcv_b, const_b, cacc_b], writes=[cacc_b])
                    P.op("dve", (lambda w2: lambda e: e.scalar_tensor_tensor(out=cacc[:], in0=hcv[:, 2:514], scalar=w2, in1=cacc[:], op0=ALU.mult, op1=ALU.add))(w2),
                         reads=[hcv_b, const_b, cacc_b], writes=[cacc_b])
                else:
                    P.op("dve", (lambda bk, j: lambda e: e.tensor_tensor(out=yc_ap[:, j, :], in0=cacc[:], in1=banks[bk][:], op=ALU.mult))(bk, j),
                         reads=[cacc_b, bank_b[bk]], writes=yc_b[j])
            P.tag = "sgu"
            for g in range(4):
                bk = gp_bank()
                mm(banks[bk][:], onesb[0:64, :], brow[0:64, l, g, :, :].rearrange("p a c -> p (a c)"), True, False, [const_b], bk)
                for sub in range(4):
                    mm(banks[bk][:, sub * 128:(sub + 1) * 128], vb_ap[:, sub, g * 128:(g + 1) * 128], wsg[:, l, g, :], False, sub == 3,
                       vb_b[sub] + [const_b], bk)
                P.op("dve", (lambda bk, g: lambda e: e.tensor_tensor(out=ya_ap[:, g, :], in0=u_ap[:, g, :], in1=banks[bk][:], op=ALU.mult))(bk, g),
                     reads=u_b[g] + [bank_b[bk]], writes=ya_b[g])
            stage_done(1)
            yield
            P.tag = "attn"
            neglam = cst[:, 1 + l:2 + l]
            gsc = cst[:, 4 + l:5 + l]
            kvseq = [(c, kb) for c in range(4) for kb in range(t + 1)]
            kvslots = {}
            kvn = {"i": 0}

            def kv_prefetch(upto):
                while kvn["i"] <= min(upto, len(kvseq) - 1):
                    c_, kb_ = kvseq[kvn["i"]]
                    kvslots[(c_, kb_)] = kvload(l, c_, kb_)
                    kvn["i"] += 1

            r1, o1, r2, o2, rr = (ap_t[:, i, :] for i in range(5))

            def post_head(c):
                P.op("dve", lambda e: e.tensor_copy(out=r1, in_=banks[6][:]), reads=[bank_b[6]], writes=[ap_b[0]])
                P.op("dve", lambda e: e.tensor_copy(out=o1, in_=banks[4][:]), reads=[bank_b[4]], writes=[ap_b[1]])
                P.op("dve", lambda e: e.tensor_copy(out=r2, in_=banks[7][:]), reads=[bank_b[7]], writes=[ap_b[2]])
                P.op("dve", lambda e: e.tensor_copy(out=o2, in_=banks[5][:]), reads=[bank_b[5]], writes=[ap_b[3]])
                P.op("dve", lambda e: e.reciprocal(out=r1, in_=r1), reads=[ap_b[0]], writes=[ap_b[0]])
                P.op("dve", lambda e: e.reciprocal(out=r2, in_=r2), reads=[ap_b[2]], writes=[ap_b[2]])
                P.op("dve", lambda e: e.tensor_tensor(out=o1, in0=o1, in1=r1, op=ALU.mult), reads=[ap_b[1], ap_b[0]], writes=[ap_b[1]])
                P.op("dve", lambda e: e.tensor_tensor(out=o2, in0=o2, in1=r2, op=ALU.mult), reads=[ap_b[3], ap_b[2]], writes=[ap_b[3]])
                P.op("dve", lambda e: e.scalar_tensor_tensor(out=o1, in0=o2, scalar=neglam, in1=o1, op0=ALU.mult, op1=ALU.add), reads=[ap_b[3], ap_b[1], const_b], writes=[ap_b[1]])

            def post_tail(c):
                P.op("act", lambda e: e.activation(out=apq[:], in_=o1, func=AF.Square), reads=[ap_b[1]], writes=[apq_b])
                bk = gp_bank(False)
                mm(banks[bk][:], onesb[:], apq[:], True, True, [apq_b, const_b], bk)
                P.op("act", (lambda bk: lambda e: e.activation(out=rr, in_=banks[bk][:], func=AF.Sqrt, bias=cst[:, 0:1], scale=1.0 / 128))(bk),
                     reads=[bank_b[bk], const_b], writes=[ap_b[4]])
                P.op("dve", lambda e: e.reciprocal(out=rr, in_=rr), reads=[ap_b[4]], writes=[ap_b[4]])
                P.op("dve", (lambda c: lambda e: e.scalar_tensor_tensor(out=yb_ap[:, c, :], in0=o1, scalar=gsc, in1=rr, op0=ALU.mult, op1=ALU.mult))(c),
                     reads=[ap_b[1], ap_b[4], const_b], writes=yb_b[c])

            pending_tail = None
            for c in range(4):
                pairs = [(kb, ks) for kb in range(t + 1) for ks in range(4)]
                npairs = len(pairs)
                defer_at = min(8, npairs - 1)
                pend = None

                def pv_l(pd, last):
                    ps_, pks, ppi, pq0, pii = pd
                    for hh in range(2):
                        mm(banks[4 + hh][:, pq0:TT], vring[:, ps_, pks * 128:(pks + 1) * 128], pt[:, ppi, hh, pq0:TT], pii == 0, last, [kvslot_b[ps_], pt_b[ppi]], 4 + hh)
                    for hh in range(2):
                        mm(banks[6 + hh][:, pq0:TT], onesb[:], pt[:, ppi, hh, pq0:TT], pii == 0, last, [const_b, pt_b[ppi]], 6 + hh)

                for ii, (kb, ks) in enumerate(pairs):
                    if ks == 0:
                        kv_prefetch(c * (t + 1) + kb + 2)
                    s = kvslots[(c, kb)]
                    diag = kb == t
                    q0 = ks * 128 if diag else 0
                    sp = ii % 2
                    pi = ii % NPT
                    for hh in range(2):
                        mm(pp[sp][:, hh * 512 + q0:(hh + 1) * 512], kring[hh * 64:(hh + 1) * 64, s, ks * 128:(ks + 1) * 128], q_ap[hh * 64:(hh + 1) * 64, c, q0:TT],
                           True, True, [kvslot_b[s]] + q_b[c], 2 * sp + hh)
                    P.op("act", (lambda sp, pi, q0: lambda e: e.activation(out=pt[:, pi, :, q0:TT], in_=pp[sp][:].rearrange("p (h c) -> p h c", h=2)[:, :, q0:TT], func=AF.Exp, scale=0.125))(sp, pi, q0),
                         reads=[bank_b[2 * sp], bank_b[2 * sp + 1]], writes=[pt_b[pi]])
                    if diag:
                        P.op("pool", (lambda pi, q0: lambda e: e.tensor_tensor(out=pt[:, pi, :, q0:q0 + 128], in0=pt[:, pi, :, q0:q0 + 128], in1=trib2[:], op=ALU.mult))(pi, q0),
                             reads=[pt_b[pi], const_b], writes=[pt_b[pi]])
                    if pend is not None:
                        pv_l(pend, False)
                    pend = (s, ks, pi, q0, ii)
                    if ii == defer_at and pending_tail is not None:
                        post_tail(pending_tail)
                        pending_tail = None
                pv_l(pend, True)
                post_head(c)
                pending_tail = c
                stage_done(2)
                yield
            post_tail(3)
            P.tag = "gates"
            ys = ((ya_ap, ya_b), (yb_ap, yb_b), (yc_ap, yc_b))
            for oc in range(8):
                wv, wb_ = wblock(t, l, 8 + oc)
                gw = wv[:, 0:3072].rearrange("p (n kc c) -> p n kc c", n=3, kc=KC)
                bw = wv[:, 3072:4608].rearrange("p (n kc c) -> p n kc c", n=3, kc=4)
                for n in range(3):
                    bk = gp_bank()
                    for kc in range(KC):
                        mm(banks[bk][:], gw[:, n, kc, :], xb[:, kc, :], kc == 0, kc == KC - 1, [xb_b, wb_], bk)
                    P.op("act", (lambda bk, n: lambda e: e.activation(out=sg[:, n, :], in_=banks[bk][:], func=AF.Sigmoid))(bk, n),
                         reads=[bank_b[bk]], writes=[sg_b[n]])
                for n in range(3):
                    bk = gp_bank()
                    y_ap, y_b = ys[n]
                    for kc in range(4):
                        mm(banks[bk][:], bw[:, n, kc, :], y_ap[:, kc, :], kc == 0, kc == 3, y_b[kc] + [wb_], bk)
                    P.op("dve", (lambda bk, n: lambda e: e.tensor_tensor(out=mt[:, n, :], in0=sg[:, n, :], in1=banks[bk][:], op=ALU.mult))(bk, n),
                         reads=[sg_b[n], bank_b[bk]], writes=[mt_b[n]])
                P.op("pool", lambda e: e.tensor_tensor(out=mt[:, 0, :], in0=mt[:, 0, :], in1=mt[:, 1, :], op=ALU.add), reads=[mt_b[0], mt_b[1]], writes=[mt_b[0]])
                P.op("pool", (lambda oc: lambda e: e.tensor_tensor(out=mg_ap[:, oc, :], in0=mt[:, 0, :], in1=mt[:, 2, :], op=ALU.add))(oc),
                     reads=[mt_b[0], mt_b[2]], writes=mg_b[oc])
            stage_done(3)
            yield
            P.tag = "w_o_ln1"
            for oc in range(8):
                if oc % 4 == 0:
                    wv, wb_ = wblock(t, l, 16 + oc // 4)
                    wv3 = wv[:, 0:4096].rearrange("p (kc c) -> p kc c", kc=KC)
                bk = gp_bank()
                for kc in range(KC):
                    mm(banks[bk][:], wv3[:, kc, (oc % 4) * 128:(oc % 4 + 1) * 128], mg_ap[:, kc, :], kc == 0, kc == KC - 1, mg_b[kc] + [wb_], bk)
                P.op("dve", (lambda bk, oc: lambda e: e.scalar_tensor_tensor(out=Bx[:, oc, :], in0=A[:, oc, :], scalar=ALPHA, in1=banks[bk][:], op0=ALU.mult, op1=ALU.add))(bk, oc),
                     reads=[A_b[oc], bank_b[bk]], writes=[B_b[oc]])
            for _ in layer_norm(Bx, B_b, l, SM_LN1G, SM_LN1B, False):
                pass
            stage_done(4)
            yield
            P.tag = "ffn_gu"
            for jb in range(11):
                wv, wb_ = wblock(t, l, 18 + jb)
                wv3 = wv[:, 0:4096].rearrange("p (kc c) -> p kc c", kc=KC)
                for jj in range(2):
                    j = 2 * jb + jj
                    bg_ = gp_bank()
                    for kc in range(KC):
                        mm(banks[bg_][:], wv3[:, kc, (2 * jj) * 128:(2 * jj + 1) * 128], xb[:, kc, :], kc == 0, kc == KC - 1, [xb_b, wb_], bg_)
                    bu_ = gp_bank()
                    for kc in range(KC):
                        mm(banks[bu_][:], wv3[:, kc, (2 * jj + 1) * 128:(2 * jj + 2) * 128], xb[:, kc, :], kc == 0, kc == KC - 1, [xb_b, wb_], bu_)
                    r = j % 2
                    P.op("act", (lambda bg_, r: lambda e: e.activation(out=sl_ap[:, r, :], in_=banks[bg_][:], func=AF.Silu))(bg_, r),
                         reads=[bank_b[bg_]], writes=sl_b[r])
                    P.op("dve", (lambda bu_, r, j: lambda e: e.tensor_tensor(out=hm_ap[:, j, :], in0=sl_ap[:, r, :], in1=banks[bu_][:], op=ALU.mult))(bu_, r, j),
                         reads=sl_b[r] + [bank_b[bu_]], writes=hm_b[j])
                if jb % 4 == 3:
                    stage_done(5)
            yield
            P.tag = "down_ln2"
            for oc in range(8):
                wv, wb_ = wblock(t, l, 29 + oc)
                wv3 = wv[:, 0:2816].rearrange("p (kc c) -> p kc c", kc=NFF)
                bk = gp_bank()
                for kc in range(NFF):
                    mm(banks[bk][:], wv3[:, kc, :], hm_ap[:, kc, :], kc == 0, kc == NFF - 1, hm_b[kc] + [wb_], bk)
                P.op("dve", (lambda bk, oc: lambda e: e.scalar_tensor_tensor(out=A[:, oc, :], in0=Bx[:, oc, :], scalar=ALPHA, in1=banks[bk][:], op0=ALU.mult, op1=ALU.add))(bk, oc),
                     reads=[B_b[oc], bank_b[bk]], writes=[A_b[oc]])
            last = l == DEPTH - 1
            for res in layer_norm(A, A_b, l, SM_LN2G, SM_LN2B, last):
                kc, r = res
                o = P.op("sp", (lambda kc, r: lambda e: e.dma_start(out=outT_v[:, kc, t0:t0 + TT], in_=ostg[:, r, :]))(kc, r),
                         reads=[ostg_b[r]], writes=[], dma_key="out%d" % r)
                out_toks.append(o.tok)
            stage_done(6)
            yield

        try:
            for t in range(NT):
                for l in range(DEPTH):
                    for _ in tile_layer(t, l):
                        pass
        except _Stop:
            pass
        if stop is not None:
            o = P.op("sp", lambda e: e.dma_start(out=outT_v[:, :, 0:TT], in_=A[:]), reads=A_b + B_b, writes=[], dma_key="out0")
            out_toks.append(o.tok)
        if dump:
            o = P.op("sp", lambda e: e.dma_start(out=dbg_d, in_=arena[:, 8:44, :]), reads=pg, writes=[], dma_key="out1")
            out_toks.append(o.tok)
            o = P.op("sp", lambda e: e.dma_start(out=dbgu_d, in_=u_ap), reads=pg, writes=[], dma_key="out1")
            out_toks.append(o.tok)
            if stop is not None and stop >= 2:
                o = P.op("sp", lambda e: e.dma_start(out=dbgp_d, in_=pt[:, 0:2, :, :].rearrange("p a h c -> p (a h) c")), reads=pt_b, writes=[], dma_key="out1")
                out_toks.append(o.tok)
                o = P.op("sp", lambda e: e.dma_start(out=dbga_d, in_=ap_t[:]), reads=ap_b, writes=[], dma_key="out1")
                out_toks.append(o.tok)

        fin = {}
        for s_, v_ in out_toks:
            fin[s_] = max(fin.get(s_, 0), v_)
        P.emit(nc, final_wait_tokens=list(fin.items()))
    build_program.last_prog = P
    return nc, len(P.ops)


def _wstream(w_in, w_branch, w_o, w_gate_up, w_down):
    def fm(cols):
        K = cols.shape[0]
        return cols.reshape(K // 128, 128, cols.shape[1]).transpose(1, 0, 2).reshape(128, -1)
    parts = []
    u = w_in[:, 0:512]
    v = w_in[:, 512:1024]
    q = w_in[:, 1024:1536]
    k = w_in[:, 1536:2048]
    vv = w_in[:, 2048:2560]
    bg = w_in[:, 2560:3072]
    cg = w_in[:, 3072:3584]
    xc = w_in[:, 3584:4096]
    zg = w_in[:, 4096:7168]
    conv_cols = []
    for j in range(4):
        for src in (cg, xc, bg):
            conv_cols.append(src[:, j * 128:(j + 1) * 128])
    conv = np.concatenate(conv_cols, axis=1)
    for blk in (v, u, q, k, vv, conv[:, 0:512], conv[:, 512:1024], conv[:, 1024:1536]):
        parts.append(fm(blk))
    for oc in range(8):
        for n in range(3):
            parts.append(fm(zg[:, n * 1024 + oc * 128:n * 1024 + (oc + 1) * 128]))
        for n in range(3):
            parts.append(fm(w_branch[n][:, oc * 128:(oc + 1) * 128]))
    for h in range(2):
        parts.append(fm(w_o[:, h * 512:(h + 1) * 512]))
    for jb in range(11):
        cols = []
        for jj in range(2):
            j = 2 * jb + jj
            cols.append(w_gate_up[:, j * 128:(j + 1) * 128])
            cols.append(w_gate_up[:, DFF + j * 128:DFF + (j + 1) * 128])
        parts.append(fm(np.concatenate(cols, axis=1)))
    for oc in range(8):
        parts.append(fm(w_down[:, oc * 128:(oc + 1) * 128]))
    out = np.concatenate(parts, axis=1)
    assert out.shape == (128, WCOLS), out.shape
    return out


def _const_tables():
    tri = (np.arange(128)[:, None] <= np.arange(128)[None, :]).astype(np.float32)
    rm = np.zeros((128, 128), np.float32)
    fs = np.zeros((128,), np.float32)
    inv_freq = (np.float32(ROPE_THETA) ** (-np.arange(0, 16, 2, dtype=np.float32) / np.float32(16))).astype(np.float32)
    for h in range(2):
        for d in range(16):
            src = d + 8 if d < 8 else d - 8
            rm[h * 64 + src, h * 64 + d] = 1.0
            fs[h * 64 + d] = -inv_freq[d] if d < 8 else inv_freq[d - 8]
    return tri, rm, fs


def _host_layout(inp, depth=DEPTH):
    f32 = np.float32
    tri, rm, fs = _const_tables()
    sm = np.zeros((128, NSM), f32)
    rw = np.zeros((1, NRW), f32)
    wst = np.zeros((depth, 128, WCOLS), f32)
    for l in range(depth):
        o = l * SM_L
        sm[:, o + SM_LN1G:o + SM_LN1G + 8] = inp["ln1_g"][l].reshape(8, 128).T
        sm[:, o + SM_LN1B:o + SM_LN1B + 8] = inp["ln1_b"][l].reshape(8, 128).T
        sm[:, o + SM_LN2G:o + SM_LN2G + 8] = inp["ln2_g"][l].reshape(8, 128).T
        sm[:, o + SM_LN2B:o + SM_LN2B + 8] = inp["ln2_b"][l].reshape(8, 128).T
        sm[:, o + SM_SUBG] = inp["subln_g"][l]
        sm[:, o + SM_CONV:o + SM_CONV + 12] = inp["conv_w"][l].reshape(4, 128, 3).transpose(1, 0, 2).reshape(128, 12)
        sm[:, SM_WSG + l * 512:SM_WSG + (l + 1) * 512] = inp["w_sgu"][l].transpose(2, 0, 1).reshape(128, 512)
        r = l * RW_L
        rw[0, r + RW_G:r + RW_G + 512] = inp["sgu_ln_g"][l]
        rw[0, r + RW_B:r + RW_B + 512] = inp["sgu_ln_b"][l]
        rw[0, r + RW_BS:r + RW_BS + 512] = inp["b_sgu"][l].reshape(512)
        rw[0, r + RW_LAM:r + RW_LAM + 256] = np.concatenate(
            [inp["lambda_q1"][l], inp["lambda_k1"][l], inp["lambda_q2"][l], inp["lambda_k2"][l]])
        wst[l] = _wstream(inp["w_in"][l], inp["w_branch"][l], inp["w_o"][l], inp["w_gate_up"][l], inp["w_down"][l])
    sm[:, SM_FS] = fs
    sm[:, SM_TRI:SM_TRI + 128] = tri
    sm[:, SM_RM:SM_RM + 128] = rm
    return sm, rw, wst


_CACHE = {}


def run_cores(inp, n_cores, S):
    inp = {k: np.asarray(v) for k, v in inp.items()}
    sm, rw, wst = _host_layout(inp)
    lambda_inits = [0.8 - 0.6 * math.exp(-0.3 * l) for l in range(DEPTH)]
    key = S
    if key not in _CACHE:
        _CACHE[key] = build_program(S, lambda_inits)[0]
    nc = _CACHE[key]
    in_maps = []
    for c in range(n_cores):
        in_maps.append({
            "xT": np.ascontiguousarray(inp["x"][c].T.astype(np.float32)),
            "pos": np.ascontiguousarray(inp["positions"][c].reshape(1, S).astype(np.int32)),
            "wst": wst, "sm": sm, "rw": rw,
        })
    res = run_bass_kernel_spmd(nc, in_maps, core_ids=list(range(n_cores)))
    out = np.stack([np.ascontiguousarray(r["outT"].T) for r in res.results], axis=0)
    return out.astype(np.float32)


def kernel(**inputs):
    x = np.asarray(inputs["x"])
    B, S, _ = x.shape
    assert B == N_CORES
    return run_cores(inputs, N_CORES, S)
```

```python
import contextlib
import math
import numpy as np
import concourse.bass as bass
import concourse.mybir as mybir
from concourse.bass_utils import run_bass_kernel_spmd

F32 = mybir.dt.float32
BF16 = mybir.dt.bfloat16
I32 = mybir.dt.int32
AF = mybir.ActivationFunctionType
ALU = mybir.AluOpType
AX = mybir.AxisListType

ENGS = ("pe", "act", "dve", "pool", "sp")

D = 1024
KC = 8
DEPTH = 2
TT = 512
DFF = 2816
NFF = 22
ALPHA = (2 * DEPTH) ** 0.25
EPS = 1e-5
ROPE_THETA = 500000.0
N_CORES = 8

BLK_LEN = [4096] * 8 + [4608] * 8 + [4096] * 2 + [4096] * 11 + [2816] * 8
NBLK = len(BLK_LEN)
BLK_OFF = [0]
for _x in BLK_LEN:
    BLK_OFF.append(BLK_OFF[-1] + _x)
WCOLS = BLK_OFF[-1]
SLOT = 4608
NSLOT = 3
NKV = 4

SM_L = 48
SM_LN1G, SM_LN1B, SM_LN2G, SM_LN2B, SM_SUBG, SM_CONV = 0, 8, 16, 24, 32, 33
SM_FS = 96
SM_TRI = 98
SM_RM = SM_TRI + 128
SM_WSG = SM_RM + 128
NSM = SM_WSG + DEPTH * 512
RW_L = 1792
RW_G, RW_B, RW_BS, RW_LAM = 0, 512, 1024, 1536
NRW = DEPTH * RW_L

MAGIC = 12582912.0
TWO_PI = 2.0 * math.pi
CW1 = 6.28125
CW2 = TWO_PI - CW1
PI_LO = 3.1415925


class Buf:
    __slots__ = ("name", "w", "r")

    def __init__(self, name=""):
        self.name = name
        self.w = {}
        self.r = {}


class Op:
    __slots__ = ("eng", "fn", "deps", "signals", "dma_key", "tok", "idx", "tag")


class Prog:
    def __init__(self):
        self.ops = []
        self.dma_count = {}
        self.tag = ""

    def op(self, eng, fn, reads=(), writes=(), dma_key=None):
        o = Op()
        o.eng = eng
        o.fn = fn
        o.dma_key = dma_key
        o.signals = False
        o.tag = self.tag
        o.idx = len(self.ops)
        deps = set()
        for b in reads:
            deps.update(b.w.values())
        for b in writes:
            deps.update(b.w.values())
            deps.update(b.r.values())
        o.deps = deps
        key = ("dma", dma_key) if dma_key is not None else eng
        for b in reads:
            b.r[key] = o.idx
        for b in writes:
            b.w[key] = o.idx
        if dma_key is not None:
            n = self.dma_count.get(dma_key, 0) + 1
            self.dma_count[dma_key] = n
            o.tok = (("dma", dma_key), 16 * n)
        else:
            o.tok = None
        self.ops.append(o)
        return o

    @staticmethod
    def _skip(p, o):
        return p.dma_key is None and o.dma_key is None and p.eng == "pe" and o.eng == "pe"

    def resolve(self):
        ops = self.ops
        for o in ops:
            for d in o.deps:
                p = ops[d]
                if p.dma_key is None and not self._skip(p, o):
                    p.signals = True
        cnt = {e: 0 for e in ENGS}
        for o in ops:
            if o.dma_key is None and o.signals:
                cnt[o.eng] += 1
                o.tok = (o.eng, cnt[o.eng])
        waited = {e: {} for e in ENGS}
        self.waits = []
        for o in ops:
            need = {}
            for d in o.deps:
                p = ops[d]
                if self._skip(p, o) or p.tok is None:
                    continue
                s, v = p.tok
                if v > need.get(s, 0):
                    need[s] = v
            ws = []
            wd = waited[o.eng]
            for s, v in need.items():
                if v > wd.get(s, 0):
                    wd[s] = v
                    ws.append((s, v))
            self.waits.append(ws)
        return cnt

    def emit(self, nc, final_wait_tokens=()):
        self.resolve()
        sem_keys = list(ENGS) + [("dma", k) for k in self.dma_count]
        with contextlib.ExitStack() as st:
            sems = {}
            for k in sem_keys:
                nm = "s_" + (k if isinstance(k, str) else "d_" + str(k[1]))
                sems[k] = st.enter_context(nc.semaphore(nm))
            block = st.enter_context(nc.Block())
            per = {e: [] for e in ENGS}
            for o in self.ops:
                per[o.eng].append(o)
            waits = self.waits

            def run(engine_name, eng):
                for o in per[engine_name]:
                    for s, v in waits[o.idx]:
                        eng.wait_ge(sems[s], v)
                    ins = o.fn(eng)
                    if o.dma_key is not None:
                        ins.then_inc(sems[("dma", o.dma_key)], 16)
                    elif o.signals:
                        ins.then_inc(sems[o.eng], 1)
                if engine_name == "sp":
                    for (s, v) in final_wait_tokens:
                        eng.wait_ge(sems[s], v)

            @block.tensor
            def _(e):
                run("pe", e)

            @block.scalar
            def _(e):
                run("act", e)

            @block.vector
            def _(e):
                run("dve", e)

            @block.gpsimd
            def _(e):
                run("pool", e)

            @block.sync
            def _(e):
                run("sp", e)


class _Stop(Exception):
    pass


def build_program(S, lambda_inits, stop=None, dump=False):
    NT = S // TT
    nc = bass.Bass("TRN2", target_bir_lowering=False)
    P = Prog()

    xT_d = nc.dram_tensor("xT", [D, S], F32, kind="ExternalInput").ap()
    pos_d = nc.dram_tensor("pos", [1, S], I32, kind="ExternalInput").ap()
    wst_d = nc.dram_tensor("wst", [DEPTH, 128, WCOLS], F32, kind="ExternalInput").ap()
    sm_d = nc.dram_tensor("sm", [128, NSM], F32, kind="ExternalInput").ap()
    rw_d = nc.dram_tensor("rw", [1, NRW], F32, kind="ExternalInput").ap()
    outT_d = nc.dram_tensor("outT", [D, S], F32, kind="ExternalOutput").ap()
    wbf_d = nc.dram_tensor("wbf", [DEPTH, 128, WCOLS], BF16, kind="Internal").ap()
    cs_d = nc.dram_tensor("cs", [128, 2, S], F32, kind="Internal").ap()
    kt_d = nc.dram_tensor("ktc", [DEPTH, 4, 128, S], BF16, kind="Internal").ap()
    vv_d = nc.dram_tensor("vvc", [DEPTH, 4, S, 128], BF16, kind="Internal").ap()

    xT_v = xT_d.rearrange("(kc p) s -> p kc s", p=128)
    outT_v = outT_d.rearrange("(kc p) s -> p kc s", p=128)

    st = contextlib.ExitStack()
    with st:
        def sb(name, shape, dt):
            return st.enter_context(nc.sbuf_tensor(name, shape, dt))

        wring = sb("wring", [128, NSLOT, SLOT], BF16)
        kring = sb("kring", [128, NKV, 512], BF16)
        vring = sb("vring", [128, NKV, 512], BF16)
        A = sb("A", [128, KC, TT], F32)
        Bx = sb("Bx", [128, KC, TT], F32)
        xb = sb("xb", [128, KC, TT], BF16)
        smt = sb("smt", [128, NSM], F32)
        gbc = sb("gbc", [128, DEPTH, 2, 512], F32)
        wsg = sb("wsg", [128, DEPTH, 4, 128], BF16)
        trib = sb("trib", [128, 128], BF16)
        trif = sb("trif", [128, 128], F32)
        rmb = sb("rmb", [128, 128], BF16)
        onesb = sb("onesb", [128, 128], BF16)
        brow = sb("brow", [64, DEPTH, 4, 4, 128], BF16)
        cst = sb("cst", [128, 16], F32)
        carry = sb("carry", [128, DEPTH, 4, 2], F32)
        NPG = 44
        arena = sb("arena", [128, NPG, 512], BF16)
        pg = [Buf("pg%d" % i) for i in range(NPG)]

        def carve(lo, n, dt=BF16):
            ap = arena[:, lo:lo + n, :]
            if dt == F32:
                ap = arena[:, lo:lo + n, :].bitcast(F32)
            return ap, pg[lo:lo + n]

        u_ap = arena[:, 0:8, :].bitcast(F32).rearrange("p (j h) c -> p j (h c)", h=2)
        u_b = [pg[2 * j:2 * j + 2] for j in range(4)]
        vb_ap = arena[:, 8:12, :]
        vb_b = [pg[8 + i:9 + i] for i in range(4)]
        q_ap = arena[:, 12:16, :]
        q_b = [pg[12 + i:13 + i] for i in range(4)]
        kst_ap = arena[:, 16:20, :]
        kst_b = [pg[16 + i:17 + i] for i in range(4)]
        vst_ap = arena[:, 20:24, :]
        vst_b = [pg[20 + i:21 + i] for i in range(4)]
        ya_ap = arena[:, 24:28, :]
        ya_b = [pg[24 + i:25 + i] for i in range(4)]
        yb_ap = arena[:, 28:32, :]
        yb_b = [pg[28 + i:29 + i] for i in range(4)]
        yc_ap = arena[:, 32:36, :]
        yc_b = [pg[32 + i:33 + i] for i in range(4)]
        mg_ap = arena[:, 36:44, :]
        mg_b = [pg[36 + i:37 + i] for i in range(8)]
        hm_ap = arena[:, 0:22, :]
        hm_b = [pg[i:i + 1] for i in range(22)]
        sl_ap = arena[:, 22:26, :].bitcast(F32).rearrange("p (j h) c -> p j (h c)", h=2)
        sl_b = [pg[22:24], pg[24:26]]
        hb_ap = arena[:, 26:28, :]
        hb_b = [pg[26:27], pg[27:28]]
        sq_ap = arena[:, 28:30, :]
        sq_b = [pg[28:29], pg[29:30]]
        lnf_ap = arena[:, 30:36, :].bitcast(F32).rearrange("p (j h) c -> p j (h c)", h=2)
        lnf_b = [pg[30:32], pg[32:34], pg[34:36]]

        vt = sb("vt", [128, 2, 512], F32)
        vt_b = [Buf("vt0"), Buf("vt1")]
        vst6 = sb("vst6", [128, 2, 6], F32)
        vmv = sb("vmv", [128, 2, 4], F32)
        vs_b = [Buf("vs0"), Buf("vs1")]
        cst_t = sb("cs_t", [128, 2, 512], F32)
        cs_b = Buf("cs")
        raw = sb("raw", [128, 2, 512], BF16)
        raw_b = [Buf("raw0"), Buf("raw1")]
        rt1 = sb("rt1", [128, 2, 512], F32)
        rt1_b = [Buf("rt1_0"), Buf("rt1_1")]
        rt2 = sb("rt2", [128, 2, 512], F32)
        rt2_b = [Buf("rt2_0"), Buf("rt2_1")]
        cgc = sb("cgc", [128, 512], F32)
        cgc_b = Buf("cgc")
        hcv = sb("hcv", [128, 516], F32)
        hcv_b = Buf("hcv")
        cacc = sb("cacc", [128, 512], F32)
        cacc_b = Buf("cacc")
        sg = sb("sg", [128, 3, 512], F32)
        sg_b = [Buf("sg0"), Buf("sg1"), Buf("sg2")]
        mt = sb("mt", [128, 3, 512], F32)
        mt_b = [Buf("mt0"), Buf("mt1"), Buf("mt2")]
        NPT = 4
        pt = sb("pt", [128, NPT, 2, 512], BF16)
        pt_b = [Buf("pt%d" % i) for i in range(NPT)]
        trib2 = sb("trib2", [128, 2, 128], BF16)
        ap_t = sb("ap_t", [128, 5, 512], F32)
        ap_b = [Buf("apt%d" % i) for i in range(5)]
        apq = sb("apq", [128, 512], BF16)
        apq_b = Buf("apq")
        ostg = sb("ostg", [128, 2, 512], F32)
        ostg_b = [Buf("ostg0"), Buf("ostg1")]

        pp = [st.enter_context(nc.psum_tensor("pp%d" % i, [128, 1024], F32)) for i in range(4)]
        banks = [pp[i // 2][:, (i % 2) * 512:(i % 2 + 1) * 512] for i in range(8)]
        bank_b = [Buf("bank%d" % i) for i in range(8)]

        A_b = [Buf("A%d" % i) for i in range(KC)]
        B_b = [Buf("B%d" % i) for i in range(KC)]
        xb_b = [Buf("xb%d" % i) for i in range(KC)]
        const_b = Buf("const")
        carry_b = Buf("carry")
        wslot_b = [Buf("ws%d" % i) for i in range(NSLOT)]
        kvslot_b = [Buf("kv%d" % i) for i in range(NKV)]
        wbf_b = [[Buf("wbf%d_%d" % (l, b)) for b in range(NBLK)] for l in range(DEPTH)]
        csd_b = Buf("csd")
        ktd_b = [[Buf("ktd%d_%d" % (l, t)) for t in range(NT)] for l in range(DEPTH)]
        vvd_b = [[Buf("vvd%d_%d" % (l, t)) for t in range(NT)] for l in range(DEPTH)]
        out_toks = []

        gp_state = {"i": 0}

        def gp_bank(use_all=True):
            n = 4 if use_all else 2
            i = gp_state["i"] % n
            gp_state["i"] += 1
            return i

        if dump:
            P.op("pool", lambda e: e.memset(arena[:], 0.0), writes=pg)
            P.op("pool", lambda e: e.memset(pt[:], 0.0), writes=pt_b)
            P.op("pool", lambda e: e.memset(ap_t[:], 0.0), writes=ap_b)
        P.op("sp", lambda e: e.dma_start(out=smt[:], in_=sm_d), writes=[const_b], dma_key="c_sm")
        for l in range(DEPTH):
            P.op("sp", (lambda l: lambda e: e.dma_start(
                out=gbc[:, l, :, :], in_=rw_d[:, l * RW_L:l * RW_L + 1024].rearrange("o (a c) -> o a c", a=2).partition_broadcast(128)))(l),
                writes=[const_b], dma_key="c_gb")
        sf_ap = [arena[:, 0:9, :].bitcast(F32).rearrange("p a c -> p (a c)"), arena[:, 9:18, :].bitcast(F32).rearrange("p a c -> p (a c)")]
        sf_b = [pg[0:9], pg[9:18]]
        ob_ap = [arena[:, 18:23, :].rearrange("p a c -> p (a c)"), arena[:, 23:28, :].rearrange("p a c -> p (a c)")]
        ob_b = [pg[18:23], pg[23:28]]
        wbf_all_b = Buf("wbf_all")
        hi_ = 0
        for l in range(DEPTH):
            for b in range(NBLK):
                half = BLK_LEN[b] // 2
                for h in range(2):
                    c0 = BLK_OFF[b] + h * half
                    r = hi_ % 2
                    P.op("sp", (lambda l, c0, half, r: lambda e: e.dma_start(out=sf_ap[r][:, 0:half], in_=wst_d[l, :, c0:c0 + half]))(l, c0, half, r),
                         writes=sf_b[r], dma_key="wpi%d" % r)
                    if hi_ % 4 < 2:
                        P.op("dve", (lambda half, r: lambda e: e.tensor_copy(out=ob_ap[r][:, 0:half], in_=sf_ap[r][:, 0:half]))(half, r),
                             reads=sf_b[r], writes=ob_b[r])
                    else:
                        P.op("act", (lambda half, r: lambda e: e.activation(out=ob_ap[r][:, 0:half], in_=sf_ap[r][:, 0:half], func=AF.Copy))(half, r),
                             reads=sf_b[r], writes=ob_b[r])
                    P.op("sp", (lambda l, c0, half, r: lambda e: e.dma_start(out=wbf_d[l, :, c0:c0 + half], in_=ob_ap[r][:, 0:half]))(l, c0, half, r),
                         reads=ob_b[r], writes=[wbf_all_b], dma_key="wcst")
                    hi_ += 1
        P.op("dve", lambda e: e.memset(cst[:], 0.0), writes=[const_b])
        P.op("dve", lambda e: e.memset(cst[:, 0:1], EPS), writes=[const_b])
        P.op("dve", lambda e: e.memset(onesb[:], 1.0), writes=[const_b])
        P.op("dve", lambda e: e.memset(carry[:], 0.0), writes=[carry_b])
        P.op("dve", lambda e: e.memset(brow[:], 0.0), writes=[const_b])
        P.op("dve", lambda e: e.tensor_copy(out=trib[:], in_=smt[:, SM_TRI:SM_TRI + 128]), reads=[const_b], writes=[const_b])
        P.op("dve", lambda e: e.tensor_copy(out=trif[:], in_=smt[:, SM_TRI:SM_TRI + 128]), reads=[const_b], writes=[const_b])
        for hh_ in range(2):
            P.op("dve", (lambda hh_: lambda e: e.tensor_copy(out=trib2[:, hh_, :], in_=smt[:, SM_TRI:SM_TRI + 128]))(hh_), reads=[const_b], writes=[const_b])
        P.op("dve", lambda e: e.tensor_copy(out=rmb[:], in_=smt[:, SM_RM:SM_RM + 128]), reads=[const_b], writes=[const_b])
        for l in range(DEPTH):
            for g in range(4):
                o0 = SM_WSG + l * 512 + g * 128
                P.op("dve", (lambda l, g, o0: lambda e: e.tensor_tensor(out=wsg[:, l, g, :], in0=smt[:, o0:o0 + 128], in1=trif[:], op=ALU.mult))(l, g, o0),
                     reads=[const_b], writes=[const_b])
        tmpf = arena[:, 0:32, :].bitcast(F32)
        tmpf = tmpf.rearrange("p a c -> p (a c)")
        setup_b = pg[0:44]
        for l in range(DEPTH):
            bsrc = rw_d[:, l * RW_L + RW_BS:l * RW_L + RW_BS + 512]
            P.op("sp", (lambda bsrc: lambda e: e.dma_start(out=tmpf[0:1, 0:512], in_=bsrc))(bsrc), writes=setup_b, dma_key="c_b")
            bh = arena[0:1, 40, :]
            bl = arena[0:1, 41, :]
            P.op("dve", lambda e: e.tensor_copy(out=bh, in_=tmpf[0:1, 0:512]), reads=setup_b, writes=setup_b)
            P.op("dve", lambda e: e.tensor_copy(out=tmpf[0:1, 512:1024], in_=bh), reads=setup_b, writes=setup_b)
            P.op("dve", lambda e: e.tensor_tensor(out=tmpf[0:1, 1024:1536], in0=tmpf[0:1, 0:512], in1=tmpf[0:1, 512:1024], op=ALU.subtract), reads=setup_b, writes=setup_b)
            P.op("dve", lambda e: e.tensor_copy(out=bl, in_=tmpf[0:1, 1024:1536]), reads=setup_b, writes=setup_b)
            for sub in range(4):
                P.op("sp", (lambda l, sub, bh: lambda e: e.dma_start(out=brow[0:1, l, :, sub, :], in_=bh.rearrange("o (g i) -> o g i", g=4)))(l, sub, bh),
                     reads=setup_b, writes=[const_b], dma_key="c_b")
                P.op("sp", (lambda l, sub, bl: lambda e: e.dma_start(out=brow[32:33, l, :, sub, :], in_=bl.rearrange("o (g i) -> o g i", g=4)))(l, sub, bl),
                     reads=setup_b, writes=[const_b], dma_key="c_b")
            lsrc = rw_d[:, l * RW_L + RW_LAM:l * RW_L + RW_LAM + 256].partition_broadcast(128)
            P.op("sp", (lambda lsrc: lambda e: e.dma_start(out=tmpf[:, 2048:2304], in_=lsrc))(lsrc), writes=setup_b, dma_key="c_b")
            P.op("dve", lambda e: e.tensor_tensor(out=tmpf[:, 2304:2368], in0=tmpf[:, 2048:2112], in1=tmpf[:, 2112:2176], op=ALU.mult), reads=setup_b, writes=setup_b)
            P.op("dve", lambda e: e.tensor_tensor(out=tmpf[:, 2368:2432], in0=tmpf[:, 2176:2240], in1=tmpf[:, 2240:2304], op=ALU.mult), reads=setup_b, writes=setup_b)
            P.op("dve", lambda e: e.reduce_sum(out=tmpf[:, 2432:2434], in_=tmpf[:, 2304:2432].rearrange("p (a c) -> p a c", a=2), axis=AX.X), reads=setup_b, writes=setup_b)
            P.op("act", lambda e: e.activation(out=tmpf[:, 2434:2436], in_=tmpf[:, 2432:2434], func=AF.Exp), reads=setup_b, writes=setup_b)
            li = float(lambda_inits[l])
            P.op("dve", (lambda l, li: lambda e: e.scalar_tensor_tensor(out=cst[:, 1 + l:2 + l], in0=tmpf[:, 2435:2436], scalar=-li, in1=tmpf[:, 2434:2435], op0=ALU.add, op1=ALU.subtract))(l, li),
                 reads=setup_b, writes=[const_b])
            P.op("dve", (lambda l, li: lambda e: e.tensor_scalar(out=cst[:, 4 + l:5 + l], in0=smt[:, l * SM_L + SM_SUBG:l * SM_L + SM_SUBG + 1], scalar1=1.0 - li, scalar2=None, op0=ALU.mult))(l, li),
                 reads=[const_b], writes=[const_b])
        RW = min(S, 2048)
        posi = arena[:, 32:32 + RW // 256, :].bitcast(I32).rearrange("p a c -> p (a c)")
        for r0 in range(0, S, RW):
            ang = tmpf[:, 0:RW]
            kk = tmpf[:, RW:2 * RW]
            yy = tmpf[:, 2 * RW:3 * RW]
            zz = tmpf[:, 3 * RW:4 * RW]
            P.op("sp", (lambda r0: lambda e: e.dma_start(out=posi, in_=pos_d[:, r0:r0 + RW].partition_broadcast(128)))(r0), writes=setup_b, dma_key="c_b")
            P.op("dve", lambda e: e.tensor_copy(out=ang, in_=posi), reads=setup_b, writes=setup_b)
            P.op("dve", lambda e: e.tensor_scalar(out=ang, in0=ang, scalar1=smt[:, SM_FS:SM_FS + 1], scalar2=None, op0=ALU.mult), reads=setup_b + [const_b], writes=setup_b)
            P.op("dve", lambda e: e.tensor_scalar(out=kk, in0=ang, scalar1=1.0 / TWO_PI, scalar2=MAGIC, op0=ALU.mult, op1=ALU.add), reads=setup_b, writes=setup_b)
            P.op("dve", lambda e: e.tensor_scalar(out=kk, in0=kk, scalar1=-MAGIC, scalar2=None, op0=ALU.add), reads=setup_b, writes=setup_b)
            P.op("dve", lambda e: e.scalar_tensor_tensor(out=yy, in0=kk, scalar=-CW1, in1=ang, op0=ALU.mult, op1=ALU.add), reads=setup_b, writes=setup_b)
            P.op("dve", lambda e: e.scalar_tensor_tensor(out=yy, in0=kk, scalar=-CW2, in1=yy, op0=ALU.mult, op1=ALU.add), reads=setup_b, writes=setup_b)
            P.op("dve", lambda e: e.tensor_scalar(out=zz, in0=yy, scalar1=PI_LO, scalar2=-PI_LO, op0=ALU.min, op1=ALU.max), reads=setup_b, writes=setup_b)
            P.op("act", lambda e: e.activation(out=zz, in_=zz, func=AF.Sin), reads=setup_b, writes=setup_b)
            P.op("sp", (lambda r0: lambda e: e.dma_start(out=cs_d[:, 1, r0:r0 + RW], in_=zz))(r0), reads=setup_b, writes=[csd_b], dma_key="c_cs")
            P.op("dve", lambda e: e.tensor_scalar(out=yy, in0=yy, scalar1=math.pi / 2, scalar2=None, op0=ALU.add), reads=setup_b, writes=setup_b)
            P.op("dve", lambda e: e.tensor_scalar(out=kk, in0=yy, scalar1=math.pi, scalar2=None, op0=ALU.is_gt), reads=setup_b, writes=setup_b)
            P.op("dve", lambda e: e.scalar_tensor_tensor(out=yy, in0=kk, scalar=-TWO_PI, in1=yy, op0=ALU.mult, op1=ALU.add), reads=setup_b, writes=setup_b)
            P.op("dve", lambda e: e.tensor_scalar(out=ang, in0=yy, scalar1=PI_LO, scalar2=-PI_LO, op0=ALU.min, op1=ALU.max), reads=setup_b + [csd_b], writes=setup_b)
            P.op("act", lambda e: e.activation(out=ang, in_=ang, func=AF.Sin), reads=setup_b, writes=setup_b)
            P.op("sp", (lambda r0: lambda e: e.dma_start(out=cs_d[:, 0, r0:r0 + RW], in_=ang))(r0), reads=setup_b, writes=[csd_b], dma_key="c_cs")

        wstate = {"emitted": 0}
        wseq = [(t, l, b) for t in range(NT) for l in range(DEPTH) for b in range(NBLK)]

        def wload_upto(n):
            while wstate["emitted"] <= min(n, len(wseq) - 1):
                k = wstate["emitted"]
                t, l, b = wseq[k]
                s = k % NSLOT
                P.op("sp", (lambda l, b, s: lambda e: e.dma_start(out=wring[:, s, 0:BLK_LEN[b]], in_=wbf_d[l, :, BLK_OFF[b]:BLK_OFF[b + 1]]))(l, b, s),
                     reads=[wbf_all_b], writes=[wslot_b[s]], dma_key="w%d" % s)
                wstate["emitted"] += 1

        def wblock(t, l, b):
            n = (t * DEPTH + l) * NBLK + b
            wload_upto(n + NSLOT - 1)
            s = n % NSLOT
            return wring[:, s, :], wslot_b[s]

        kvstate = {"n": 0}

        def kvload(l, c, kb):
            n = kvstate["n"]
            kvstate["n"] += 1
            s = n % NKV
            P.op("sp", (lambda l, c, kb, s: lambda e: e.dma_start(out=kring[:, s, :], in_=kt_d[l, c, :, kb * TT:(kb + 1) * TT]))(l, c, kb, s),
                 reads=[ktd_b[l][kb], vvd_b[l][kb]], writes=[kvslot_b[s]], dma_key="kk%d" % s)
            P.op("sp", (lambda l, c, kb, s: lambda e: e.dma_start(
                out=vring[:, s, :].rearrange("p (a c) -> p a c", a=4),
                in_=vv_d[l, c, kb * TT:(kb + 1) * TT, :].rearrange("(a p) c -> p a c", p=128)))(l, c, kb, s),
                reads=[ktd_b[l][kb], vvd_b[l][kb]], writes=[kvslot_b[s]], dma_key="kv%d" % s)
            return s

        def mm(out, lhsT, rhs, start, stop, reads, bank):
            P.op("pe", lambda e: e.matmul(out, lhsT=lhsT, rhs=rhs, start=start, stop=stop), reads=reads, writes=[bank_b[bank]])

        def layer_norm(src_ap, src_b, l, g_col, b_col, last):
            bs, bq = 6, 7
            for kc in range(KC):
                r = kc % 2
                P.op("act", (lambda kc, r: lambda e: e.activation(out=hb_ap[:, r, :], in_=src_ap[:, kc, :], func=AF.Copy))(kc, r),
                     reads=[src_b[kc]], writes=hb_b[r])
                P.op("act", (lambda kc, r: lambda e: e.activation(out=sq_ap[:, r, :], in_=src_ap[:, kc, :], func=AF.Square))(kc, r),
                     reads=[src_b[kc]], writes=sq_b[r])
                mm(banks[bs][:], onesb[:], hb_ap[:, r, :], kc == 0, kc == KC - 1, hb_b[r] + [const_b], bs)
                mm(banks[bq][:], onesb[:], sq_ap[:, r, :], kc == 0, kc == KC - 1, sq_b[r] + [const_b], bq)
            mean = lnf_ap[:, 0, :]
            rstd = lnf_ap[:, 1, :]
            msq = lnf_ap[:, 2, :]
            P.op("dve", lambda e: e.tensor_scalar(out=mean, in0=banks[bs][:], scalar1=1.0 / D, scalar2=None, op0=ALU.mult), reads=[bank_b[bs]], writes=lnf_b[0])
            P.op("dve", lambda e: e.tensor_tensor(out=msq, in0=mean, in1=mean, op=ALU.mult), reads=lnf_b[0], writes=lnf_b[2])
            P.op("dve", lambda e: e.scalar_tensor_tensor(out=msq, in0=banks[bq][:], scalar=1.0 / D, in1=msq, op0=ALU.mult, op1=ALU.subtract), reads=[bank_b[bq]] + lnf_b[2], writes=lnf_b[2])
            P.op("act", lambda e: e.activation(out=rstd, in_=msq, func=AF.Ln, bias=cst[:, 0:1], scale=1.0), reads=lnf_b[2] + [const_b], writes=lnf_b[1])
            P.op("act", lambda e: e.activation(out=rstd, in_=rstd, func=AF.Exp, scale=-0.5), reads=lnf_b[1], writes=lnf_b[1])
            for kc in range(KC):
                r = kc % 2
                gcol = smt[:, l * SM_L + g_col + kc:l * SM_L + g_col + kc + 1]
                bcol = smt[:, l * SM_L + b_col + kc:l * SM_L + b_col + kc + 1]
                tmp = sl_ap[:, r, :]
                neng = "pool" if kc in (2, 5) else "dve"
                if neng == "pool":
                    tmp = ap_t[:, 2 + (kc // 4), :]
                    tmp_b = [ap_b[2 + (kc // 4)]]
                else:
                    tmp_b = sl_b[r]
                P.op(neng, (lambda kc, tmp: lambda e: e.tensor_tensor(out=tmp, in0=src_ap[:, kc, :], in1=mean, op=ALU.subtract))(kc, tmp),
                     reads=[src_b[kc]] + lnf_b[0], writes=tmp_b)
                P.op(neng, (lambda tmp: lambda e: e.tensor_tensor(out=tmp, in0=tmp, in1=rstd, op=ALU.mult))(tmp),
                     reads=tmp_b + lnf_b[1], writes=tmp_b)
                if not last:
                    P.op("act", (lambda kc, tmp, gcol, bcol: lambda e: e.activation(out=src_ap[:, kc, :], in_=tmp, func=AF.Identity, scale=gcol, bias=bcol))(kc, tmp, gcol, bcol),
                         reads=tmp_b + [const_b], writes=[src_b[kc]])
                    P.op("act", (lambda kc, tmp, gcol, bcol: lambda e: e.activation(out=xb[:, kc, :], in_=tmp, func=AF.Identity, scale=gcol, bias=bcol))(kc, tmp, gcol, bcol),
                         reads=tmp_b + [const_b], writes=[xb_b[kc]])
                else:
                    P.op("act", (lambda kc, tmp, gcol, bcol, r: lambda e: e.activation(out=ostg[:, r, :], in_=tmp, func=AF.Identity, scale=gcol, bias=bcol))(kc, tmp, gcol, bcol, r),
                         reads=tmp_b + [const_b], writes=[ostg_b[r]])
                    yield kc, r

        dbg_d = nc.dram_tensor("dbg", [128, 36, 512], BF16, kind="ExternalOutput").ap() if dump else None
        dbgu_d = nc.dram_tensor("dbgu", [128, 4, 512], F32, kind="ExternalOutput").ap() if dump else None
        dbgp_d = nc.dram_tensor("dbgp", [128, 4, 512], BF16, kind="ExternalOutput").ap() if dump else None
        dbga_d = nc.dram_tensor("dbga", [128, 5, 512], F32, kind="ExternalOutput").ap() if dump else None

        def stage_done(k):
            if stop is not None and k >= stop:
                raise _Stop()

        def tile_layer(t, l):
            stage_done(0)
            t0 = t * TT
            sml = l * SM_L
            P.tag = "load"
            if l == 0:
                P.op("sp", lambda e: e.dma_start(out=A[:], in_=xT_v[:, :, t0:t0 + TT]), writes=A_b, dma_key="xin")
                for kc_ in range(KC):
                    eng_ = ("dve", "act", "pool", "dve", "act", "dve", "act", "pool")[kc_]
                    if eng_ == "act":
                        P.op("act", (lambda kc_: lambda e: e.activation(out=xb[:, kc_, :], in_=A[:, kc_, :], func=AF.Copy))(kc_), reads=[A_b[kc_]], writes=[xb_b[kc_]])
                    else:
                        P.op(eng_, (lambda kc_: lambda e: e.tensor_copy(out=xb[:, kc_, :], in_=A[:, kc_, :]))(kc_), reads=[A_b[kc_]], writes=[xb_b[kc_]])
                P.op("sp", lambda e: e.dma_start(out=cst_t[:], in_=cs_d[:, :, t0:t0 + TT]), reads=[csd_b], writes=[cs_b], dma_key="csin")
            Ct = cst_t[:, 0, :]
            St = cst_t[:, 1, :]

            P.tag = "w_v"
            wv, wb_ = wblock(t, l, 0)
            wv3 = wv[:, 0:4096].rearrange("p (kc c) -> p kc c", kc=KC)
            vbk = [gp_bank() for _ in range(4)]
            for kc in range(KC):
                for sub in range(4):
                    mm(banks[vbk[sub]][:], xb[:, kc, sub * 128:(sub + 1) * 128], wv3[:, kc, :], kc == 0, kc == KC - 1, [xb_b[kc], wb_], vbk[sub])
            for sub in range(4):
                bk = vbk[sub]
                r = sub % 2
                P.op("act", (lambda bk, r: lambda e: e.activation(out=vt[:, r, :], in_=banks[bk][:], func=AF.Gelu))(bk, r),
                     reads=[bank_b[bk]], writes=[vt_b[r]])
                P.op("dve", (lambda r: lambda e: e.bn_stats(out=vst6[:, r, :], in_=vt[:, r, :]))(r), reads=[vt_b[r]], writes=[vs_b[r]])
                P.op("dve", (lambda r: lambda e: e.bn_aggr(out=vmv[:, r, 0:2], in_=vst6[:, r, :]))(r), reads=[vs_b[r]], writes=[vs_b[r]])
                P.op("act", (lambda r: lambda e: e.activation(out=vmv[:, r, 2:3], in_=vmv[:, r, 1:2], func=AF.Sqrt, bias=cst[:, 0:1], scale=1.0))(r),
                     reads=[vs_b[r], const_b], writes=[vs_b[r]])
                P.op("dve", (lambda r: lambda e: e.reciprocal(out=vmv[:, r, 3:4], in_=vmv[:, r, 2:3]))(r), reads=[vs_b[r]], writes=[vs_b[r]])
                P.op("dve", (lambda r: lambda e: e.tensor_scalar(out=vt[:, r, :], in0=vt[:, r, :], scalar1=vmv[:, r, 0:1], scalar2=vmv[:, r, 3:4], op0=ALU.subtract, op1=ALU.mult))(r),
                     reads=[vt_b[r], vs_b[r]], writes=[vt_b[r]])
                P.op("pool", (lambda r: lambda e: e.tensor_tensor(out=vt[:, r, :], in0=vt[:, r, :], in1=gbc[:, l, 0, :], op=ALU.mult))(r),
                     reads=[vt_b[r], const_b], writes=[vt_b[r]])
                P.op("pool", (lambda r, sub: lambda e: e.tensor_tensor(out=vb_ap[:, sub, :], in0=vt[:, r, :], in1=gbc[:, l, 1, :], op=ALU.add))(r, sub),
                     reads=[vt_b[r], const_b], writes=vb_b[sub])
            P.tag = "w_u"
            wv, wb_ = wblock(t, l, 1)
            wv3 = wv[:, 0:4096].rearrange("p (kc c) -> p kc c", kc=KC)
            for j in range(4):
                bk = gp_bank()
                for kc in range(KC):
                    mm(banks[bk][:], wv3[:, kc, j * 128:(j + 1) * 128], xb[:, kc, :], kc == 0, kc == KC - 1, [xb_b[kc], wb_], bk)
                P.op("act", (lambda bk, j: lambda e: e.activation(out=u_ap[:, j, :], in_=banks[bk][:], func=AF.Gelu))(bk, j),
                     reads=[bank_b[bk]], writes=u_b[j])
            P.tag = "w_qk"
            rope_pend = []

            def rope_finish(item):
                r, j, dst_ap, dst_b = item
                bk2 = gp_bank()
                mm(banks[bk2][:], rmb[:], raw[:, r, :], True, True, [raw_b[r], const_b], bk2)
                P.op("dve", (lambda bk2, r: lambda e: e.tensor_tensor(out=rt2[:, r, :], in0=banks[bk2][:], in1=St, op=ALU.mult))(bk2, r),
                     reads=[bank_b[bk2], cs_b], writes=[rt2_b[r]])
                P.op("pool", (lambda r, j, dst_ap: lambda e: e.tensor_tensor(out=dst_ap[:, j, :], in0=rt1[:, r, :], in1=rt2[:, r, :], op=ALU.add))(r, j, dst_ap),
                     reads=[rt1_b[r], rt2_b[r]], writes=dst_b[j])

            for bi, (dst_ap, dst_b) in ((2, (q_ap, q_b)), (3, (kst_ap, kst_b))):
                wv, wb_ = wblock(t, l, bi)
                wv3 = wv[:, 0:4096].rearrange("p (kc c) -> p kc c", kc=KC)
                for j in range(4):
                    bk = gp_bank()
                    for kc in range(KC):
                        mm(banks[bk][:], wv3[:, kc, j * 128:(j + 1) * 128], xb[:, kc, :], kc == 0, kc == KC - 1, [xb_b[kc], wb_], bk)
                    r = j % 2
                    P.op("act", (lambda bk, r: lambda e: e.activation(out=raw[:, r, :], in_=banks[bk][:], func=AF.Copy))(bk, r),
                         reads=[bank_b[bk]], writes=[raw_b[r], bank_b[bk]])
                    P.op("dve", (lambda bk, r: lambda e: e.tensor_tensor(out=rt1[:, r, :], in0=banks[bk][:], in1=Ct, op=ALU.mult))(bk, r),
                         reads=[bank_b[bk], cs_b], writes=[rt1_b[r]])
                    if rope_pend:
                        rope_finish(rope_pend.pop())
                    rope_pend.append((r, j, dst_ap, dst_b[j] if False else dst_b))
            while rope_pend:
                rope_finish(rope_pend.pop())
            for c in range(4):
                P.op("sp", (lambda c: lambda e: e.dma_start(out=kt_d[l, c, :, t0:t0 + TT], in_=kst_ap[:, c, :]))(c),
                     reads=kst_b[c], writes=[ktd_b[l][t]], dma_key="kw%d" % c)
            P.tag = "w_V"
            wv, wb_ = wblock(t, l, 4)
            wv3 = wv[:, 0:4096].rearrange("p (kc c) -> p kc c", kc=KC)
            for sub in range(4):
                bk = gp_bank()
                for kc in range(KC):
                    mm(banks[bk][:], xb[:, kc, sub * 128:(sub + 1) * 128], wv3[:, kc, :], kc == 0, kc == KC - 1, [xb_b[kc], wb_], bk)
                P.op("act", (lambda bk, sub: lambda e: e.activation(out=vst_ap[:, sub, :], in_=banks[bk][:], func=AF.Copy))(bk, sub),
                     reads=[bank_b[bk]], writes=vst_b[sub])
            for c in range(4):
                P.op("sp", (lambda c: lambda e: e.dma_start(
                    out=vv_d[l, c, t0:t0 + TT, :].rearrange("(a p) c -> p a c", p=128),
                    in_=vst_ap[:, :, c * 128:(c + 1) * 128]))(c),
                    reads=[b for bb in vst_b for b in bb], writes=[vvd_b[l][t]], dma_key="vw%d" % c)
            P.tag = "w_conv"
            for ci in range(12):
                j, kind = ci // 3, ci % 3
                bi, cc = 5 + ci // 4, ci % 4
                if cc == 0:
                    wv, wb_ = wblock(t, l, bi)
                    wv3 = wv[:, 0:4096].rearrange("p (kc c) -> p kc c", kc=KC)
                bk = gp_bank()
                for kc in range(KC):
                    mm(banks[bk][:], wv3[:, kc, cc * 128:(cc + 1) * 128], xb[:, kc, :], kc == 0, kc == KC - 1, [xb_b[kc], wb_], bk)
                if kind == 0:
                    P.op("act", (lambda bk: lambda e: e.activation(out=cgc[:], in_=banks[bk][:], func=AF.Copy))(bk), reads=[bank_b[bk]], writes=[cgc_b])
                elif kind == 1:
                    P.op("pool", (lambda j: lambda e: e.tensor_copy(out=hcv[:, 0:2], in_=carry[:, l, j, :]))(j), reads=[carry_b], writes=[hcv_b])
                    P.op("dve", (lambda bk: lambda e: e.tensor_tensor(out=hcv[:, 2:514], in0=cgc[:], in1=banks[bk][:], op=ALU.mult))(bk),
                         reads=[cgc_b, bank_b[bk]], writes=[hcv_b])
                    P.op("pool", (lambda j: lambda e: e.tensor_copy(out=carry[:, l, j, :], in_=hcv[:, 512:514]))(j), reads=[hcv_b], writes=[carry_b])
                    w0 = smt[:, sml + SM_CONV + j * 3 + 0:sml + SM_CONV + j * 3 + 1]
                    w1 = smt[:, sml + SM_CONV + j * 3 + 1:sml + SM_CONV + j * 3 + 2]
                    w2 = smt[:, sml + SM_CONV + j * 3 + 2:sml + SM_CONV + j * 3 + 3]
                    P.op("dve", (lambda w0: lambda e: e.tensor_scalar(out=cacc[:], in0=hcv[:, 0:512], scalar1=w0, scalar2=None, op0=ALU.mult))(w0),
                         reads=[hcv_b, const_b], writes=[cacc_b])
                    P.op("dve", (lambda w1: lambda e: e.scalar_tensor_tensor(out=cacc[:], in0=hcv[:, 1:513], scalar=w1, in1=cacc[:], op0=ALU.mult, op1=ALU.add))(w1),
                         reads=[hcv_b, const_b, cacc_b], writes=[cacc_b])
                    P.op("dve", (lambda w2: lambda e: e.scalar_tensor_tensor(out=cacc[:], in0=hcv[:, 2:514], scalar=w2, in1=cacc[:], op0=ALU.mult, op1=ALU.add))(w2),
                         reads=[hcv_b, const_b, cacc_b], writes=[cacc_b])
                else:
                    P.op("dve", (lambda bk, j: lambda e: e.tensor_tensor(out=yc_ap[:, j, :], in0=cacc[:], in1=banks[bk][:], op=ALU.mult))(bk, j),
                         reads=[cacc_b, bank_b[bk]], writes=yc_b[j])
            P.tag = "sgu"
            for g in range(4):
                bk = gp_bank()
                mm(banks[bk][:], onesb[0:64, :], brow[0:64, l, g, :, :].rearrange("p a c -> p (a c)"), True, False, [const_b], bk)
                for sub in range(4):
                    mm(banks[bk][:, sub * 128:(sub + 1) * 128], vb_ap[:, sub, g * 128:(g + 1) * 128], wsg[:, l, g, :], False, sub == 3,
                       vb_b[sub] + [const_b], bk)
                P.op("dve", (lambda bk, g: lambda e: e.tensor_tensor(out=ya_ap[:, g, :], in0=u_ap[:, g, :], in1=banks[bk][:], op=ALU.mult))(bk, g),
                     reads=u_b[g] + [bank_b[bk]], writes=ya_b[g])
            stage_done(1)
            yield
            P.tag = "attn"
            neglam = cst[:, 1 + l:2 + l]
            gsc = cst[:, 4 + l:5 + l]
            kvseq = [(c, kb) for c in range(4) for kb in range(t + 1)]
            kvslots = {}
            kvn = {"i": 0}

            def kv_prefetch(upto):
                while kvn["i"] <= min(upto, len(kvseq) - 1):
                    c_, kb_ = kvseq[kvn["i"]]
                    kvslots[(c_, kb_)] = kvload(l, c_, kb_)
                    kvn["i"] += 1

            r1, o1, r2, o2, rr = (ap_t[:, i, :] for i in range(5))

            def post_head(c):
                P.op("dve", lambda e: e.tensor_copy(out=r1, in_=banks[6][:]), reads=[bank_b[6]], writes=[ap_b[0]])
                P.op("dve", lambda e: e.tensor_copy(out=o1, in_=banks[4][:]), reads=[bank_b[4]], writes=[ap_b[1]])
                P.op("dve", lambda e: e.tensor_copy(out=r2, in_=banks[7][:]), reads=[bank_b[7]], writes=[ap_b[2]])
                P.op("dve", lambda e: e.tensor_copy(out=o2, in_=banks[5][:]), reads=[bank_b[5]], writes=[ap_b[3]])
                P.op("dve", lambda e: e.reciprocal(out=r1, in_=r1), reads=[ap_b[0]], writes=[ap_b[0]])
                P.op("dve", lambda e: e.reciprocal(out=r2, in_=r2), reads=[ap_b[2]], writes=[ap_b[2]])
                P.op("dve", lambda e: e.tensor_tensor(out=o1, in0=o1, in1=r1, op=ALU.mult), reads=[ap_b[1], ap_b[0]], writes=[ap_b[1]])
                P.op("dve", lambda e: e.tensor_tensor(out=o2, in0=o2, in1=r2, op=ALU.mult), reads=[ap_b[3], ap_b[2]], writes=[ap_b[3]])
                P.op("dve", lambda e: e.scalar_tensor_tensor(out=o1, in0=o2, scalar=neglam, in1=o1, op0=ALU.mult, op1=ALU.add), reads=[ap_b[3], ap_b[1], const_b], writes=[ap_b[1]])

            def post_tail(c):
                P.op("act", lambda e: e.activation(out=apq[:], in_=o1, func=AF.Square), reads=[ap_b[1]], writes=[apq_b])
                bk = gp_bank(False)
                mm(banks[bk][:], onesb[:], apq[:], True, True, [apq_b, const_b], bk)
                P.op("act", (lambda bk: lambda e: e.activation(out=rr, in_=banks[bk][:], func=AF.Sqrt, bias=cst[:, 0:1], scale=1.0 / 128))(bk),
                     reads=[bank_b[bk], const_b], writes=[ap_b[4]])
                P.op("dve", lambda e: e.reciprocal(out=rr, in_=rr), reads=[ap_b[4]], writes=[ap_b[4]])
                P.op("dve", (lambda c: lambda e: e.scalar_tensor_tensor(out=yb_ap[:, c, :], in0=o1, scalar=gsc, in1=rr, op0=ALU.mult, op1=ALU.mult))(c),
                     reads=[ap_b[1], ap_b[4], const_b], writes=yb_b[c])

            pending_tail = None
            for c in range(4):
                pairs = [(kb, ks) for kb in range(t + 1) for ks in range(4)]
                npairs = len(pairs)
                defer_at = min(8, npairs - 1)
                pend = None

                def pv_l(pd, last):
                    ps_, pks, ppi, pq0, pii = pd
                    for hh in range(2):
                        mm(banks[4 + hh][:, pq0:TT], vring[:, ps_, pks * 128:(pks + 1) * 128], pt[:, ppi, hh, pq0:TT], pii == 0, last, [kvslot_b[ps_], pt_b[ppi]], 4 + hh)
                    for hh in range(2):
                        mm(banks[6 + hh][:, pq0:TT], onesb[:], pt[:, ppi, hh, pq0:TT], pii == 0, last, [const_b, pt_b[ppi]], 6 + hh)

                for ii, (kb, ks) in enumerate(pairs):
                    if ks == 0:
                        kv_prefetch(c * (t + 1) + kb + 2)
                    s = kvslots[(c, kb)]
                    diag = kb == t
                    q0 = ks * 128 if diag else 0
                    sp = ii % 2
                    pi = ii % NPT
                    for hh in range(2):
                        mm(pp[sp][:, hh * 512 + q0:(hh + 1) * 512], kring[hh * 64:(hh + 1) * 64, s, ks * 128:(ks + 1) * 128], q_ap[hh * 64:(hh + 1) * 64, c, q0:TT],
                           True, True, [kvslot_b[s]] + q_b[c], 2 * sp + hh)
                    P.op("act", (lambda sp, pi, q0: lambda e: e.activation(out=pt[:, pi, :, q0:TT], in_=pp[sp][:].rearrange("p (h c) -> p h c", h=2)[:, :, q0:TT], func=AF.Exp, scale=0.125))(sp, pi, q0),
                         reads=[bank_b[2 * sp], bank_b[2 * sp + 1]], writes=[pt_b[pi]])
                    if diag:
                        P.op("pool", (lambda pi, q0: lambda e: e.tensor_tensor(out=pt[:, pi, :, q0:q0 + 128], in0=pt[:, pi, :, q0:q0 + 128], in1=trib2[:], op=ALU.mult))(pi, q0),
                             reads=[pt_b[pi], const_b], writes=[pt_b[pi]])
                    if pend is not None:
                        pv_l(pend, False)
                    pend = (s, ks, pi, q0, ii)
                    if ii == defer_at and pending_tail is not None:
                        post_tail(pending_tail)
                        pending_tail = None
                pv_l(pend, True)
                post_head(c)
                pending_tail = c
                stage_done(2)
                yield
            post_tail(3)
            P.tag = "gates"
            ys = ((ya_ap, ya_b), (yb_ap, yb_b), (yc_ap, yc_b))
            for oc in range(8):
                wv, wb_ = wblock(t, l, 8 + oc)
                gw = wv[:, 0:3072].rearrange("p (n kc c) -> p n kc c", n=3, kc=KC)
                bw = wv[:, 3072:4608].rearrange("p (n kc c) -> p n kc c", n=3, kc=4)
                for n in range(3):
                    bk = gp_bank()
                    for kc in range(KC):
                        mm(banks[bk][:], gw[:, n, kc, :], xb[:, kc, :], kc == 0, kc == KC - 1, [xb_b[kc], wb_], bk)
                    P.op("act", (lambda bk, n: lambda e: e.activation(out=sg[:, n, :], in_=banks[bk][:], func=AF.Sigmoid))(bk, n),
                         reads=[bank_b[bk]], writes=[sg_b[n]])
                for n in (0, 2, 1):
                    bk = gp_bank()
                    y_ap, y_b = ys[n]
                    for kc in range(4):
                        mm(banks[bk][:], bw[:, n, kc, :], y_ap[:, kc, :], kc == 0, kc == 3, y_b[kc] + [wb_], bk)
                    P.op("dve", (lambda bk, n: lambda e: e.tensor_tensor(out=mt[:, n, :], in0=sg[:, n, :], in1=banks[bk][:], op=ALU.mult))(bk, n),
                         reads=[sg_b[n], bank_b[bk]], writes=[mt_b[n]])
                P.op("pool", lambda e: e.tensor_tensor(out=mt[:, 0, :], in0=mt[:, 0, :], in1=mt[:, 2, :], op=ALU.add), reads=[mt_b[0], mt_b[2]], writes=[mt_b[0]])
                P.op("pool", (lambda oc: lambda e: e.tensor_tensor(out=mg_ap[:, oc, :], in0=mt[:, 0, :], in1=mt[:, 1, :], op=ALU.add))(oc),
                     reads=[mt_b[0], mt_b[1]], writes=mg_b[oc])
            stage_done(3)
            yield
            P.tag = "w_o_ln1"
            for oc in range(8):
                if oc % 4 == 0:
                    wv, wb_ = wblock(t, l, 16 + oc // 4)
                    wv3 = wv[:, 0:4096].rearrange("p (kc c) -> p kc c", kc=KC)
                bk = gp_bank()
                for kc in range(KC):
                    mm(banks[bk][:], wv3[:, kc, (oc % 4) * 128:(oc % 4 + 1) * 128], mg_ap[:, kc, :], kc == 0, kc == KC - 1, mg_b[kc] + [wb_], bk)
                P.op("dve", (lambda bk, oc: lambda e: e.scalar_tensor_tensor(out=Bx[:, oc, :], in0=A[:, oc, :], scalar=ALPHA, in1=banks[bk][:], op0=ALU.mult, op1=ALU.add))(bk, oc),
                     reads=[A_b[oc], bank_b[bk]], writes=[B_b[oc]])
            for _ in layer_norm(Bx, B_b, l, SM_LN1G, SM_LN1B, False):
                pass
            stage_done(4)
            yield
            P.tag = "ffn_gu"
            for jb in range(11):
                wv, wb_ = wblock(t, l, 18 + jb)
                wv3 = wv[:, 0:4096].rearrange("p (kc c) -> p kc c", kc=KC)
                fbk = None
                if jb == 0:
                    fbk = [gp_bank() for _ in range(4)]
                    for kc in range(KC):
                        for q4 in range(4):
                            mm(banks[fbk[q4]][:], wv3[:, kc, q4 * 128:(q4 + 1) * 128], xb[:, kc, :], kc == 0, kc == KC - 1, [xb_b[kc], wb_], fbk[q4])
                for jj in range(2):
                    j = 2 * jb + jj
                    if fbk is not None:
                        bg_, bu_ = fbk[2 * jj], fbk[2 * jj + 1]
                    else:
                        bg_ = gp_bank()
                        for kc in range(KC):
                            mm(banks[bg_][:], wv3[:, kc, (2 * jj) * 128:(2 * jj + 1) * 128], xb[:, kc, :], kc == 0, kc == KC - 1, [xb_b[kc], wb_], bg_)
                        bu_ = gp_bank()
                        for kc in range(KC):
                            mm(banks[bu_][:], wv3[:, kc, (2 * jj + 1) * 128:(2 * jj + 2) * 128], xb[:, kc, :], kc == 0, kc == KC - 1, [xb_b[kc], wb_], bu_)
                    r = j % 2
                    P.op("act", (lambda bg_, r: lambda e: e.activation(out=sl_ap[:, r, :], in_=banks[bg_][:], func=AF.Silu))(bg_, r),
                         reads=[bank_b[bg_]], writes=sl_b[r])
                    P.op("dve", (lambda bu_, r, j: lambda e: e.tensor_tensor(out=hm_ap[:, j, :], in0=sl_ap[:, r, :], in1=banks[bu_][:], op=ALU.mult))(bu_, r, j),
                         reads=sl_b[r] + [bank_b[bu_]], writes=hm_b[j])
                if jb % 4 == 3:
                    stage_done(5)
            yield
            P.tag = "down_ln2"
            for oc in range(8):
                wv, wb_ = wblock(t, l, 29 + oc)
                wv3 = wv[:, 0:2816].rearrange("p (kc c) -> p kc c", kc=NFF)
                bk = gp_bank()
                for kc in range(NFF):
                    mm(banks[bk][:], wv3[:, kc, :], hm_ap[:, kc, :], kc == 0, kc == NFF - 1, hm_b[kc] + [wb_], bk)
                P.op("dve", (lambda bk, oc: lambda e: e.scalar_tensor_tensor(out=A[:, oc, :], in0=Bx[:, oc, :], scalar=ALPHA, in1=banks[bk][:], op0=ALU.mult, op1=ALU.add))(bk, oc),
                     reads=[B_b[oc], bank_b[bk]], writes=[A_b[oc]])
            last = l == DEPTH - 1
            for res in layer_norm(A, A_b, l, SM_LN2G, SM_LN2B, last):
                kc, r = res
                o = P.op("sp", (lambda kc, r: lambda e: e.dma_start(out=outT_v[:, kc, t0:t0 + TT], in_=ostg[:, r, :]))(kc, r),
                         reads=[ostg_b[r]], writes=[], dma_key="out%d" % r)
                out_toks.append(o.tok)
            stage_done(6)
            yield

        try:
            for t in range(NT):
                for l in range(DEPTH):
                    for _ in tile_layer(t, l):
                        pass
        except _Stop:
            pass
        if stop is not None:
            o = P.op("sp", lambda e: e.dma_start(out=outT_v[:, :, 0:TT], in_=A[:]), reads=A_b + B_b, writes=[], dma_key="out0")
            out_toks.append(o.tok)
        if dump:
            o = P.op("sp", lambda e: e.dma_start(out=dbg_d, in_=arena[:, 8:44, :]), reads=pg, writes=[], dma_key="out1")
            out_toks.append(o.tok)
            o = P.op("sp", lambda e: e.dma_start(out=dbgu_d, in_=u_ap), reads=pg, writes=[], dma_key="out1")
            out_toks.append(o.tok)
            if stop is not None and stop >= 2:
                o = P.op("sp", lambda e: e.dma_start(out=dbgp_d, in_=pt[:, 0:2, :, :].rearrange("p a h c -> p (a h) c")), reads=pt_b, writes=[], dma_key="out1")
                out_toks.append(o.tok)
                o = P.op("sp", lambda e: e.dma_start(out=dbga_d, in_=ap_t[:]), reads=ap_b, writes=[], dma_key="out1")
                out_toks.append(o.tok)

        fin = {}
        for s_, v_ in out_toks:
            fin[s_] = max(fin.get(s_, 0), v_)
        P.emit(nc, final_wait_tokens=list(fin.items()))
    build_program.last_prog = P
    return nc, len(P.ops)


def _wstream(w_in, w_branch, w_o, w_gate_up, w_down):
    def fm(cols):
        K = cols.shape[0]
        return cols.reshape(K // 128, 128, cols.shape[1]).transpose(1, 0, 2).reshape(128, -1)
    parts = []
    u = w_in[:, 0:512]
    v = w_in[:, 512:1024]
    q = w_in[:, 1024:1536]
    k = w_in[:, 1536:2048]
    vv = w_in[:, 2048:2560]
    bg = w_in[:, 2560:3072]
    cg = w_in[:, 3072:3584]
    xc = w_in[:, 3584:4096]
    zg = w_in[:, 4096:7168]
    conv_cols = []
    for j in range(4):
        for src in (cg, xc, bg):
            conv_cols.append(src[:, j * 128:(j + 1) * 128])
    conv = np.concatenate(conv_cols, axis=1)
    for blk in (v, u, q, k, vv, conv[:, 0:512], conv[:, 512:1024], conv[:, 1024:1536]):
        parts.append(fm(blk))
    for oc in range(8):
        for n in range(3):
            parts.append(fm(zg[:, n * 1024 + oc * 128:n * 1024 + (oc + 1) * 128]))
        for n in range(3):
            parts.append(fm(w_branch[n][:, oc * 128:(oc + 1) * 128]))
    for h in range(2):
        parts.append(fm(w_o[:, h * 512:(h + 1) * 512]))
    for jb in range(11):
        cols = []
        for jj in range(2):
            j = 2 * jb + jj
            cols.append(w_gate_up[:, j * 128:(j + 1) * 128])
            cols.append(w_gate_up[:, DFF + j * 128:DFF + (j + 1) * 128])
        parts.append(fm(np.concatenate(cols, axis=1)))
    for oc in range(8):
        parts.append(fm(w_down[:, oc * 128:(oc + 1) * 128]))
    out = np.concatenate(parts, axis=1)
    assert out.shape == (128, WCOLS), out.shape
    return out


def _const_tables():
    tri = (np.arange(128)[:, None] <= np.arange(128)[None, :]).astype(np.float32)
    rm = np.zeros((128, 128), np.float32)
    fs = np.zeros((128,), np.float32)
    inv_freq = (np.float32(ROPE_THETA) ** (-np.arange(0, 16, 2, dtype=np.float32) / np.float32(16))).astype(np.float32)
    for h in range(2):
        for d in range(16):
            src = d + 8 if d < 8 else d - 8
            rm[h * 64 + src, h * 64 + d] = 1.0
            fs[h * 64 + d] = -inv_freq[d] if d < 8 else inv_freq[d - 8]
    return tri, rm, fs


def _host_layout(inp, depth=DEPTH):
    f32 = np.float32
    tri, rm, fs = _const_tables()
    sm = np.zeros((128, NSM), f32)
    rw = np.zeros((1, NRW), f32)
    wst = np.zeros((depth, 128, WCOLS), f32)
    for l in range(depth):
        o = l * SM_L
        sm[:, o + SM_LN1G:o + SM_LN1G + 8] = inp["ln1_g"][l].reshape(8, 128).T
        sm[:, o + SM_LN1B:o + SM_LN1B + 8] = inp["ln1_b"][l].reshape(8, 128).T
        sm[:, o + SM_LN2G:o + SM_LN2G + 8] = inp["ln2_g"][l].reshape(8, 128).T
        sm[:, o + SM_LN2B:o + SM_LN2B + 8] = inp["ln2_b"][l].reshape(8, 128).T
        sm[:, o + SM_SUBG] = inp["subln_g"][l]
        sm[:, o + SM_CONV:o + SM_CONV + 12] = inp["conv_w"][l].reshape(4, 128, 3).transpose(1, 0, 2).reshape(128, 12)
        sm[:, SM_WSG + l * 512:SM_WSG + (l + 1) * 512] = inp["w_sgu"][l].transpose(2, 0, 1).reshape(128, 512)
        r = l * RW_L
        rw[0, r + RW_G:r + RW_G + 512] = inp["sgu_ln_g"][l]
        rw[0, r + RW_B:r + RW_B + 512] = inp["sgu_ln_b"][l]
        rw[0, r + RW_BS:r + RW_BS + 512] = inp["b_sgu"][l].reshape(512)
        rw[0, r + RW_LAM:r + RW_LAM + 256] = np.concatenate(
            [inp["lambda_q1"][l], inp["lambda_k1"][l], inp["lambda_q2"][l], inp["lambda_k2"][l]])
        wst[l] = _wstream(inp["w_in"][l], inp["w_branch"][l], inp["w_o"][l], inp["w_gate_up"][l], inp["w_down"][l])
    sm[:, SM_FS] = fs
    sm[:, SM_TRI:SM_TRI + 128] = tri
    sm[:, SM_RM:SM_RM + 128] = rm
    return sm, rw, wst


_CACHE = {}


def run_cores(inp, n_cores, S):
    inp = {k: np.asarray(v) for k, v in inp.items()}
    sm, rw, wst = _host_layout(inp)
    lambda_inits = [0.8 - 0.6 * math.exp(-0.3 * l) for l in range(DEPTH)]
    key = S
    if key not in _CACHE:
        _CACHE[key] = build_program(S, lambda_inits)[0]
    nc = _CACHE[key]
    in_maps = []
    for c in range(n_cores):
        in_maps.append({
            "xT": np.ascontiguousarray(inp["x"][c].T.astype(np.float32)),
            "pos": np.ascontiguousarray(inp["positions"][c].reshape(1, S).astype(np.int32)),
            "wst": wst, "sm": sm, "rw": rw,
        })
    res = run_bass_kernel_spmd(nc, in_maps, core_ids=list(range(n_cores)))
    out = np.stack([np.ascontiguousarray(r["outT"].T) for r in res.results], axis=0)
    return out.astype(np.float32)


def kernel(**inputs):
    x = np.asarray(inputs["x"])
    B, S, _ = x.shape
    assert B == N_CORES
    return run_cores(inputs, N_CORES, S)
```
